# Optimizing a Trainium2 kernel written in Bass

```python
import math
import jax, jax.numpy as jnp
from jax import lax
import numpy as np

D_MODEL = 1024
BATCH = 32
SEQ = 2048
DEPTH = 4

N_BRANCH = 4
BRANCH_WIDTH = 512
SSD_D_INNER = 512
SSD_HEADDIM = 64
SSD_HEADS = SSD_D_INNER // SSD_HEADDIM
SSD_GROUPS = 2
SSD_STATE = 128
SSD_CONV = 4
SSD_CHUNK = 128
SSD_XBC = SSD_D_INNER + 2 * SSD_GROUPS * SSD_STATE
GMLP_WIDTH = 512
GMLP_GROUPS = 4
GMLP_CHUNK = 128
MLA_HEADS = 4
MLA_Q_RANK = 384
MLA_KV_RANK = 128
MLA_NOPE = 128
MLA_ROPE = 64
MLA_V = 128
MLA_QK = MLA_NOPE + MLA_ROPE
ROPE_THETA = 10000.0
ATTN_BLOCK = 128
SC_WIDTH = 512
SC_KERNEL = 3
D_FF = 2816
NORM_EPS = 1e-6
IN_SIZES = (SSD_D_INNER, SSD_XBC, SSD_HEADS, 2 * GMLP_WIDTH, MLA_Q_RANK, MLA_KV_RANK, MLA_ROPE, 3 * SC_WIDTH, N_BRANCH * D_MODEL)
IN_TOTAL = sum(IN_SIZES)

kernel_name = "hybrid_gated_parallel_mixer_block"

F32 = jnp.float32


def rms_norm(x, w):
    x32 = x.astype(F32)
    y = x32 * lax.rsqrt(jnp.mean(x32 * x32, axis=-1, keepdims=True) + NORM_EPS)
    return (y * w.astype(F32)).astype(x.dtype)


def split_cols(x, sizes):
    out = []
    off = 0
    for s in sizes:
        out.append(x[..., off:off + s])
        off += s
    return out


def causal_dwconv(x, w):
    K = w.shape[0]
    L = x.shape[1]
    xp = jnp.pad(x, ((0, 0), (K - 1, 0), (0, 0)))
    out = xp[:, 0:L] * w[0]
    for k in range(1, K):
        out = out + xp[:, k:k + L] * w[k]
    return out


def swiglu(h, w_gu, w_down):
    g, u = jnp.split(h @ w_gu, 2, axis=-1)
    return (jax.nn.silu(g) * u) @ w_down


def apply_rope(x, cos, sin):
    x1, x2 = jnp.split(x.astype(F32), 2, axis=-1)
    out = jnp.concatenate([x1 * cos - x2 * sin, x2 * cos + x1 * sin], axis=-1)
    return out.astype(x.dtype)


def ssd_chunked_scan(xh, dt, A, Bg, Cg):
    b, L, H, P = xh.shape
    G, N = Bg.shape[2], Bg.shape[3]
    R = H // G
    Q = SSD_CHUNK
    c = L // Q
    dtc = dt.reshape(b, c, Q, G, R)
    a_cs = jnp.cumsum(dtc * A.reshape(G, R), axis=2)
    xdt = xh.astype(F32).reshape(b, c, Q, G, R, P) * dtc[..., None]
    Bc = Bg.astype(F32).reshape(b, c, Q, G, N)
    Cc = Cg.astype(F32).reshape(b, c, Q, G, N)
    causal = jnp.tril(jnp.ones((Q, Q), bool))[None, None, :, :, None, None]
    seg = a_cs[:, :, :, None] - a_cs[:, :, None, :]
    decay_in = jnp.exp(jnp.where(causal, seg, -jnp.inf))
    cb = jnp.einsum('bctgn,bcsgn->bctsg', Cc, Bc)
    y_diag = jnp.einsum('bctsgr,bcsgrp->bctgrp', cb[..., None] * decay_in, xdt)
    decay_out = jnp.exp(a_cs[:, :, -1:] - a_cs)
    states = jnp.einsum('bcsgn,bcsgrp->bcgrpn', Bc, xdt * decay_out[..., None])
    a_tot = a_cs[:, :, -1]

    def step(h, inp):
        a_c, s_c = inp
        return jnp.exp(a_c)[..., None, None] * h + s_c, h

    h0 = jnp.zeros((b, G, R, P, N), F32)
    _, prev = lax.scan(step, h0, (jnp.moveaxis(a_tot, 1, 0), jnp.moveaxis(states, 1, 0)))
    prev = jnp.moveaxis(prev, 0, 1)
    y_off = jnp.einsum('bctgn,bcgrpn->bctgrp', Cc, prev) * jnp.exp(a_cs)[..., None]
    return (y_diag + y_off).reshape(b, L, H, P).astype(xh.dtype)


def ssd_branch(z, xbc_raw, dt_raw, conv_w, conv_b, dt_bias, a_log, d_skip, norm_w):
    b, L, _ = z.shape
    xbc = jax.nn.silu(causal_dwconv(xbc_raw, conv_w) + conv_b)
    xs, Bm, Cm = split_cols(xbc, (SSD_D_INNER, SSD_GROUPS * SSD_STATE, SSD_GROUPS * SSD_STATE))
    dt = jax.nn.softplus(dt_raw.astype(F32) + dt_bias.astype(F32))
    A = -jnp.exp(a_log.astype(F32))
    xh = xs.reshape(b, L, SSD_HEADS, SSD_HEADDIM)
    y = ssd_chunked_scan(xh, dt, A,
                         Bm.reshape(b, L, SSD_GROUPS, SSD_STATE),
                         Cm.reshape(b, L, SSD_GROUPS, SSD_STATE))
    y = y + d_skip[:, None] * xh
    y = y.reshape(b, L, SSD_D_INNER) * jax.nn.silu(z)
    y = rms_norm(y.reshape(b, L, SSD_GROUPS, SSD_D_INNER // SSD_GROUPS),
                 norm_w.reshape(SSD_GROUPS, SSD_D_INNER // SSD_GROUPS))
    return y.reshape(b, L, SSD_D_INNER)


def gmlp_branch(uv_raw, v_norm, w_s, b_s):
    b, L, _ = uv_raw.shape
    Q = GMLP_CHUNK
    c = L // Q
    dg = GMLP_WIDTH // GMLP_GROUPS
    u, v = jnp.split(jax.nn.gelu(uv_raw, approximate=False), 2, axis=-1)
    v = rms_norm(v, v_norm).reshape(b, c, Q, GMLP_GROUPS, dg)
    w_causal = w_s * jnp.tril(jnp.ones((Q, Q), w_s.dtype))
    sv = jnp.einsum('gts,bcsgd->bctgd', w_causal, v) + b_s.T[:, :, None]
    return u * sv.reshape(b, L, GMLP_WIDTH)


def causal_block_attention(q, k, v, scale):
    b, L, H, dq = q.shape
    nblk = L // ATTN_BLOCK
    qb = jnp.swapaxes(q.reshape(b, nblk, ATTN_BLOCK, H, dq), 0, 1)
    key_pos = jnp.arange(L)

    def one_block(args):
        q_blk, i = args
        s = jnp.einsum('bqhd,bkhd->bhqk', q_blk, k).astype(F32) * scale
        q_pos = i * ATTN_BLOCK + jnp.arange(ATTN_BLOCK)
        mask = key_pos[None, :] <= q_pos[:, None]
        p = jax.nn.softmax(jnp.where(mask, s, -jnp.inf), axis=-1)
        return jnp.einsum('bhqk,bkhd->bqhd', p.astype(v.dtype), v)

    out = lax.map(one_block, (qb, jnp.arange(nblk)))
    return jnp.swapaxes(out, 0, 1).reshape(b, L, H, v.shape[-1])


def mla_branch(q_lat, kv_lat, k_pe, cos, sin, q_norm, w_qb, kv_norm, w_kvb, qk_q, qk_k):
    b, L, _ = q_lat.shape
    H = MLA_HEADS
    q = (rms_norm(q_lat, q_norm) @ w_qb).reshape(b, L, H, MLA_QK)
    kv = (rms_norm(kv_lat, kv_norm) @ w_kvb).reshape(b, L, H, MLA_NOPE + MLA_V)
    k_nope, v = kv[..., :MLA_NOPE], kv[..., MLA_NOPE:]
    k = jnp.concatenate([k_nope, jnp.broadcast_to(k_pe[:, :, None, :], (b, L, H, MLA_ROPE))], axis=-1)
    q = rms_norm(q, qk_q)
    k = rms_norm(k, qk_k)
    q = jnp.concatenate([q[..., :MLA_NOPE], apply_rope(q[..., MLA_NOPE:], cos, sin)], axis=-1)
    k = jnp.concatenate([k[..., :MLA_NOPE], apply_rope(k[..., MLA_NOPE:], cos, sin)], axis=-1)
    o = causal_block_attention(q, k, v, MLA_QK ** -0.5)
    return o.reshape(b, L, H * MLA_V)


def short_conv_branch(sc_raw, conv_w):
    bg, cg, xin = jnp.split(sc_raw, 3, axis=-1)
    return bg * causal_dwconv(cg * xin, conv_w)


def hybrid_mixer(h, cos, sin, w_in, ssd_conv_w, ssd_conv_b, ssd_dt_bias, ssd_a_log, ssd_d, ssd_norm,
                 gmlp_v_norm, gmlp_w_s, gmlp_b_s, mla_q_norm, mla_w_qb, mla_kv_norm, mla_w_kvb,
                 mla_qk_q, mla_qk_k, sc_conv_w, w_branch, w_out):
    b, L, _ = h.shape
    proj = h @ w_in
    z, xbc, dt_raw, uv, q_lat, kv_lat, k_pe, sc, gates = split_cols(proj, IN_SIZES)
    y_a = ssd_branch(z, xbc, dt_raw, ssd_conv_w, ssd_conv_b, ssd_dt_bias, ssd_a_log, ssd_d, ssd_norm)
    y_b = gmlp_branch(uv, gmlp_v_norm, gmlp_w_s, gmlp_b_s)
    y_c = mla_branch(q_lat, kv_lat, k_pe, cos, sin, mla_q_norm, mla_w_qb, mla_kv_norm, mla_w_kvb, mla_qk_q, mla_qk_k)
    y_d = short_conv_branch(sc, sc_conv_w)
    br = jnp.stack([y_a, y_b, y_c, y_d], axis=2)
    per = jnp.einsum('blnd,nde->blne', br, w_branch)
    gate = jax.nn.sigmoid(gates.reshape(b, L, N_BRANCH, D_MODEL))
    return jnp.sum(gate * per, axis=2) @ w_out


def setup_inputs(seed: int = 0) -> dict:
    key = jax.random.key(seed)
    ks = jax.random.split(key, 32)

    def nrm(k, shape, scale):
        return jax.random.normal(k, shape, F32) * scale

    def gain(k, shape):
        return 1.0 + 0.02 * jax.random.normal(k, shape, F32)

    dt0 = jnp.exp(jax.random.uniform(ks[9], (DEPTH, SSD_HEADS), F32, math.log(1e-3), math.log(1e-1)))
    return {
        "x": nrm(ks[0], (BATCH, SEQ, D_MODEL), 1.0),
        "positions": jnp.arange(SEQ, dtype=jnp.int32)[None, :] + jax.random.randint(ks[1], (BATCH, 1), 0, SEQ, dtype=jnp.int32),
        "ffn1_norm": gain(ks[2], (DEPTH, D_MODEL)),
        "ffn1_w_gu": nrm(ks[3], (DEPTH, D_MODEL, 2 * D_FF), D_MODEL ** -0.5),
        "ffn1_w_down": nrm(ks[4], (DEPTH, D_FF, D_MODEL), D_FF ** -0.5),
        "mix_norm": gain(ks[5], (DEPTH, D_MODEL)),
        "w_in": nrm(ks[6], (DEPTH, D_MODEL, IN_TOTAL), D_MODEL ** -0.5),
        "ssd_conv_w": nrm(ks[7], (DEPTH, SSD_CONV, SSD_XBC), SSD_CONV ** -0.5),
        "ssd_conv_b": nrm(ks[8], (DEPTH, SSD_XBC), 0.02),
        "ssd_dt_bias": dt0 + jnp.log(-jnp.expm1(-dt0)),
        "ssd_a_log": jnp.log(jax.random.uniform(ks[10], (DEPTH, SSD_HEADS), F32, 1.0, 16.0)),
        "ssd_d": 1.0 + 0.1 * jax.random.normal(ks[11], (DEPTH, SSD_HEADS), F32),
        "ssd_norm": gain(ks[12], (DEPTH, SSD_D_INNER)),
        "gmlp_v_norm": gain(ks[13], (DEPTH, GMLP_WIDTH)),
        "gmlp_w_s": nrm(ks[14], (DEPTH, GMLP_GROUPS, GMLP_CHUNK, GMLP_CHUNK), GMLP_CHUNK ** -0.5),
        "gmlp_b_s": 1.0 + 0.02 * jax.random.normal(ks[15], (DEPTH, GMLP_GROUPS, GMLP_CHUNK), F32),
        "mla_q_norm": gain(ks[16], (DEPTH, MLA_Q_RANK)),
        "mla_w_qb": nrm(ks[17], (DEPTH, MLA_Q_RANK, MLA_HEADS * MLA_QK), MLA_Q_RANK ** -0.5),
        "mla_kv_norm": gain(ks[18], (DEPTH, MLA_KV_RANK)),
        "mla_w_kvb": nrm(ks[19], (DEPTH, MLA_KV_RANK, MLA_HEADS * (MLA_NOPE + MLA_V)), MLA_KV_RANK ** -0.5),
        "mla_qk_q": gain(ks[20], (DEPTH, MLA_QK)),
        "mla_qk_k": gain(ks[21], (DEPTH, MLA_QK)),
        "sc_conv_w": nrm(ks[22], (DEPTH, SC_KERNEL, SC_WIDTH), SC_KERNEL ** -0.5),
        "w_branch": nrm(ks[23], (DEPTH, N_BRANCH, BRANCH_WIDTH, D_MODEL), BRANCH_WIDTH ** -0.5),
        "w_out": nrm(ks[24], (DEPTH, D_MODEL, D_MODEL), D_MODEL ** -0.5),
        "ffn2_norm": gain(ks[25], (DEPTH, D_MODEL)),
        "ffn2_w_gu": nrm(ks[26], (DEPTH, D_MODEL, 2 * D_FF), D_MODEL ** -0.5),
        "ffn2_w_down": nrm(ks[27], (DEPTH, D_FF, D_MODEL), D_FF ** -0.5),
    }


def reference(x, positions, ffn1_norm, ffn1_w_gu, ffn1_w_down, mix_norm, w_in, ssd_conv_w, ssd_conv_b,
              ssd_dt_bias, ssd_a_log, ssd_d, ssd_norm, gmlp_v_norm, gmlp_w_s, gmlp_b_s, mla_q_norm,
              mla_w_qb, mla_kv_norm, mla_w_kvb, mla_qk_q, mla_qk_k, sc_conv_w, w_branch, w_out,
              ffn2_norm, ffn2_w_gu, ffn2_w_down):
    inv_freq = ROPE_THETA ** (-jnp.arange(0, MLA_ROPE, 2, dtype=F32) / MLA_ROPE)
    ang = positions.astype(F32)[..., None] * inv_freq
    cos = jnp.cos(ang)[:, :, None, :]
    sin = jnp.sin(ang)[:, :, None, :]
    for l in range(DEPTH):
        x = x + 0.5 * swiglu(rms_norm(x, ffn1_norm[l]), ffn1_w_gu[l], ffn1_w_down[l])
        x = x + hybrid_mixer(rms_norm(x, mix_norm[l]), cos, sin, w_in[l], ssd_conv_w[l], ssd_conv_b[l],
                             ssd_dt_bias[l], ssd_a_log[l], ssd_d[l], ssd_norm[l], gmlp_v_norm[l],
                             gmlp_w_s[l], gmlp_b_s[l], mla_q_norm[l], mla_w_qb[l], mla_kv_norm[l],
                             mla_w_kvb[l], mla_qk_q[l], mla_qk_k[l], sc_conv_w[l], w_branch[l], w_out[l])
        x = x + 0.5 * swiglu(rms_norm(x, ffn2_norm[l]), ffn2_w_gu[l], ffn2_w_down[l])
    return x
```

```python
import numpy as np
import concourse.bass as bass
import concourse.mybir as mybir
from concourse.bass_utils import run_bass_kernel_spmd
from contextlib import ExitStack

F32 = mybir.dt.float32
BF16 = mybir.dt.bfloat16
I32 = mybir.dt.int32
AF = mybir.ActivationFunctionType
ALU = mybir.AluOpType

D = 1024
NCH = 8
DFF = 2816
NJ = 22
INTOT = 8776
EPS = 1e-6
O_Z, O_XBC, O_DT, O_UV, O_QL, O_KVL, O_KPE, O_SC, O_G = 0, 512, 1536, 1544, 2568, 2952, 3080, 3144, 4680
NPC = 112
NPR = 24


class T:
    __slots__ = ("name", "w", "r", "sem", "ndma", "persist")

    def __init__(self, name, persist=False):
        self.name = name
        self.persist = persist
        self.w = None
        self.r = {}
        self.sem = None
        self.ndma = 0


class Op:
    __slots__ = ("eng", "fn", "deps", "needs", "key", "val", "dma", "epoch", "fs")


ENGS = ("pe", "act", "dve", "pool", "sp")


class Sched:
    def __init__(self, nc, es):
        self.nc = nc
        self.es = es
        self.ops = {e: [] for e in ENGS}
        self.epoch = 0
        self.dma_ops = []
        self.tiles = []

    def T(self, name, persist=False):
        t = T(name, persist)
        self.tiles.append(t)
        return t

    def PT(self, name):
        if not hasattr(self, "_pt"):
            self._pt = {}
        if name not in self._pt:
            self._pt[name] = self.T(name)
        return self._pt[name]

    def sb(self, name, shape, dtype):
        return self.es.enter_context(self.nc.sbuf_tensor("sb_" + name, list(shape), dtype))

    def op(self, eng, fn, reads=(), writes=(), dma_tile=None, fs=0):
        o = Op()
        o.fs = fs
        o.eng = eng
        o.fn = fn
        o.deps = []
        o.needs = False
        o.dma = dma_tile
        o.epoch = self.epoch
        o.key = None
        o.val = 0
        is_dma = dma_tile is not None

        def dep(p, raw=False):
            if p is None or p is o:
                return
            if p.dma is None and not is_dma and p.eng == eng:
                if not raw or eng == "pe":
                    return
                if p.fs >= 512 and o.fs >= 512:
                    return
            p.needs = True
            o.deps.append(p)

        for t in reads:
            dep(t.w, True)
        for t in writes:
            dep(t.w)
            for r in t.r.values():
                dep(r)
        for t in reads:
            t.r[("dma", id(o)) if is_dma else eng] = o
        for t in writes:
            t.w = o
            t.r = {}
        if is_dma:
            o.needs = True
            if not dma_tile.persist:
                self.dma_ops.append(o)
        self.ops[eng].append(o)
        return o

    def barrier(self):
        lasts = []
        BENGS = ("pe", "act", "dve", "sp")
        for e in BENGS:
            for o in reversed(self.ops[e]):
                if o.dma is None and o.fn is not None:
                    lasts.append(o)
                    break
        pend = list(self.dma_ops)
        self.dma_ops = []
        for e in BENGS:
            o = Op()
            o.fs = 0
            o.eng = e
            o.fn = None
            o.deps = []
            o.needs = False
            o.dma = None
            o.epoch = self.epoch
            o.key = None
            o.val = 0
            for p in lasts:
                if p.eng != e:
                    p.needs = True
                    o.deps.append(p)
            for p in pend:
                o.deps.append(p)
            self.ops[e].append(o)
        for t in self.tiles:
            if not t.persist:
                t.w = None
                t.r = {}

    def emit(self):
        nc = self.nc
        sems = {}

        def getsem(key):
            if key not in sems:
                sems[key] = self.es.enter_context(nc.semaphore("s%d" % len(sems)))
            return sems[key]

        for e in ENGS:
            cnt = {}
            for o in self.ops[e]:
                if o.dma is not None:
                    t = o.dma
                    t.ndma += 1
                    o.key = ("dma", id(t))
                    o.val = 16 * t.ndma
                elif o.needs:
                    k = (e, o.epoch)
                    cnt[k] = cnt.get(k, 0) + 1
                    o.key = k
                    o.val = cnt[k]
        for e in ENGS:
            for o in self.ops[e]:
                if o.needs:
                    getsem(o.key)
        import os
        if os.environ.get("MK_DEBUG"):
            mx = {}
            for e in ENGS:
                for o in self.ops[e]:
                    if o.key is not None:
                        mx[o.key] = max(mx.get(o.key, 0), o.val)
            print("NSEMS", len(sems), "MAXVALS", sorted([(str(k)[:30], v) for k, v in mx.items()], key=lambda kv: -kv[1])[:12])
            print("NOPS", {e: len(self.ops[e]) for e in ENGS})
        block = self.es.enter_context(nc.Block())

        def run(eng_name, eng):
            waited = {}
            for o in self.ops[eng_name]:
                need = {}
                for p in o.deps:
                    if waited.get(p.key, 0) < p.val:
                        if need.get(p.key, 0) < p.val:
                            need[p.key] = p.val
                for k, v in need.items():
                    eng.wait_ge(sems[k], v)
                    waited[k] = v
                if o.fn is None:
                    continue
                ins = o.fn(eng)
                if o.dma is not None:
                    ins.then_inc(sems[o.key], 16)
                elif o.needs:
                    ins.then_inc(sems[o.key], 1)

        @block.tensor
        def _(e):
            run("pe", e)

        @block.scalar
        def _(e):
            run("act", e)

        @block.vector
        def _(e):
            run("dve", e)

        @block.gpsimd
        def _(e):
            run("pool", e)

        @block.sync
        def _(e):
            run("sp", e)


def build_program(L, NSEQ, DEPTH, cfg=None):
    cfg = cfg or {}
    NT = L // 512
    NB = L // 128
    nc = bass.Bass("TRN2", target_bir_lowering=False)
    dr = {}

    def din(name, shape, dt=F32):
        dr[name] = nc.dram_tensor(name, list(shape), dt, kind="ExternalInput").ap()
        return dr[name]

    x_d = din("x", [NSEQ, L, D])
    pos_d = din("positions", [NSEQ, L], I32)
    cst_d = din("cst", [128, 640])
    pcols_d = din("pcols", [DEPTH, NPC, 128])
    prow_d = din("prow", [DEPTH, NPR])
    f1gu = din("ffn1_w_gu", [DEPTH, D, 2 * DFF])
    f1dn = din("ffn1_w_down", [DEPTH, DFF, D])
    f2gu = din("ffn2_w_gu", [DEPTH, D, 2 * DFF])
    f2dn = din("ffn2_w_down", [DEPTH, DFF, D])
    win_d = din("w_in", [DEPTH, D, INTOT])
    ws_d = din("gmlp_w_s", [DEPTH, 4, 128, 128])
    bs_d = din("gmlp_b_s", [DEPTH, 512])
    vnw_d = din("gmlp_v_norm", [DEPTH, 512])
    wqb_d = din("mla_w_qb", [DEPTH, 384, 768])
    wkvb_d = din("mla_w_kvb", [DEPTH, 128, 1024])
    wbr_d = din("w_branch", [DEPTH, 4, 512, D])
    wout_d = din("w_out", [DEPTH, D, D])
    out_d = nc.dram_tensor("out", [NSEQ, L, D], F32, kind="ExternalOutput").ap()

    es = ExitStack()
    S = Sched(nc, es)

    xT = S.sb("xT", [128, NCH, L], F32)
    xT_t = [S.T("xT%d" % i) for i in range(NT)]
    NRING = 6
    ring = [S.sb("ring%d" % i, [128, 4096], BF16) for i in range(NRING)]
    ring_t = [S.T("ring%d" % i, True) for i in range(NRING)]
    ring_pos = [0]
    cst = S.sb("cst", [128, 640], F32)
    cst_t = S.T("cst", True)
    ident = cst[:, 0:128]
    tri = cst[:, 128:256]
    maskneg = cst[:, 256:384]
    ones32 = cst[:, 384:512]
    cbf = S.sb("cbf", [128, 384], BF16)
    cbf_t = S.T("cbf", True)
    ones16 = cbf[:, 0:128]
    tri16 = cbf[:, 128:256]
    ident16 = cbf[:, 256:384]
    prow = S.sb("prow", [128, NPR], F32)
    prow_t = S.T("prow", True)
    expA = S.sb("expA", [128, 8], F32)
    expA_t = S.T("expA", True)
    pc = S.sb("pc", [128, NPC], F32)
    pc_t = S.T("pc", True)
    pcst = S.sb("pcst", [NPC, 128], F32)
    pcst_t = S.T("pcst", True)
    ARENA = 90 * 1024
    arena = S.sb("arena", [128, ARENA // 4], F32)

    def carve(off, shape, dt):
        n = 1
        for s in shape[1:]:
            n *= s
        bpe = 4 if dt in (F32, I32) else 2
        assert off % 4 == 0 and off + n * bpe <= ARENA, (off, shape)
        v = arena[0:shape[0], off // 4: off // 4 + (n * bpe) // 4]
        if dt != F32:
            v = v.bitcast(dt)
        if len(shape) == 3:
            v = v.rearrange("p (a b) -> p a b", b=shape[2])
        elif len(shape) == 4:
            v = v.rearrange("p (a b c) -> p a b c", b=shape[2], c=shape[3])
        return v

    psb = [es.enter_context(nc.psum_tensor("ps%d" % i, [128, 512], F32)) for i in range(8)]
    psb_t = [S.T("ps%d" % i) for i in range(8)]
    ps_pos = [0]

    ps_n = [8]

    def ps():
        i = ps_pos[0] % ps_n[0]
        ps_pos[0] += 1
        return psb[i], psb_t[i]

    def ring_next():
        i = ring_pos[0] % NRING
        ring_pos[0] += 1
        return ring[i], ring_t[i]

    def fsz(ap):
        n = 1
        for d_ in ap.shape[1:]:
            n *= d_
        return n

    def mm(out, lhsT, rhs, start, stop, reads, writes):
        S.op("pe", lambda e: e.matmul(out, lhsT, rhs, start=start, stop=stop), reads, writes)

    def tr(out, in_, idn, reads, writes):
        S.op("pe", lambda e: e.transpose(out, in_, idn), reads, writes)

    def act(out, in_, func, reads, writes, bias=None, scale=None):
        kw = {}
        if bias is not None:
            kw["bias"] = bias
        if scale is not None:
            kw["scale"] = scale
        S.op("act", lambda e: e.activation(out=out, in_=in_, func=func, **kw), reads, writes, fs=fsz(out))

    def tt(out, in0, in1, op, reads, writes, eng="dve"):
        S.op(eng, lambda e: e.tensor_tensor(out=out, in0=in0, in1=in1, op=op), reads, writes, fs=fsz(out))

    def ts(out, in0, s1, s2, op0, op1, reads, writes, eng="dve"):
        if op1 is None:
            S.op(eng, lambda e: e.tensor_scalar(out=out, in0=in0, scalar1=s1, scalar2=None, op0=op0), reads, writes, fs=fsz(out))
        else:
            S.op(eng, lambda e: e.tensor_scalar(out=out, in0=in0, scalar1=s1, scalar2=s2, op0=op0, op1=op1), reads, writes, fs=fsz(out))

    def stt(out, in0, scalar, in1, op0, op1, reads, writes, eng="dve"):
        S.op(eng, lambda e: e.scalar_tensor_tensor(out=out, in0=in0, scalar=scalar, in1=in1, op0=op0, op1=op1), reads, writes, fs=fsz(out))

    def cp(out, in_, reads, writes, eng="dve"):
        S.op(eng, lambda e: e.tensor_copy(out=out, in_=in_), reads, writes, fs=fsz(out))

    def dma(eng, out, in_, tile, reads=(), writes=()):
        S.op(eng, lambda e: e.dma_start(out=out, in_=in_), reads, writes, dma_tile=tile)

    def wload(view_out, src, page_t):
        dma("pool", view_out, src, page_t, writes=[page_t])

    dma("sp", cst[:], cst_d, cst_t, writes=[cst_t])
    cp(ones16, ones32, [cst_t], [cbf_t])
    cp(tri16, tri, [cst_t], [cbf_t])
    cp(ident16, ident, [cst_t], [cbf_t])

    def load_layer_params(l):
        dma("sp", pcst[:], pcols_d[l], pcst_t, writes=[pcst_t])
        p_, pt_ = ps()
        tr(p_[:, 0:NPC], pcst[:], ident[0:NPC, 0:NPC], [pcst_t, cst_t], [pt_])
        cp(pc[:], p_[:, 0:NPC], [pt_], [pc_t])
        dma("sp", prow[:], prow_d[l:l + 1, :].partition_broadcast(128), prow_t, writes=[prow_t])
        act(expA[:], prow[:, 8:16], AF.Exp, [prow_t], [expA_t])

    def load_x(s):
        stg = [carve(i * 4096, [128, 1024], F32) for i in range(2)]
        stg_t = [S.PT("stg%d" % i) for i in range(2)]
        for b in range(NB):
            st, st_t = stg[b % 2], stg_t[b % 2]
            dma("sp", st, x_d[s, b * 128:(b + 1) * 128, :], st_t, writes=[st_t])
            for half in range(2):
                p_, pt_ = ps()
                for c4 in range(4):
                    c = half * 4 + c4
                    tr(p_[:, c4 * 128:(c4 + 1) * 128], st[:, c * 128:(c + 1) * 128], ident, [st_t, cst_t], [pt_])
                S.op("act", (lambda e, p_=p_, half=half, b=b: e.activation(
                    out=xT[:, half * 4:half * 4 + 4, b * 128:(b + 1) * 128],
                    in_=p_[:].rearrange("p (a b) -> p a b", b=128), func=AF.Copy)),
                    [pt_], [xT_t[b // 4]])

    def store_x(s):
        stg = [carve(i * 4096, [128, 1024], F32) for i in range(2)]
        stg_t = [S.PT("ostg%d" % i) for i in range(2)]
        for b in range(NB):
            st, st_t = stg[b % 2], stg_t[b % 2]
            for half in range(2):
                p_, pt_ = ps()
                for c4 in range(4):
                    c = half * 4 + c4
                    tr(p_[:, c4 * 128:(c4 + 1) * 128], xT[:, c, b * 128:(b + 1) * 128], ident, [xT_t[b // 4], cst_t], [pt_])
                act(st[:, half * 512:(half + 1) * 512], p_[:], AF.Copy, [pt_], [st_t])
            dma("sp", out_d[s, b * 128:(b + 1) * 128, :], st, st_t, reads=[st_t])

    def rsqrt_from_ss(out, ss, inv_n, reads, out_t):
        ts(out, ss, inv_n, EPS, ALU.mult, ALU.add, reads, [out_t])
        act(out, out, AF.Sqrt, [out_t], [out_t])
        S.op("dve", lambda e: e.reciprocal(out=out, in_=out), [out_t], [out_t], fs=fsz(out))

    def rmsnorm_tile(tt_i, wcol0, hT, hT_tt, sq, sq_t, rstd, rstd_t):
        tok = slice(tt_i * 512, (tt_i + 1) * 512)
        p_, pt_ = ps()
        for c in range(NCH):
            q, q_t = sq[c % len(sq)], sq_t[c % len(sq)]
            act(q, xT[:, c, tok], AF.Square, [xT_t[tt_i]], [q_t])
            mm(p_[:], ones16, q, c == 0, c == NCH - 1, [q_t, cbf_t], [pt_])
        rsqrt_from_ss(rstd, p_[:], 1.0 / D, [pt_], rstd_t)
        for c in range(NCH):
            stt(hT[:, c, tok], xT[:, c, tok], pc[:, wcol0 + c:wcol0 + c + 1], rstd, ALU.mult, ALU.mult,
                [xT_t[tt_i], pc_t, rstd_t], [hT_tt[tt_i]])

    FF_PARTS = [(0, 8), (8, 15), (15, 22)]

    def ffn(l, wgu_d, wdn_d, normcol):
        S.barrier()
        off = 0
        hT = carve(off, [128, NCH, L], BF16); off += NCH * L * 2
        aT = carve(off, [128, 8, L], BF16); off += 8 * L * 2
        sq = [carve(off + i * 1024, [128, 512], BF16) for i in range(3)]; off += 3 * 1024
        sg = [carve(off + i * 2048, [128, 512], F32) for i in range(3)]; off += 3 * 2048
        rstd = carve(off, [128, 512], F32); off += 2048
        hT_tt = [S.T("hT%d" % i) for i in range(NT)]
        aT_tt = [S.T("aT%d" % i) for i in range(NT)]
        sq_t = [S.T("sq%d" % i) for i in range(3)]
        sg_t = [S.T("sg%d" % i) for i in range(3)]
        rstd_t = S.T("rstd")
        for t_i in range(NT):
            rmsnorm_tile(t_i, normcol, hT, hT_tt, sq, sq_t, rstd, rstd_t)
        wgu = wgu_d[l].rearrange("(kc p) n -> p kc n", p=128)
        wdn = wdn_d[l].rearrange("(j p) n -> p j n", p=128)
        sgi = 0
        for (j0, j1) in FF_PARTS:
            j = j0
            while j < j1:
                nb = min(4, j1 - j)
                pg, pg_t = ring_next()
                pu, pu_t = ring_next()
                wg_v = pg[:, 0:8 * nb * 128].rearrange("p (a b) -> p a b", b=nb * 128)
                wu_v = pu[:, 0:8 * nb * 128].rearrange("p (a b) -> p a b", b=nb * 128)
                wload(wg_v, wgu[:, :, j * 128:(j + nb) * 128], pg_t)
                wload(wu_v, wgu[:, :, DFF + j * 128:DFF + (j + nb) * 128], pu_t)
                for jj in range(nb):
                    for t_i in range(NT):
                        tok = slice(t_i * 512, (t_i + 1) * 512)
                        g_, gt_ = ps()
                        u_, ut_ = ps()
                        for k in range(NCH):
                            mm(g_[:], wg_v[:, k, jj * 128:(jj + 1) * 128], hT[:, k, tok], k == 0, k == NCH - 1,
                               [pg_t, hT_tt[t_i]], [gt_])
                        for k in range(NCH):
                            mm(u_[:], wu_v[:, k, jj * 128:(jj + 1) * 128], hT[:, k, tok], k == 0, k == NCH - 1,
                               [pu_t, hT_tt[t_i]], [ut_])
                        s_, st_ = sg[sgi % 3], sg_t[sgi % 3]
                        sgi += 1
                        act(s_, g_[:], AF.Silu, [gt_], [st_])
                        tt(aT[:, j + jj - j0, tok], u_[:], s_, ALU.mult, [ut_, st_], [aT_tt[t_i]])
                j += nb
            nj = j1 - j0
            pages = []
            j = 0
            while j < nj:
                nb = min(4, nj - j)
                pd, pd_t = ring_next()
                wd_v = pd[:, 0:nb * 1024].rearrange("p (a b) -> p a b", b=1024)
                wload(wd_v, wdn[:, j0 + j:j0 + j + nb, :], pd_t)
                for jj in range(nb):
                    pages.append((wd_v, jj, pd_t))
                j += nb
            for oc in range(NCH):
                for t_i in range(NT):
                    tok = slice(t_i * 512, (t_i + 1) * 512)
                    d_, dt_ = ps()
                    for jx in range(nj):
                        wd_v, jj, pd_t = pages[jx]
                        mm(d_[:], wd_v[:, jj, oc * 128:(oc + 1) * 128], aT[:, jx, tok], jx == 0, jx == nj - 1,
                           [pd_t, aT_tt[t_i]], [dt_])
                    stt(xT[:, oc, tok], d_[:], 0.5, xT[:, oc, tok], ALU.mult, ALU.add, [dt_, xT_t[t_i]], [xT_t[t_i]])


    WS0 = 48 * 1024

    class WSAlloc:
        def __init__(self):
            self.off = WS0

        def get(self, shape, dt):
            n = 1
            for d_ in shape[1:]:
                n *= d_
            nb = n * (4 if dt in (F32, I32) else 2)
            nb = (nb + 3) // 4 * 4
            v = carve(self.off, shape, dt)
            self.off += nb
            return v

    def bc_mid(ap2d, n):
        return ap2d.unsqueeze(1).to_broadcast([ap2d.shape[0], n, ap2d.shape[1]])

    def bc_last(ap2d, n):
        return ap2d.unsqueeze(2).to_broadcast([ap2d.shape[0], ap2d.shape[1], n])

    def v3(ap2d, b):
        return ap2d.rearrange("p (a b) -> p a b", b=b)

    def mixer_layer(l, s):
        S.barrier()
        hT = carve(0, [128, NCH, L], BF16)
        yT = carve(32 * 1024, [128, 4, L], BF16)
        hT_tt = [S.T("mhT%d" % i) for i in range(NT)]
        yT_tt = [S.T("yT%d" % i) for i in range(NT)]
        win = win_d[l].rearrange("(kc p) n -> p kc n", p=128)

        def wpage(c0, ncols):
            pg, pg_t = ring_next()
            v = pg[:, 0:8 * ncols].rearrange("p (a b) -> p a b", b=ncols)
            wload(v, win[:, :, c0:c0 + ncols], pg_t)
            return v, pg_t

        wsn = WSAlloc()
        sq = [wsn.get([128, 512], BF16) for _ in range(3)]
        sq_t = [S.T("msq%d" % i) for i in range(3)]
        rstd = wsn.get([128, 512], F32)
        rstd_t = S.T("mrstd")
        for t_i in range(NT):
            rmsnorm_tile(t_i, 8, hT, hT_tt, sq, sq_t, rstd, rstd_t)

        def gating(i):
            S.barrier()
            ws = WSAlloc()
            gated = ws.get([128, 8, 512], BF16)
            gated_t = S.T("gated")
            sig = [ws.get([128, 512], F32) for _ in range(2)]
            sig_t = [S.T("sig%d" % k) for k in range(2)]
            pb, pb_t = ring_next()
            wbr = pb[:, 0:4096].rearrange("p (a b) -> p a b", b=1024)
            wload(wbr, wbr_d[l, i].rearrange("(kc p) n -> p kc n", p=128), pb_t)
            wg = [wpage(O_G + i * 1024 + hh * 512, 512) for hh in range(2)]
            wo = []
            wov = wout_d[l].rearrange("(kc p) n -> p kc n", p=128)
            for hh in range(2):
                pg, pg_t = ring_next()
                v = pg[:, 0:4096].rearrange("p (a b) -> p a b", b=512)
                wload(v, wov[:, :, hh * 512:(hh + 1) * 512], pg_t)
                wo.append((v, pg_t))
            si = 0
            for t_i in range(NT):
                tok = slice(t_i * 512, (t_i + 1) * 512)
                for oc in range(8):
                    per_, pert_ = ps()
                    for kc in range(4):
                        mm(per_[:], wbr[:, kc, oc * 128:(oc + 1) * 128], yT[:, kc, tok], kc == 0, kc == 3,
                           [pb_t, yT_tt[t_i]], [pert_])
                    g_, gt_ = ps()
                    wgv, wg_t = wg[oc // 4]
                    for k in range(8):
                        mm(g_[:], wgv[:, k, (oc % 4) * 128:(oc % 4 + 1) * 128], hT[:, k, tok], k == 0, k == 7,
                           [wg_t, hT_tt[t_i]], [gt_])
                    sg_, sgt_ = sig[si % 2], sig_t[si % 2]
                    si += 1
                    act(sg_, g_[:], AF.Sigmoid, [gt_], [sgt_])
                    tt(gated[:, oc, :], per_[:], sg_, ALU.mult, [pert_, sgt_], [gated_t])
                for oc2 in range(8):
                    o_, ot_ = ps()
                    wov_, wo_t = wo[oc2 // 4]
                    for oc in range(8):
                        mm(o_[:], wov_[:, oc, (oc2 % 4) * 128:(oc2 % 4 + 1) * 128], gated[:, oc, :], oc == 0, oc == 7,
                           [wo_t, gated_t], [ot_])
                    tt(xT[:, oc2, tok], o_[:], xT[:, oc2, tok], ALU.add, [ot_, xT_t[t_i]], [xT_t[t_i]])
            S.barrier()

        def branch_d():
            S.barrier()
            ws = WSAlloc()
            tbuf = ws.get([128, 516], F32)
            tbuf_t = S.T("tbuf")
            acc = ws.get([128, 512], F32)
            acc_t = S.T("dacc")
            cgs = ws.get([128, 512], F32)
            cgs_t = S.T("cgs")
            wb = wpage(O_SC, 512)
            wc = wpage(O_SC + 512, 512)
            wx = wpage(O_SC + 1024, 512)
            for c in range(4):
                for t_i in range(NT):
                    tok = slice(t_i * 512, (t_i + 1) * 512)
                    pss = []
                    for (wv, w_t) in (wb, wc, wx):
                        p_, pt_ = ps()
                        for k in range(8):
                            mm(p_[:], wv[:, k, c * 128:(c + 1) * 128], hT[:, k, tok], k == 0, k == 7,
                               [w_t, hT_tt[t_i]], [pt_])
                        pss.append((p_, pt_))
                    (b_, bt_), (c_, ct_), (x_, xt_) = pss
                    act(cgs, c_[:], AF.Copy, [ct_], [cgs_t])
                    if t_i == 0:
                        S.op("dve", lambda e: e.memset(tbuf[:, 0:2], 0.0), [], [tbuf_t])
                    else:
                        cp(tbuf[:, 0:2], tbuf[:, 512:514], [tbuf_t], [tbuf_t])
                    tt(tbuf[:, 2:514], x_[:], cgs, ALU.mult, [xt_, cgs_t], [tbuf_t])
                    ts(acc, tbuf[:, 0:512], pc[:, 78 + c:79 + c], None, ALU.mult, None, [tbuf_t, pc_t], [acc_t])
                    for k in (1, 2):
                        stt(acc, tbuf[:, k:k + 512], pc[:, 78 + k * 4 + c:79 + k * 4 + c], acc, ALU.mult, ALU.add,
                            [tbuf_t, pc_t, acc_t], [acc_t])
                    tt(yT[:, c, tok], b_[:], acc, ALU.mult, [bt_, acc_t], [yT_tt[t_i]])

        def branch_b():
            S.barrier()
            ws = WSAlloc()
            wstg = ws.get([128, 4, 128], F32)
            wstg_t = S.PT("wstg")
            wsT = ws.get([128, 4, 128], BF16)
            wsT_t = S.T("wsT")
            bsrow = ws.get([1, 512], BF16)
            bsrow_t = S.T("bsrow")
            vnw = ws.get([128, 512], F32)
            vnw_t = S.PT("vnw")
            dma("sp", vnw, vnw_d[l:l + 1, :].partition_broadcast(128), vnw_t, writes=[vnw_t])
            vgs = [ws.get([128, 512], F32) for _ in range(2)]
            vg_ts = [S.T("vg%d" % i) for i in range(2)]
            vsqs = [ws.get([128, 512], F32) for _ in range(2)]
            vsq_ts = [S.T("vsq%d" % i) for i in range(2)]
            vsss = [ws.get([128, 2], F32) for _ in range(2)]
            vss_ts = [S.T("vss%d" % i) for i in range(2)]
            vns = [ws.get([128, 512], BF16) for _ in range(2)]
            vn_ts = [S.T("vn%d" % i) for i in range(2)]
            dma("sp", wstg, ws_d[l].rearrange("g t s -> t g s"), wstg_t, writes=[wstg_t])
            bsf = ws.get([1, 512], F32)
            bsf_t = S.PT("bsf")
            dma("sp", bsf, bs_d[l:l + 1, :], bsf_t, writes=[bsf_t])
            cp(bsrow, bsf, [bsf_t], [bsrow_t])
            p_, pt_ = ps()
            for g in range(4):
                tr(p_[:, g * 128:(g + 1) * 128], wstg[:, g, :], ident, [wstg_t, cst_t], [pt_])
            tt(wsT, v3(p_[:], 128), bc_mid(tri, 4), ALU.mult, [pt_, cst_t], [wsT_t])
            wu = wpage(O_UV, 512)
            wv = wpage(O_UV + 512, 512)
            for t_i in range(NT):
                tok = slice(t_i * 512, (t_i + 1) * 512)
                for c in range(4):
                    u_, ut_ = ps()
                    for k in range(8):
                        mm(u_[:], wu[0][:, k, c * 128:(c + 1) * 128], hT[:, k, tok], k == 0, k == 7,
                           [wu[1], hT_tt[t_i]], [ut_])
                    act(yT[:, c, tok], u_[:], AF.Gelu, [ut_], [yT_tt[t_i]])
                for b in range(4):
                    tb = slice(t_i * 512 + b * 128, t_i * 512 + (b + 1) * 128)
                    bi = b % 2
                    vg, vg_t, vsq, vsq_t = vgs[bi], vg_ts[bi], vsqs[bi], vsq_ts[bi]
                    vss, vss_t, vn, vn_t = vsss[bi], vss_ts[bi], vns[bi], vn_ts[bi]
                    v_, vt_ = ps()
                    for k in range(8):
                        mm(v_[:], hT[:, k, tb], wv[0][:, k, :], k == 0, k == 7, [wv[1], hT_tt[t_i]], [vt_])
                    act(vg, v_[:], AF.Gelu, [vt_], [vg_t])
                    tt(vsq, vg, vg, ALU.mult, [vg_t], [vsq_t])
                    S.op("dve", (lambda e, vss=vss, vsq=vsq: e.reduce_sum(out=vss[:, 0:1], in_=vsq, axis=mybir.AxisListType.X)), [vsq_t], [vss_t])
                    rsqrt_from_ss(vss[:, 1:2], vss[:, 0:1], 1.0 / 512, [vss_t], vss_t)
                    stt(vn, vg, vss[:, 1:2], vnw, ALU.mult, ALU.mult, [vg_t, vss_t, vnw_t], [vn_t])
                    sv_, svt_ = ps()
                    for g in range(4):
                        mm(sv_[:, g * 128:(g + 1) * 128], vn[:, g * 128:(g + 1) * 128], wsT[:, g, :], True, False,
                           [vn_t, wsT_t], [svt_])
                        mm(sv_[:, g * 128:(g + 1) * 128], ones16[0:1, :], bsrow[0:1, g * 128:(g + 1) * 128], False, True,
                           [cbf_t, bsrow_t], [svt_])
                    tt(yT[:, :, tb], yT[:, :, tb], v3(sv_[:], 128), ALU.mult, [svt_, yT_tt[t_i]], [yT_tt[t_i]])

        def branch_a():
            S.barrier()
            ws = WSAlloc()
            xbcT = ws.get([128, 8, 512], BF16); xbcT_t = S.T("xbcT")
            rawh = ws.get([128, 516], F32); rawh_t = S.T("rawh")
            acc = ws.get([128, 512], F32); acc_t = S.T("aacc")
            halo = ws.get([128, 8, 4], F32); halo_t = S.T("halo")
            dtr = ws.get([128, 32], F32); dtr_t = S.T("dtr")
            dtv = ws.get([128, 32], F32); dtv_t = S.T("dtv")
            av = ws.get([128, 32], F32); av_t = S.T("av")
            sm = ws.get([128, 128], F32); sm_t = S.T("sm")
            Rb = ws.get([128, 8, 128], F32); Rb_t = S.T("Rb")
            W1 = ws.get([128, 8, 128], F32); W1_t = S.T("W1")
            MT = ws.get([128, 8, 128], BF16); MT_t = S.T("MT")
            xdt = ws.get([128, 8, 64], BF16); xdt_t = S.T("xdt")
            xw = ws.get([128, 8, 64], BF16); xw_t = S.T("xw")
            Btok = ws.get([128, 256], BF16); Btok_t = S.T("Btok")
            st32 = ws.get([128, 8, 64], F32); st32_t = S.T("st32")
            st16 = ws.get([128, 512], BF16); st16_t = S.T("st16")
            ysb = ws.get([128, 512], F32); ysb_t = S.T("ysb")
            yraw = ws.get([128, 4, 512], F32); yraw_t = S.T("yraw")
            sz = acc; sz_t = acc_t
            gsq = MT.rearrange("p a b -> p (a b)")[:, 0:512]; gsq_t = MT_t
            grs = rawh[:, 0:512]; grs_t = rawh_t
            wz = wpage(O_Z, 512)
            wx0 = wpage(O_XBC, 512)
            wx1 = wpage(O_XBC + 512, 512)
            wdt = wpage(O_DT, 8)
            wxb = (wx0, wx1)
            acs4, eacs4, dout4, eatot4 = sm[:, 0:32], sm[:, 32:64], sm[:, 64:96], sm[:, 96:128]
            for t_i in range(NT):
                tok = slice(t_i * 512, (t_i + 1) * 512)
                for f in range(8):
                    p_, pt_ = ps()
                    wv, w_t = wxb[f // 4]
                    for k in range(8):
                        mm(p_[:], wv[:, k, (f % 4) * 128:(f % 4 + 1) * 128], hT[:, k, tok], k == 0, k == 7,
                           [w_t, hT_tt[t_i]], [pt_])
                    if t_i == 0:
                        S.op("dve", lambda e: e.memset(rawh[:, 0:3], 0.0), [], [rawh_t])
                    else:
                        cp(rawh[:, 0:3], halo[:, f, 0:3], [halo_t], [rawh_t])
                    act(rawh[:, 3:515], p_[:], AF.Copy, [pt_], [rawh_t])
                    cp(halo[:, f, 0:3], rawh[:, 512:515], [rawh_t], [halo_t])
                    ts(acc, rawh[:, 0:512], pc[:, 24 + f:25 + f], None, ALU.mult, None, [rawh_t, pc_t], [acc_t])
                    for k in (1, 2, 3):
                        stt(acc, rawh[:, k:k + 512], pc[:, 24 + k * 8 + f:25 + k * 8 + f], acc, ALU.mult, ALU.add,
                            [rawh_t, pc_t, acc_t], [acc_t])
                    act(xbcT[:, f, :], acc, AF.Silu, [acc_t, pc_t], [xbcT_t], bias=pc[:, 56 + f:57 + f])
                d_, dt_ = ps()
                for c in range(4):
                    for k in range(8):
                        mm(d_[:, c * 8:(c + 1) * 8], hT[:, k, t_i * 512 + c * 128:t_i * 512 + (c + 1) * 128], wdt[0][:, k, :],
                           k == 0, k == 7, [wdt[1], hT_tt[t_i]], [dt_])
                tt(v3(dtr, 8), v3(d_[:, 0:32], 8), bc_mid(prow[:, 0:8], 4), ALU.add, [dt_, prow_t], [dtr_t])
                act(dtr, dtr, AF.Exp, [dtr_t], [dtr_t])
                act(dtv, dtr, AF.Ln, [dtr_t], [dtv_t], bias=1.0)
                stt(v3(av, 8), v3(dtv, 8), -1.0, bc_mid(expA[:], 4), ALU.mult, ALU.mult, [dtv_t, expA_t], [av_t])
                cu_, cut_ = ps()
                mm(cu_[:, 0:32], tri, av, True, True, [cst_t, av_t], [cut_])
                mm(cu_[:, 32:64], ones32, av, True, True, [cst_t, av_t], [cut_])
                act(acs4, cu_[:, 0:32], AF.Copy, [cut_], [sm_t])
                act(eacs4, cu_[:, 0:32], AF.Exp, [cut_], [sm_t])
                act(eatot4, cu_[:, 32:64], AF.Exp, [cut_], [sm_t])
                tt(dout4, cu_[:, 32:64], acs4, ALU.subtract, [cut_, sm_t], [sm_t])
                act(dout4, dout4, AF.Exp, [sm_t], [sm_t])
                for c in range(4):
                    gc = t_i * 4 + c
                    ct = slice(c * 128, (c + 1) * 128)
                    a_c = av[:, c * 8:(c + 1) * 8]
                    acs = acs4[:, c * 8:(c + 1) * 8]
                    eacs = eacs4[:, c * 8:(c + 1) * 8]
                    dout = dout4[:, c * 8:(c + 1) * 8]
                    eatot = eatot4[:, c * 8:(c + 1) * 8]
                    tt(Rb, bc_mid(tri, 8), bc_last(a_c, 128), ALU.mult, [cst_t, av_t], [Rb_t])
                    cb_, cbt_ = ps()
                    for g in range(2):
                        mm(cb_[:, g * 128:(g + 1) * 128], xbcT[:, 4 + g, ct], xbcT[:, 6 + g, ct], True, True,
                           [xbcT_t], [cbt_])
                    for g in range(2):
                        bc_, bct_ = ps()
                        mm(bc_[:], ones32, Rb[:, g * 4:(g + 1) * 4, :].rearrange("p a b -> p (a b)"), True, True,
                           [cst_t, Rb_t], [bct_])
                        tt(W1[:, g * 4:(g + 1) * 4, :], v3(bc_[:], 128), bc_mid(maskneg, 4), ALU.add, [bct_, cst_t], [W1_t])
                        tt(W1[:, g * 4:(g + 1) * 4, :], W1[:, g * 4:(g + 1) * 4, :], bc_last(acs[:, g * 4:(g + 1) * 4], 128),
                           ALU.subtract, [W1_t, sm_t], [W1_t])
                    act(W1, W1, AF.Exp, [W1_t], [W1_t])
                    for g in range(2):
                        tt(MT[:, g * 4:(g + 1) * 4, :], W1[:, g * 4:(g + 1) * 4, :], bc_mid(cb_[:, g * 128:(g + 1) * 128], 4),
                           ALU.mult, [W1_t, cbt_], [MT_t])
                    xs_, xst_ = ps()
                    xs16 = xs_[:].bitcast(BF16)
                    for cc in range(4):
                        tr(xs16[:, cc * 128:(cc + 1) * 128], xbcT[:, cc, ct], ident16, [xbcT_t, cbf_t], [xst_])
                    tt(xdt, v3(xs16[:, 0:512], 64), bc_last(dtv[:, c * 8:(c + 1) * 8], 64), ALU.mult, [xst_, dtv_t], [xdt_t])
                    tt(xw, xdt, bc_last(dout, 64), ALU.mult, [xdt_t, sm_t], [xw_t])
                    b_, bt_ = ps()
                    b16 = b_[:].bitcast(BF16)
                    for g in range(2):
                        tr(b16[:, g * 128:(g + 1) * 128], xbcT[:, 4 + g, ct], ident16, [xbcT_t, cbf_t], [bt_])
                    act(Btok, b16[:, 0:256], AF.Copy, [bt_], [Btok_t])
                    y_, yt_ = ps()
                    for h in range(8):
                        mm(y_[:, h * 64:(h + 1) * 64], MT[:, h, :], xdt[:, h, :], True, True, [MT_t, xdt_t], [yt_])
                    if gc > 0:
                        yo_, yot_ = ps()
                        for g in range(2):
                            mm(yo_[:, g * 256:(g + 1) * 256], xbcT[:, 6 + g, ct], st16[:, g * 256:(g + 1) * 256], True, True,
                               [xbcT_t, st16_t], [yot_])
                        tt(v3(ysb, 64), v3(yo_[:], 64), bc_last(eacs, 64), ALU.mult, [yot_, sm_t], [ysb_t])
                        tt(ysb, ysb, y_[:], ALU.add, [ysb_t, yt_], [ysb_t])
                    else:
                        cp(ysb, y_[:], [yt_], [ysb_t])
                    s_, st_ = ps()
                    for g in range(2):
                        mm(s_[:, g * 256:(g + 1) * 256], Btok[:, g * 128:(g + 1) * 128],
                           xw[:, g * 4:(g + 1) * 4, :].rearrange("p a b -> p (a b)"), True, True, [Btok_t, xw_t], [st_])
                    if gc > 0:
                        tt(st32, st32, bc_last(eatot, 64), ALU.mult, [st32_t, sm_t], [st32_t])
                        tt(st32, st32, v3(s_[:], 64), ALU.add, [st32_t, st_], [st32_t])
                    else:
                        cp(st32, v3(s_[:], 64), [st_], [st32_t])
                    act(st16, st32.rearrange("p a b -> p (a b)"), AF.Copy, [st32_t], [st16_t])
                    yT_, yTt_ = ps()
                    for cc in range(4):
                        tr(yT_[:, cc * 128:(cc + 1) * 128], ysb[:, cc * 128:(cc + 1) * 128], ident, [ysb_t, cst_t], [yTt_])
                    act(yraw[:, :, ct], v3(yT_[:], 128), AF.Copy, [yTt_], [yraw_t])
                for cc in range(4):
                    stt(yraw[:, cc, :], xbcT[:, cc, :], pc[:, 96 + cc:97 + cc], yraw[:, cc, :], ALU.mult, ALU.add,
                        [xbcT_t, pc_t, yraw_t], [yraw_t])
                for cc in range(4):
                    z_, zt_ = ps()
                    for k in range(8):
                        mm(z_[:], wz[0][:, k, cc * 128:(cc + 1) * 128], hT[:, k, tok], k == 0, k == 7,
                           [wz[1], hT_tt[t_i]], [zt_])
                    act(sz, z_[:], AF.Silu, [zt_], [sz_t])
                    tt(yraw[:, cc, :], yraw[:, cc, :], sz, ALU.mult, [yraw_t, sz_t], [yraw_t])
                for g in range(2):
                    ss_, sst_ = ps()
                    for j, cc in enumerate((2 * g, 2 * g + 1)):
                        act(gsq, yraw[:, cc, :], AF.Square, [yraw_t], [gsq_t])
                        mm(ss_[:], ones16, gsq, j == 0, j == 1, [gsq_t, cbf_t], [sst_])
                    rsqrt_from_ss(grs, ss_[:], 1.0 / 256, [sst_], grs_t)
                    for cc in (2 * g, 2 * g + 1):
                        stt(yT[:, cc, tok], yraw[:, cc, :], pc[:, 64 + cc:65 + cc], grs, ALU.mult, ALU.mult,
                            [yraw_t, pc_t, grs_t], [yT_tt[t_i]])


        def branch_c():
            S.barrier()
            import math
            ws = WSAlloc()
            qnT = ws.get([128, 3, L], BF16); qnT_tt = [S.T("qnT%d" % i) for i in range(NT)]
            kvnT = ws.get([128, L], BF16); kvnT_tt = [S.T("kvnT%d" % i) for i in range(NT)]
            kper = ws.get([64, L], F32); kper_tt = [S.T("kper%d" % i) for i in range(NT)]
            sqkpe = ws.get([64, L], BF16); sqkpe_tt = [S.T("sqkpe%d" % i) for i in range(NT)]
            sqb = ws.get([128, 512], BF16); sqb_t = S.T("csq")
            rs = ws.get([128, 512], F32); rs_t = S.T("crs")
            t1 = ws.get([128, 512], F32); t1_t = S.PT("ct1")
            t2 = ws.get([128, 512], F32); t2_t = S.T("ct2")
            Qns = [ws.get([128, 512], BF16) for _ in range(2)]; Qn_ts = [S.T("Qn%d" % i) for i in range(2)]
            Qrs = [ws.get([64, 512], BF16) for _ in range(2)]; Qr_ts = [S.T("Qr%d" % i) for i in range(2)]
            pT = [ws.get([128, 512], BF16) for _ in range(3)]; pT_t = [S.T("pT%d" % i) for i in range(3)]
            pq, pq_t = ring_next()
            wql = pq[:, 0:8 * 384].rearrange("p (a b) -> p a b", b=384)
            wload(wql, win[:, :, O_QL:O_QL + 384], pq_t)
            pk, pk_t = ring_next()
            wkl = pk[:, 0:8 * 192].rearrange("p (a b) -> p a b", b=192)
            wload(wkl, win[:, :, O_KVL:O_KVL + 192], pk_t)
            pk2, pk2_t = ring_next()
            wks = pk2[:, 0:8 * 64].rearrange("p (a b) -> p a b", b=64)
            wks_b = pk2[:, 1024:1024 + 8 * 64].rearrange("p (a b) -> p a b", b=64)
            dma("pool", wks[:, :, 0:32], win[:, :, O_KPE + 32:O_KPE + 64], pk2_t, writes=[pk2_t])
            dma("pool", wks[:, :, 32:64], win[:, :, O_KPE:O_KPE + 32], pk2_t, writes=[pk2_t])
            pcs, pcs_t = ring_next()
            cs32 = pcs[:].bitcast(F32)
            assert L <= 1024 or True
            cos2 = None
            if 2 * L * 4 <= 8192:
                cos2 = cs32[0:64, 0:L]
                sin2 = cs32[0:64, L:2 * L]
                sin_t = pcs_t
            else:
                pcs2, pcs2_t = ring_next()
                cos2 = cs32[0:64, 0:L]
                sin2 = pcs2[:].bitcast(F32)[0:64, 0:L]
                sin_t = pcs2_t
            cos_t = pcs_t
            invf = cst[0:64, 512:513]
            sgn = cst[0:64, 513:514]
            TWO_PI = 2.0 * math.pi
            posi = t1[0:64, :].bitcast(I32)
            for t_i in range(NT):
                tok = slice(t_i * 512, (t_i + 1) * 512)
                a_, k_, m_ = t2[0:64, :], rs[0:64, :], t1[0:64, :]
                dma("sp", posi, pos_d[s:s + 1, tok].partition_broadcast(64), t1_t, writes=[t1_t])
                cp(a_, posi, [t1_t], [t2_t])
                ts(a_, a_, invf, None, ALU.mult, None, [t2_t, cst_t], [t2_t])
                for which, dst, dst_t in ((0, sin2, sin_t), (1, cos2, cos_t)):
                    if which == 1:
                        ts(a_, a_, math.pi / 2, None, ALU.add, None, [t2_t], [t2_t])
                    ts(k_, a_, 1.0 / TWO_PI, None, ALU.mult, None, [t2_t], [rs_t])
                    cp(posi, k_, [rs_t], [t1_t])
                    cp(k_, posi, [t1_t], [rs_t])
                    stt(k_, k_, -TWO_PI, a_, ALU.mult, ALU.add, [rs_t, t2_t], [rs_t])
                    ts(m_, k_, math.pi, None, ALU.is_gt, None, [rs_t], [t1_t])
                    stt(k_, m_, -TWO_PI, k_, ALU.mult, ALU.add, [t1_t, rs_t], [rs_t])
                    ts(m_, k_, -math.pi, None, ALU.is_lt, None, [rs_t], [t1_t])
                    stt(k_, m_, TWO_PI, k_, ALU.mult, ALU.add, [t1_t, rs_t], [rs_t])
                    ts(k_, k_, math.pi, -math.pi, ALU.min, ALU.max, [rs_t], [rs_t])
                    act(dst[:, tok], k_, AF.Sin, [rs_t], [dst_t])
                ts(sin2[:, tok], sin2[:, tok], sgn, None, ALU.mult, None, [sin_t, cst_t], [sin_t])
            for t_i in range(NT):
                tok = slice(t_i * 512, (t_i + 1) * 512)
                qps = []
                for kc in range(3):
                    p_, pt_ = ps()
                    for k in range(8):
                        mm(p_[:], wql[:, k, kc * 128:(kc + 1) * 128], hT[:, k, tok], k == 0, k == 7, [pq_t, hT_tt[t_i]], [pt_])
                    qps.append((p_, pt_))
                ss_, sst_ = ps()
                for kc in range(3):
                    act(sqb, qps[kc][0][:], AF.Square, [qps[kc][1]], [sqb_t])
                    mm(ss_[:], ones16, sqb, kc == 0, kc == 2, [sqb_t, cbf_t], [sst_])
                rsqrt_from_ss(rs, ss_[:], 1.0 / 384, [sst_], rs_t)
                for kc in range(3):
                    stt(qnT[:, kc, tok], qps[kc][0][:], pc[:, 68 + kc:69 + kc], rs, ALU.mult, ALU.mult,
                        [qps[kc][1], pc_t, rs_t], [qnT_tt[t_i]])
                p_, pt_ = ps()
                for k in range(8):
                    mm(p_[:], wkl[:, k, 0:128], hT[:, k, tok], k == 0, k == 7, [pk_t, hT_tt[t_i]], [pt_])
                act(sqb, p_[:], AF.Square, [pt_], [sqb_t])
                ss_, sst_ = ps()
                mm(ss_[:], ones16, sqb, True, True, [sqb_t, cbf_t], [sst_])
                rsqrt_from_ss(rs, ss_[:], 1.0 / 128, [sst_], rs_t)
                stt(kvnT[:, tok], p_[:], pc[:, 71:72], rs, ALU.mult, ALU.mult, [pt_, pc_t, rs_t], [kvnT_tt[t_i]])
                kp_, kpt_ = ps()
                for k in range(8):
                    mm(kp_[0:64, :], wkl[:, k, 128:192], hT[:, k, tok], k == 0, k == 7, [pk_t, hT_tt[t_i]], [kpt_])
                ks_, kst_ = ps()
                for k in range(8):
                    mm(ks_[0:64, :], wks[:, k, :], hT[:, k, tok], k == 0, k == 7, [pk2_t, hT_tt[t_i]], [kst_])
                act(sqkpe[:, tok], kp_[0:64, :], AF.Square, [kpt_], [sqkpe_tt[t_i]])
                stt(t1[0:64, :], kp_[0:64, :], pc[0:64, 76:77], cos2[:, tok], ALU.mult, ALU.mult, [kpt_, pc_t, cos_t], [t1_t])
                stt(t2[0:64, :], ks_[0:64, :], pc[0:64, 77:78], sin2[:, tok], ALU.mult, ALU.mult, [kst_, pc_t, sin_t], [t2_t])
                tt(kper[:, tok], t1[0:64, :], t2[0:64, :], ALU.add, [t1_t, t2_t], [kper_tt[t_i]])
            pw, pw_t = ring_next()
            wqb = pw[:, 0:3 * 768].rearrange("p (a b) -> p a b", b=768)
            wqs = pw[:, 3072:3072 + 3 * 256].rearrange("p (a b) -> p a b", b=256)
            wqv = wqb_d[l].rearrange("(kc p) n -> p kc n", p=128)
            wq4 = wqb_d[l].rearrange("(kc p) (h d) -> p kc h d", p=128, d=192)
            dma("pool", wqb, wqv, pw_t, writes=[pw_t])
            wqs4 = wqs.rearrange("p a (h d) -> p a h d", d=64)
            for kc in range(3):
                dma("pool", wqs4[:, kc, :, 0:32], wq4[:, kc, :, 160:192], pw_t, writes=[pw_t])
                dma("pool", wqs4[:, kc, :, 32:64], wq4[:, kc, :, 128:160], pw_t, writes=[pw_t])
            pv, pv_t = ring_next()
            wkvb = pv[:, 0:1024]
            wload(wkvb, wkvb_d[l], pv_t)
            scale = 192.0 ** -0.5
            pKn, pKn_t = ring_next()
            pKV, pKV_t = ring_next()
            Kn_tt = [S.T("Kn%d" % i) for i in range(NT)]
            KV_tt = [S.T("KV%d" % i) for i in range(NT)]
            Kn = pKn[:, 0:L]
            Kr = pKn[0:64, 2048:2048 + L]
            Vt = pKV[:, 0:NB * 128].rearrange("p (a b) -> p a b", b=128)
            first = [True]

            def prep(h, t_i):
                tok = slice(t_i * 512, (t_i + 1) * 512)
                Qn, Qn_t, Qr, Qr_t = Qns[t_i % 2], Qn_ts[t_i % 2], Qrs[t_i % 2], Qr_ts[t_i % 2]
                wK = [Kn_tt[t_i]] + ([pKn_t] if first[0] else [])
                wV = [KV_tt[t_i]] + ([pKV_t] if first[0] else [])
                first[0] = False
                kn_, knt_ = ps()
                mm(kn_[:], wkvb[:, h * 256:h * 256 + 128], kvnT[:, tok], True, True, [pv_t, kvnT_tt[t_i]], [knt_])
                act(sqb, kn_[:], AF.Square, [knt_], [sqb_t])
                ss_, sst_ = ps()
                mm(ss_[:], ones16, sqb, True, False, [sqb_t, cbf_t], [sst_])
                mm(ss_[:], ones16[0:64, :], sqkpe[:, tok], False, True, [sqkpe_tt[t_i], cbf_t], [sst_])
                rsqrt_from_ss(rs, ss_[:], 1.0 / 192, [sst_], rs_t)
                stt(Kn[:, tok], kn_[:], pc[:, 75:76], rs, ALU.mult, ALU.mult, [knt_, pc_t, rs_t], wK)
                tt(Kr[:, tok], kper[:, tok], rs[0:64, :], ALU.mult, [kper_tt[t_i], rs_t], [Kn_tt[t_i]])
                v_, vt_ = ps()
                for b_ in range(4):
                    tb = slice(t_i * 512 + b_ * 128, t_i * 512 + (b_ + 1) * 128)
                    mm(v_[:, b_ * 128:(b_ + 1) * 128], kvnT[:, tb], wkvb[:, h * 256 + 128:h * 256 + 256], True, True,
                       [pv_t, kvnT_tt[t_i]], [vt_])
                act(Vt[:, t_i * 4:(t_i + 1) * 4, :], v3(v_[:], 128), AF.Copy, [vt_], wV)
                qn_, qnt_ = ps()
                for kc in range(3):
                    mm(qn_[:], wqb[:, kc, h * 192:h * 192 + 128], qnT[:, kc, tok], kc == 0, kc == 2, [pw_t, qnT_tt[t_i]], [qnt_])
                qr_, qrt_ = ps()
                for kc in range(3):
                    mm(qr_[0:64, :], wqb[:, kc, h * 192 + 128:h * 192 + 192], qnT[:, kc, tok], kc == 0, kc == 2,
                       [pw_t, qnT_tt[t_i]], [qrt_])
                qs_, qst_ = ps()
                for kc in range(3):
                    mm(qs_[0:64, :], wqs[:, kc, h * 64:(h + 1) * 64], qnT[:, kc, tok], kc == 0, kc == 2,
                       [pw_t, qnT_tt[t_i]], [qst_])
                ss_, sst_ = ps()
                act(sqb, qn_[:], AF.Square, [qnt_], [sqb_t])
                mm(ss_[:], ones16, sqb, True, False, [sqb_t, cbf_t], [sst_])
                act(Qr, qr_[0:64, :], AF.Square, [qrt_], [Qr_t])
                mm(ss_[:], ones16[0:64, :], Qr, False, True, [Qr_t, cbf_t], [sst_])
                rsqrt_from_ss(rs, ss_[:], 1.0 / 192, [sst_], rs_t)
                stt(Qn, qn_[:], pc[:, 72:73], rs, ALU.mult, ALU.mult, [qnt_, pc_t, rs_t], [Qn_t])
                stt(t1[0:64, :], qr_[0:64, :], pc[0:64, 73:74], cos2[:, tok], ALU.mult, ALU.mult, [qrt_, pc_t, cos_t], [t1_t])
                stt(t2[0:64, :], qs_[0:64, :], pc[0:64, 74:75], sin2[:, tok], ALU.mult, ALU.mult, [qst_, pc_t, sin_t], [t2_t])
                tt(t1[0:64, :], t1[0:64, :], t2[0:64, :], ALU.add, [t1_t, t2_t], [t1_t])
                tt(Qr, t1[0:64, :], rs[0:64, :], ALU.mult, [t1_t, rs_t], [Qr_t])

            def attention(h, t_i):
                tok = slice(t_i * 512, (t_i + 1) * 512)
                Qn, Qn_t, Qr, Qr_t = Qns[t_i % 2], Qn_ts[t_i % 2], Qrs[t_i % 2], Qr_ts[t_i % 2]
                ps_n[0] = 6
                o_, ot_ = psb[6], psb_t[6]
                dn_, dnt_ = psb[7], psb_t[7]
                nk = 4 * t_i + 4

                def s_stage(kc):
                    j = kc - 4 * t_i
                    q0 = j * 128 if j >= 0 else 0
                    kk = slice(kc * 128, (kc + 1) * 128)
                    ktt = Kn_tt[kc // 4]
                    s_, st_ = ps()
                    mm(s_[:, q0:512], Kn[:, kk], Qn[:, q0:512], True, False, [pKn_t, ktt, Qn_t], [st_])
                    mm(s_[:, q0:512], Kr[:, kk], Qr[:, q0:512], False, True, [pKn_t, ktt, Qr_t], [st_])
                    p_i, p_it = pT[kc % 3], pT_t[kc % 3]
                    act(p_i[:, q0:512], s_[:, q0:512], AF.Exp, [st_], [p_it], scale=scale)
                    if j >= 0:
                        tt(p_i[:, q0:q0 + 128], p_i[:, q0:q0 + 128], tri16, ALU.mult, [p_it, cbf_t], [p_it])
                    return (kc, q0, p_i, p_it)

                def pv_stage(item):
                    kc, q0, p_i, p_it = item
                    mm(o_[:, q0:512], Vt[:, kc, :], p_i[:, q0:512], kc == 0, kc == nk - 1, [pKV_t, KV_tt[kc // 4], p_it], [ot_])
                    mm(dn_[:, q0:512], ones16, p_i[:, q0:512], kc == 0, kc == nk - 1, [cbf_t, p_it], [dnt_])

                pend_ = []
                for kc in range(nk):
                    pend_.append(s_stage(kc))
                    if len(pend_) > 2:
                        pv_stage(pend_.pop(0))
                while pend_:
                    pv_stage(pend_.pop(0))
                S.op("dve", lambda e: e.reciprocal(out=rs, in_=dn_[:]), [dnt_], [rs_t], fs=512)
                tt(yT[:, h, tok], o_[:], rs, ALU.mult, [ot_, rs_t], [yT_tt[t_i]])
                ps_n[0] = 8

            for h in range(4):
                prep(h, 0)
                for t_i in range(NT):
                    if t_i + 1 < NT:
                        prep(h, t_i + 1)
                    attention(h, t_i)

        env = dict(locals())
        for i, (ch, fn) in enumerate((("A", branch_a), ("B", branch_b), ("C", None), ("D", branch_d))):
            if ch not in cfg.get("branches", "ABCD"):
                continue
            if ch == "C":
                branch_c()
            else:
                fn()
            gating(i)

    mixer = cfg.get("mixer", None)
    for s in range(NSEQ):
        S.epoch = s
        S.barrier()
        load_x(s)
        for l in range(DEPTH):
            S.barrier()
            load_layer_params(l)
            if cfg.get("ffn1", True):
                ffn(l, f1gu, f1dn, 0)
            if mixer is not None:
                mixer_layer(l, s)
            if cfg.get("ffn2", True):
                ffn(l, f2gu, f2dn, 16)
        S.barrier()
        store_x(s)
    S.barrier()
    S.emit()
    es.close()
    return nc


def host_consts():
    c = np.zeros((128, 640), np.float32)
    c[:, 0:128] = np.eye(128, dtype=np.float32)
    s = np.arange(128)[:, None]
    t = np.arange(128)[None, :]
    c[:, 128:256] = (s <= t).astype(np.float32)
    c[:, 256:384] = np.where(t >= s, 0.0, -30000.0).astype(np.float32)
    c[:, 384:512] = 1.0
    invf = (10000.0 ** (-(np.arange(0, 64, 2, dtype=np.float32) / np.float32(64.0)))).astype(np.float32)
    c[0:32, 512] = invf
    c[32:64, 512] = invf
    c[0:32, 513] = -1.0
    c[32:64, 513] = 1.0
    return c


def host_layout(inp, DEPTH):
    pcols = np.zeros((DEPTH, NPC, 128), np.float32)
    prow = np.zeros((DEPTH, NPR), np.float32)
    for l in range(DEPTH):
        pcols[l, 0:8] = inp["ffn1_norm"][l].reshape(8, 128)
        pcols[l, 8:16] = inp["mix_norm"][l].reshape(8, 128)
        pcols[l, 16:24] = inp["ffn2_norm"][l].reshape(8, 128)
        pcols[l, 24:56] = inp["ssd_conv_w"][l].reshape(4, 8, 128).reshape(32, 128)
        pcols[l, 56:64] = inp["ssd_conv_b"][l].reshape(8, 128)
        pcols[l, 64:68] = inp["ssd_norm"][l].reshape(4, 128)
        pcols[l, 68:71] = inp["mla_q_norm"][l].reshape(3, 128)
        pcols[l, 71] = inp["mla_kv_norm"][l]
        for r0, w in ((72, inp["mla_qk_q"][l]), (75, inp["mla_qk_k"][l])):
            pcols[l, r0] = w[0:128]
            pcols[l, r0 + 1, 0:64] = w[128:192]
            pcols[l, r0 + 2, 0:32] = w[160:192]
            pcols[l, r0 + 2, 32:64] = w[128:160]
        pcols[l, 78:90] = inp["sc_conv_w"][l].reshape(3, 4, 128).reshape(12, 128)
        pcols[l, 96:100] = np.repeat(inp["ssd_d"][l], 64).reshape(4, 128)
        prow[l, 0:8] = inp["ssd_dt_bias"][l]
        prow[l, 8:16] = inp["ssd_a_log"][l]
        prow[l, 16:24] = inp["ssd_d"][l]
    return pcols, prow


_CACHE = {}


def make_in_maps(inp, NCORE, NSEQ, DEPTH):
    pcols, prow = host_layout(inp, DEPTH)
    cstv = host_consts()
    shared = {
        "cst": cstv, "pcols": pcols, "prow": prow,
        "ffn1_w_gu": inp["ffn1_w_gu"], "ffn1_w_down": inp["ffn1_w_down"],
        "ffn2_w_gu": inp["ffn2_w_gu"], "ffn2_w_down": inp["ffn2_w_down"],
        "w_in": inp["w_in"], "gmlp_w_s": inp["gmlp_w_s"],
        "gmlp_b_s": inp["gmlp_b_s"].reshape(DEPTH, 512),
        "gmlp_v_norm": inp["gmlp_v_norm"],
        "mla_w_qb": inp["mla_w_qb"], "mla_w_kvb": inp["mla_w_kvb"],
        "w_branch": inp["w_branch"], "w_out": inp["w_out"],
    }
    in_maps = []
    for c in range(NCORE):
        m = dict(shared)
        m["x"] = np.ascontiguousarray(inp["x"][c * NSEQ:(c + 1) * NSEQ])
        m["positions"] = np.ascontiguousarray(inp["positions"][c * NSEQ:(c + 1) * NSEQ]).astype(np.int32)
        in_maps.append(m)
    return in_maps


def mixer_block(env, l, s):
    pass


def kernel(**inputs):
    inp = {k: np.asarray(v) for k, v in inputs.items()}
    B, L, _ = inp["x"].shape
    DEPTH = inp["w_in"].shape[0]
    NCORE = 8
    NSEQ = B // NCORE
    key = (L, NSEQ, DEPTH)
    if key not in _CACHE:
        _CACHE[key] = build_program(L, NSEQ, DEPTH, {"mixer": mixer_block})
    nc = _CACHE[key]
    in_maps = make_in_maps(inp, NCORE, NSEQ, DEPTH)
    res = run_bass_kernel_spmd(nc, in_maps, core_ids=list(range(NCORE)))
    return np.concatenate([r["out"] for r in res.results], axis=0)
```

```python
import numpy as np
import concourse.bass as bass
import concourse.mybir as mybir
from concourse.bass_utils import run_bass_kernel_spmd
from contextlib import ExitStack

F32 = mybir.dt.float32
BF16 = mybir.dt.bfloat16
I32 = mybir.dt.int32
AF = mybir.ActivationFunctionType
ALU = mybir.AluOpType

D = 1024
NCH = 8
DFF = 2816
NJ = 22
INTOT = 8776
EPS = 1e-6
O_Z, O_XBC, O_DT, O_UV, O_QL, O_KVL, O_KPE, O_SC, O_G = 0, 512, 1536, 1544, 2568, 2952, 3080, 3144, 4680
NPC = 112
NPR = 24


class T:
    __slots__ = ("name", "w", "r", "sem", "ndma", "persist")

    def __init__(self, name, persist=False):
        self.name = name
        self.persist = persist
        self.w = None
        self.r = {}
        self.sem = None
        self.ndma = 0


class Op:
    __slots__ = ("eng", "fn", "deps", "needs", "key", "val", "dma", "epoch", "fs")


ENGS = ("pe", "act", "dve", "pool", "sp")


class Sched:
    def __init__(self, nc, es):
        self.nc = nc
        self.es = es
        self.ops = {e: [] for e in ENGS}
        self.epoch = 0
        self.dma_ops = []
        self.tiles = []

    def T(self, name, persist=False):
        t = T(name, persist)
        self.tiles.append(t)
        return t

    def PT(self, name):
        if not hasattr(self, "_pt"):
            self._pt = {}
        if name not in self._pt:
            self._pt[name] = self.T(name)
        return self._pt[name]

    def sb(self, name, shape, dtype):
        return self.es.enter_context(self.nc.sbuf_tensor("sb_" + name, list(shape), dtype))

    def op(self, eng, fn, reads=(), writes=(), dma_tile=None, fs=0):
        o = Op()
        o.fs = fs
        o.eng = eng
        o.fn = fn
        o.deps = []
        o.needs = False
        o.dma = dma_tile
        o.epoch = self.epoch
        o.key = None
        o.val = 0
        is_dma = dma_tile is not None

        def dep(p, raw=False):
            if p is None or p is o:
                return
            if p.dma is None and not is_dma and p.eng == eng:
                if not raw or eng == "pe":
                    return
                if p.fs >= 512 and o.fs >= 512:
                    return
            p.needs = True
            o.deps.append(p)

        for t in reads:
            dep(t.w, True)
        for t in writes:
            dep(t.w)
            for r in t.r.values():
                dep(r)
        for t in reads:
            t.r[("dma", id(o)) if is_dma else eng] = o
        for t in writes:
            t.w = o
            t.r = {}
        if is_dma:
            o.needs = True
            if not dma_tile.persist:
                self.dma_ops.append(o)
        self.ops[eng].append(o)
        return o

    def barrier(self):
        lasts = []
        BENGS = ("pe", "act", "dve", "sp")
        for e in BENGS:
            for o in reversed(self.ops[e]):
                if o.dma is None and o.fn is not None:
                    lasts.append(o)
                    break
        pend = list(self.dma_ops)
        self.dma_ops = []
        for e in BENGS:
            o = Op()
            o.fs = 0
            o.eng = e
            o.fn = None
            o.deps = []
            o.needs = False
            o.dma = None
            o.epoch = self.epoch
            o.key = None
            o.val = 0
            for p in lasts:
                if p.eng != e:
                    p.needs = True
                    o.deps.append(p)
            for p in pend:
                o.deps.append(p)
            self.ops[e].append(o)
        for t in self.tiles:
            if not t.persist:
                t.w = None
                t.r = {}

    def emit(self):
        nc = self.nc
        sems = {}

        def getsem(key):
            if key not in sems:
                sems[key] = self.es.enter_context(nc.semaphore("s%d" % len(sems)))
            return sems[key]

        for e in ENGS:
            cnt = {}
            for o in self.ops[e]:
                if o.dma is not None:
                    t = o.dma
                    t.ndma += 1
                    o.key = ("dma", id(t))
                    o.val = 16 * t.ndma
                elif o.needs:
                    k = (e, o.epoch)
                    cnt[k] = cnt.get(k, 0) + 1
                    o.key = k
                    o.val = cnt[k]
        for e in ENGS:
            for o in self.ops[e]:
                if o.needs:
                    getsem(o.key)
        import os
        if os.environ.get("MK_DEBUG"):
            mx = {}
            for e in ENGS:
                for o in self.ops[e]:
                    if o.key is not None:
                        mx[o.key] = max(mx.get(o.key, 0), o.val)
            print("NSEMS", len(sems), "MAXVALS", sorted([(str(k)[:30], v) for k, v in mx.items()], key=lambda kv: -kv[1])[:12])
            print("NOPS", {e: len(self.ops[e]) for e in ENGS})
        block = self.es.enter_context(nc.Block())

        def run(eng_name, eng):
            waited = {}
            for o in self.ops[eng_name]:
                need = {}
                for p in o.deps:
                    if waited.get(p.key, 0) < p.val:
                        if need.get(p.key, 0) < p.val:
                            need[p.key] = p.val
                for k, v in need.items():
                    eng.wait_ge(sems[k], v)
                    waited[k] = v
                if o.fn is None:
                    continue
                ins = o.fn(eng)
                if o.dma is not None:
                    ins.then_inc(sems[o.key], 16)
                elif o.needs:
                    ins.then_inc(sems[o.key], 1)

        @block.tensor
        def _(e):
            run("pe", e)

        @block.scalar
        def _(e):
            run("act", e)

        @block.vector
        def _(e):
            run("dve", e)

        @block.gpsimd
        def _(e):
            run("pool", e)

        @block.sync
        def _(e):
            run("sp", e)


def build_program(L, NSEQ, DEPTH, cfg=None):
    cfg = cfg or {}
    NT = L // 512
    NB = L // 128
    nc = bass.Bass("TRN2", target_bir_lowering=False)
    dr = {}

    def din(name, shape, dt=F32):
        dr[name] = nc.dram_tensor(name, list(shape), dt, kind="ExternalInput").ap()
        return dr[name]

    x_d = din("x", [NSEQ, L, D])
    pos_d = din("positions", [NSEQ, L], I32)
    cst_d = din("cst", [128, 648])
    pcols_d = din("pcols", [DEPTH, NPC, 128])
    prow_d = din("prow", [DEPTH, NPR])
    f1gu = din("ffn1_w_gu", [DEPTH, D, 2 * DFF])
    f1dn = din("ffn1_w_down", [DEPTH, DFF, D])
    f2gu = din("ffn2_w_gu", [DEPTH, D, 2 * DFF])
    f2dn = din("ffn2_w_down", [DEPTH, DFF, D])
    win_d = din("w_in", [DEPTH, D, INTOT])
    ws_d = din("gmlp_w_s", [DEPTH, 4, 128, 128])
    bs_d = din("gmlp_b_s", [DEPTH, 512])
    vnw_d = din("gmlp_v_norm", [DEPTH, 512])
    wqb_d = din("mla_w_qb", [DEPTH, 384, 768])
    wkvb_d = din("mla_w_kvb", [DEPTH, 128, 1024])
    wbr_d = din("w_branch", [DEPTH, 4, 512, D])
    wout_d = din("w_out", [DEPTH, D, D])
    out_d = nc.dram_tensor("out", [NSEQ, L, D], F32, kind="ExternalOutput").ap()

    es = ExitStack()
    S = Sched(nc, es)

    xT = S.sb("xT", [128, NCH, L], F32)
    xT_t = [S.T("xT%d" % i) for i in range(NT)]
    NRING = 6
    ring = [S.sb("ring%d" % i, [128, 4096], BF16) for i in range(NRING)]
    ring_t = [S.T("ring%d" % i, True) for i in range(NRING)]
    ring_pos = [0]
    cst = S.sb("cst", [128, 648], F32)
    cst_t = S.T("cst", True)
    ident = cst[:, 0:128]
    tri = cst[:, 128:256]
    maskneg = cst[:, 256:384]
    ones32 = cst[:, 384:512]
    triS = cst[:, 520:648]
    cbf = S.sb("cbf", [128, 384], BF16)
    cbf_t = S.T("cbf", True)
    ones16 = cbf[:, 0:128]
    tri16 = cbf[:, 128:256]
    ident16 = cbf[:, 256:384]
    prow = S.sb("prow", [128, NPR], F32)
    prow_t = S.T("prow", True)
    expA = S.sb("expA", [128, 8], F32)
    expA_t = S.T("expA", True)
    pc = S.sb("pc", [128, NPC], F32)
    pc_t = S.T("pc", True)
    pcst = S.sb("pcst", [NPC, 128], F32)
    pcst_t = S.T("pcst", True)
    ARENA = 90 * 1024
    arena = S.sb("arena", [128, ARENA // 4], F32)

    def carve(off, shape, dt):
        n = 1
        for s in shape[1:]:
            n *= s
        bpe = 4 if dt in (F32, I32) else 2
        assert off % 4 == 0 and off + n * bpe <= ARENA, (off, shape)
        v = arena[0:shape[0], off // 4: off // 4 + (n * bpe) // 4]
        if dt != F32:
            v = v.bitcast(dt)
        if len(shape) == 3:
            v = v.rearrange("p (a b) -> p a b", b=shape[2])
        elif len(shape) == 4:
            v = v.rearrange("p (a b c) -> p a b c", b=shape[2], c=shape[3])
        return v

    psb = [es.enter_context(nc.psum_tensor("ps%d" % i, [128, 512], F32)) for i in range(8)]
    psb_t = [S.T("ps%d" % i) for i in range(8)]
    ps_pos = [0]

    ps_n = [8]

    def ps():
        i = ps_pos[0] % ps_n[0]
        ps_pos[0] += 1
        return psb[i], psb_t[i]

    def ring_next():
        i = ring_pos[0] % NRING
        ring_pos[0] += 1
        return ring[i], ring_t[i]

    def fsz(ap):
        n = 1
        for d_ in ap.shape[1:]:
            n *= d_
        return n

    def mm(out, lhsT, rhs, start, stop, reads, writes):
        S.op("pe", lambda e: e.matmul(out, lhsT, rhs, start=start, stop=stop), reads, writes)

    def tr(out, in_, idn, reads, writes):
        S.op("pe", lambda e: e.transpose(out, in_, idn), reads, writes)

    def act(out, in_, func, reads, writes, bias=None, scale=None):
        kw = {}
        if bias is not None:
            kw["bias"] = bias
        if scale is not None:
            kw["scale"] = scale
        S.op("act", lambda e: e.activation(out=out, in_=in_, func=func, **kw), reads, writes, fs=fsz(out))

    def tt(out, in0, in1, op, reads, writes, eng="dve"):
        S.op(eng, lambda e: e.tensor_tensor(out=out, in0=in0, in1=in1, op=op), reads, writes, fs=fsz(out))

    def ts(out, in0, s1, s2, op0, op1, reads, writes, eng="dve"):
        if op1 is None:
            S.op(eng, lambda e: e.tensor_scalar(out=out, in0=in0, scalar1=s1, scalar2=None, op0=op0), reads, writes, fs=fsz(out))
        else:
            S.op(eng, lambda e: e.tensor_scalar(out=out, in0=in0, scalar1=s1, scalar2=s2, op0=op0, op1=op1), reads, writes, fs=fsz(out))

    def stt(out, in0, scalar, in1, op0, op1, reads, writes, eng="dve"):
        S.op(eng, lambda e: e.scalar_tensor_tensor(out=out, in0=in0, scalar=scalar, in1=in1, op0=op0, op1=op1), reads, writes, fs=fsz(out))

    def cp(out, in_, reads, writes, eng="dve"):
        S.op(eng, lambda e: e.tensor_copy(out=out, in_=in_), reads, writes, fs=fsz(out))

    def dma(eng, out, in_, tile, reads=(), writes=()):
        S.op(eng, lambda e: e.dma_start(out=out, in_=in_), reads, writes, dma_tile=tile)

    def wload(view_out, src, page_t):
        dma("pool", view_out, src, page_t, writes=[page_t])

    dma("sp", cst[:], cst_d, cst_t, writes=[cst_t])
    cp(ones16, ones32, [cst_t], [cbf_t])
    cp(tri16, tri, [cst_t], [cbf_t])
    cp(ident16, ident, [cst_t], [cbf_t])

    def load_layer_params(l):
        dma("sp", pcst[:], pcols_d[l], pcst_t, writes=[pcst_t])
        p_, pt_ = ps()
        tr(p_[:, 0:NPC], pcst[:], ident[0:NPC, 0:NPC], [pcst_t, cst_t], [pt_])
        cp(pc[:], p_[:, 0:NPC], [pt_], [pc_t])
        dma("sp", prow[:], prow_d[l:l + 1, :].partition_broadcast(128), prow_t, writes=[prow_t])
        act(expA[:], prow[:, 8:16], AF.Exp, [prow_t], [expA_t])

    def load_x(s):
        stg = [carve(i * 4096, [128, 1024], F32) for i in range(2)]
        stg_t = [S.PT("stg%d" % i) for i in range(2)]
        for b in range(NB):
            st, st_t = stg[b % 2], stg_t[b % 2]
            dma("sp", st, x_d[s, b * 128:(b + 1) * 128, :], st_t, writes=[st_t])
            for half in range(2):
                p_, pt_ = ps()
                for c4 in range(4):
                    c = half * 4 + c4
                    tr(p_[:, c4 * 128:(c4 + 1) * 128], st[:, c * 128:(c + 1) * 128], ident, [st_t, cst_t], [pt_])
                S.op("act", (lambda e, p_=p_, half=half, b=b: e.activation(
                    out=xT[:, half * 4:half * 4 + 4, b * 128:(b + 1) * 128],
                    in_=p_[:].rearrange("p (a b) -> p a b", b=128), func=AF.Copy)),
                    [pt_], [xT_t[b // 4]])

    def store_x(s):
        stg = [carve(i * 4096, [128, 1024], F32) for i in range(2)]
        stg_t = [S.PT("ostg%d" % i) for i in range(2)]
        for b in range(NB):
            st, st_t = stg[b % 2], stg_t[b % 2]
            for half in range(2):
                p_, pt_ = ps()
                for c4 in range(4):
                    c = half * 4 + c4
                    tr(p_[:, c4 * 128:(c4 + 1) * 128], xT[:, c, b * 128:(b + 1) * 128], ident, [xT_t[b // 4], cst_t], [pt_])
                act(st[:, half * 512:(half + 1) * 512], p_[:], AF.Copy, [pt_], [st_t])
            dma("sp", out_d[s, b * 128:(b + 1) * 128, :], st, st_t, reads=[st_t])

    def rsqrt_from_ss(out, ss, inv_n, reads, out_t):
        ts(out, ss, inv_n, EPS, ALU.mult, ALU.add, reads, [out_t])
        act(out, out, AF.Sqrt, [out_t], [out_t])
        S.op("dve", lambda e: e.reciprocal(out=out, in_=out), [out_t], [out_t], fs=fsz(out))

    def rmsnorm_tile(tt_i, wcol0, hT, hT_tt, sq, sq_t, rstd, rstd_t):
        tok = slice(tt_i * 512, (tt_i + 1) * 512)
        p_, pt_ = ps()
        for c in range(NCH):
            q, q_t = sq[c % len(sq)], sq_t[c % len(sq)]
            act(q, xT[:, c, tok], AF.Square, [xT_t[tt_i]], [q_t])
            mm(p_[:], ones16, q, c == 0, c == NCH - 1, [q_t, cbf_t], [pt_])
        if isinstance(rstd, list):
            rstd, rstd_t = rstd[tt_i % len(rstd)], rstd_t[tt_i % len(rstd_t)]
        rsqrt_from_ss(rstd, p_[:], 1.0 / D, [pt_], rstd_t)
        for c in range(NCH):
            stt(hT[:, c, tok], xT[:, c, tok], pc[:, wcol0 + c:wcol0 + c + 1], rstd, ALU.mult, ALU.mult,
                [xT_t[tt_i], pc_t, rstd_t], [hT_tt[tt_i]])

    FF_PARTS = [(0, 8), (8, 15), (15, 22)]

    def ffn(l, wgu_d, wdn_d, normcol):
        S.barrier()
        off = 0
        hT = carve(off, [128, NCH, L], BF16); off += NCH * L * 2
        aT = carve(off, [128, 8, L], BF16); off += 8 * L * 2
        sq = [carve(off + i * 1024, [128, 512], BF16) for i in range(3)]; off += 3 * 1024
        sg = [carve(off + i * 2048, [128, 512], F32) for i in range(3)]; off += 3 * 2048
        rstd = [carve(off + i * 2048, [128, 512], F32) for i in range(2)]; off += 4096
        hT_tt = [S.T("hT%d" % i) for i in range(NT)]
        aT_tt = [S.T("aT%d" % i) for i in range(NT)]
        sq_t = [S.T("sq%d" % i) for i in range(3)]
        sg_t = [S.T("sg%d" % i) for i in range(3)]
        rstd_t = [S.T("rstd%d" % i) for i in range(2)]
        for t_i in range(NT):
            rmsnorm_tile(t_i, normcol, hT, hT_tt, sq, sq_t, rstd, rstd_t)
        wgu = wgu_d[l].rearrange("(kc p) n -> p kc n", p=128)
        wdn = wdn_d[l].rearrange("(j p) n -> p j n", p=128)
        sgi = 0
        for (j0, j1) in FF_PARTS:
            j = j0
            while j < j1:
                nb = min(4, j1 - j)
                pg, pg_t = ring_next()
                pu, pu_t = ring_next()
                wg_v = pg[:, 0:8 * nb * 128].rearrange("p (a b) -> p a b", b=nb * 128)
                wu_v = pu[:, 0:8 * nb * 128].rearrange("p (a b) -> p a b", b=nb * 128)
                wload(wg_v, wgu[:, :, j * 128:(j + nb) * 128], pg_t)
                wload(wu_v, wgu[:, :, DFF + j * 128:DFF + (j + nb) * 128], pu_t)
                for jj in range(nb):
                    for t_i in range(NT):
                        tok = slice(t_i * 512, (t_i + 1) * 512)
                        g_, gt_ = ps()
                        u_, ut_ = ps()
                        for k in range(NCH):
                            mm(g_[:], wg_v[:, k, jj * 128:(jj + 1) * 128], hT[:, k, tok], k == 0, k == NCH - 1,
                               [pg_t, hT_tt[t_i]], [gt_])
                        for k in range(NCH):
                            mm(u_[:], wu_v[:, k, jj * 128:(jj + 1) * 128], hT[:, k, tok], k == 0, k == NCH - 1,
                               [pu_t, hT_tt[t_i]], [ut_])
                        s_, st_ = sg[sgi % 3], sg_t[sgi % 3]
                        sgi += 1
                        act(s_, g_[:], AF.Silu, [gt_], [st_])
                        tt(aT[:, j + jj - j0, tok], u_[:], s_, ALU.mult, [ut_, st_], [aT_tt[t_i]])
                j += nb
            nj = j1 - j0
            pages = []
            j = 0
            while j < nj:
                nb = min(4, nj - j)
                pd, pd_t = ring_next()
                wd_v = pd[:, 0:nb * 1024].rearrange("p (a b) -> p a b", b=1024)
                wload(wd_v, wdn[:, j0 + j:j0 + j + nb, :], pd_t)
                for jj in range(nb):
                    pages.append((wd_v, jj, pd_t))
                j += nb
            for oc in range(NCH):
                for t_i in range(NT):
                    tok = slice(t_i * 512, (t_i + 1) * 512)
                    d_, dt_ = ps()
                    for jx in range(nj):
                        wd_v, jj, pd_t = pages[jx]
                        mm(d_[:], wd_v[:, jj, oc * 128:(oc + 1) * 128], aT[:, jx, tok], jx == 0, jx == nj - 1,
                           [pd_t, aT_tt[t_i]], [dt_])
                    stt(xT[:, oc, tok], d_[:], 0.5, xT[:, oc, tok], ALU.mult, ALU.add, [dt_, xT_t[t_i]], [xT_t[t_i]])


    WS0 = 48 * 1024

    class WSAlloc:
        def __init__(self):
            self.off = WS0

        def get(self, shape, dt):
            n = 1
            for d_ in shape[1:]:
                n *= d_
            nb = n * (4 if dt in (F32, I32) else 2)
            nb = (nb + 3) // 4 * 4
            v = carve(self.off, shape, dt)
            self.off += nb
            return v

    def bc_mid(ap2d, n):
        return ap2d.unsqueeze(1).to_broadcast([ap2d.shape[0], n, ap2d.shape[1]])

    def bc_last(ap2d, n):
        return ap2d.unsqueeze(2).to_broadcast([ap2d.shape[0], ap2d.shape[1], n])

    def v3(ap2d, b):
        return ap2d.rearrange("p (a b) -> p a b", b=b)

    def mixer_layer(l, s):
        S.barrier()
        hT = carve(0, [128, NCH, L], BF16)
        yT = carve(32 * 1024, [128, 4, L], BF16)
        hT_tt = [S.T("mhT%d" % i) for i in range(NT)]
        yT_tt = [S.T("yT%d" % i) for i in range(NT)]
        win = win_d[l].rearrange("(kc p) n -> p kc n", p=128)

        def wpage(c0, ncols):
            pg, pg_t = ring_next()
            v = pg[:, 0:8 * ncols].rearrange("p (a b) -> p a b", b=ncols)
            wload(v, win[:, :, c0:c0 + ncols], pg_t)
            return v, pg_t

        wsn = WSAlloc()
        sq = [wsn.get([128, 512], BF16) for _ in range(3)]
        sq_t = [S.T("msq%d" % i) for i in range(3)]
        rstd = [wsn.get([128, 512], F32) for _ in range(2)]
        rstd_t = [S.T("mrstd%d" % i) for i in range(2)]
        for t_i in range(NT):
            rmsnorm_tile(t_i, 8, hT, hT_tt, sq, sq_t, rstd, rstd_t)

        def gating(i):
            S.barrier()
            ws = WSAlloc()
            gated = ws.get([128, 8, 512], BF16)
            gated_t = S.T("gated")
            sig = [ws.get([128, 512], F32) for _ in range(2)]
            sig_t = [S.T("sig%d" % k) for k in range(2)]
            pb, pb_t = ring_next()
            wbr = pb[:, 0:4096].rearrange("p (a b) -> p a b", b=1024)
            wload(wbr, wbr_d[l, i].rearrange("(kc p) n -> p kc n", p=128), pb_t)
            wg = [wpage(O_G + i * 1024 + hh * 512, 512) for hh in range(2)]
            wo = []
            wov = wout_d[l].rearrange("(kc p) n -> p kc n", p=128)
            for hh in range(2):
                pg, pg_t = ring_next()
                v = pg[:, 0:4096].rearrange("p (a b) -> p a b", b=512)
                wload(v, wov[:, :, hh * 512:(hh + 1) * 512], pg_t)
                wo.append((v, pg_t))
            si = 0
            for t_i in range(NT):
                tok = slice(t_i * 512, (t_i + 1) * 512)
                for oc in range(8):
                    per_, pert_ = ps()
                    for kc in range(4):
                        mm(per_[:], wbr[:, kc, oc * 128:(oc + 1) * 128], yT[:, kc, tok], kc == 0, kc == 3,
                           [pb_t, yT_tt[t_i]], [pert_])
                    g_, gt_ = ps()
                    wgv, wg_t = wg[oc // 4]
                    for k in range(8):
                        mm(g_[:], wgv[:, k, (oc % 4) * 128:(oc % 4 + 1) * 128], hT[:, k, tok], k == 0, k == 7,
                           [wg_t, hT_tt[t_i]], [gt_])
                    sg_, sgt_ = sig[si % 2], sig_t[si % 2]
                    si += 1
                    act(sg_, g_[:], AF.Sigmoid, [gt_], [sgt_])
                    tt(gated[:, oc, :], per_[:], sg_, ALU.mult, [pert_, sgt_], [gated_t])
                for oc2 in range(8):
                    o_, ot_ = ps()
                    wov_, wo_t = wo[oc2 // 4]
                    for oc in range(8):
                        mm(o_[:], wov_[:, oc, (oc2 % 4) * 128:(oc2 % 4 + 1) * 128], gated[:, oc, :], oc == 0, oc == 7,
                           [wo_t, gated_t], [ot_])
                    tt(xT[:, oc2, tok], o_[:], xT[:, oc2, tok], ALU.add, [ot_, xT_t[t_i]], [xT_t[t_i]])
            S.barrier()

        def branch_d():
            S.barrier()
            ws = WSAlloc()
            tbuf = ws.get([128, 516], F32)
            tbuf_t = S.T("tbuf")
            acc = ws.get([128, 512], F32)
            acc_t = S.T("dacc")
            cgs = ws.get([128, 512], F32)
            cgs_t = S.T("cgs")
            wb = wpage(O_SC, 512)
            wc = wpage(O_SC + 512, 512)
            wx = wpage(O_SC + 1024, 512)
            for c in range(4):
                for t_i in range(NT):
                    tok = slice(t_i * 512, (t_i + 1) * 512)
                    pss = []
                    for (wv, w_t) in (wb, wc, wx):
                        p_, pt_ = ps()
                        for k in range(8):
                            mm(p_[:], wv[:, k, c * 128:(c + 1) * 128], hT[:, k, tok], k == 0, k == 7,
                               [w_t, hT_tt[t_i]], [pt_])
                        pss.append((p_, pt_))
                    (b_, bt_), (c_, ct_), (x_, xt_) = pss
                    act(cgs, c_[:], AF.Copy, [ct_], [cgs_t])
                    if t_i == 0:
                        S.op("dve", lambda e: e.memset(tbuf[:, 0:2], 0.0), [], [tbuf_t])
                    else:
                        cp(tbuf[:, 0:2], tbuf[:, 512:514], [tbuf_t], [tbuf_t])
                    tt(tbuf[:, 2:514], x_[:], cgs, ALU.mult, [xt_, cgs_t], [tbuf_t])
                    ts(acc, tbuf[:, 0:512], pc[:, 78 + c:79 + c], None, ALU.mult, None, [tbuf_t, pc_t], [acc_t])
                    for k in (1, 2):
                        stt(acc, tbuf[:, k:k + 512], pc[:, 78 + k * 4 + c:79 + k * 4 + c], acc, ALU.mult, ALU.add,
                            [tbuf_t, pc_t, acc_t], [acc_t])
                    tt(yT[:, c, tok], b_[:], acc, ALU.mult, [bt_, acc_t], [yT_tt[t_i]])

        def branch_b():
            S.barrier()
            ws = WSAlloc()
            wstg = ws.get([128, 4, 128], F32)
            wstg_t = S.PT("wstg")
            wsT = ws.get([128, 4, 128], BF16)
            wsT_t = S.T("wsT")
            bsrow = ws.get([1, 512], BF16)
            bsrow_t = S.T("bsrow")
            vnw = ws.get([128, 512], F32)
            vnw_t = S.PT("vnw")
            dma("sp", vnw, vnw_d[l:l + 1, :].partition_broadcast(128), vnw_t, writes=[vnw_t])
            vgs = [ws.get([128, 512], F32) for _ in range(2)]
            vg_ts = [S.T("vg%d" % i) for i in range(2)]
            vsqs = [ws.get([128, 512], F32) for _ in range(2)]
            vsq_ts = [S.T("vsq%d" % i) for i in range(2)]
            vsss = [ws.get([128, 2], F32) for _ in range(2)]
            vss_ts = [S.T("vss%d" % i) for i in range(2)]
            vns = [ws.get([128, 512], BF16) for _ in range(2)]
            vn_ts = [S.T("vn%d" % i) for i in range(2)]
            dma("sp", wstg, ws_d[l].rearrange("g t s -> t g s"), wstg_t, writes=[wstg_t])
            bsf = ws.get([1, 512], F32)
            bsf_t = S.PT("bsf")
            dma("sp", bsf, bs_d[l:l + 1, :], bsf_t, writes=[bsf_t])
            cp(bsrow, bsf, [bsf_t], [bsrow_t])
            p_, pt_ = ps()
            for g in range(4):
                tr(p_[:, g * 128:(g + 1) * 128], wstg[:, g, :], ident, [wstg_t, cst_t], [pt_])
            tt(wsT, v3(p_[:], 128), bc_mid(tri, 4), ALU.mult, [pt_, cst_t], [wsT_t])
            wu = wpage(O_UV, 512)
            wv = wpage(O_UV + 512, 512)
            for t_i in range(NT):
                tok = slice(t_i * 512, (t_i + 1) * 512)
                for c in range(4):
                    u_, ut_ = ps()
                    for k in range(8):
                        mm(u_[:], wu[0][:, k, c * 128:(c + 1) * 128], hT[:, k, tok], k == 0, k == 7,
                           [wu[1], hT_tt[t_i]], [ut_])
                    act(yT[:, c, tok], u_[:], AF.Gelu, [ut_], [yT_tt[t_i]])
                for b in range(4):
                    tb = slice(t_i * 512 + b * 128, t_i * 512 + (b + 1) * 128)
                    bi = b % 2
                    vg, vg_t, vsq, vsq_t = vgs[bi], vg_ts[bi], vsqs[bi], vsq_ts[bi]
                    vss, vss_t, vn, vn_t = vsss[bi], vss_ts[bi], vns[bi], vn_ts[bi]
                    v_, vt_ = ps()
                    for k in range(8):
                        mm(v_[:], hT[:, k, tb], wv[0][:, k, :], k == 0, k == 7, [wv[1], hT_tt[t_i]], [vt_])
                    act(vg, v_[:], AF.Gelu, [vt_], [vg_t])
                    tt(vsq, vg, vg, ALU.mult, [vg_t], [vsq_t])
                    S.op("dve", (lambda e, vss=vss, vsq=vsq: e.reduce_sum(out=vss[:, 0:1], in_=vsq, axis=mybir.AxisListType.X)), [vsq_t], [vss_t])
                    rsqrt_from_ss(vss[:, 1:2], vss[:, 0:1], 1.0 / 512, [vss_t], vss_t)
                    stt(vn, vg, vss[:, 1:2], vnw, ALU.mult, ALU.mult, [vg_t, vss_t, vnw_t], [vn_t])
                    sv_, svt_ = ps()
                    for g in range(4):
                        mm(sv_[:, g * 128:(g + 1) * 128], vn[:, g * 128:(g + 1) * 128], wsT[:, g, :], True, False,
                           [vn_t, wsT_t], [svt_])
                        mm(sv_[:, g * 128:(g + 1) * 128], ones16[0:1, :], bsrow[0:1, g * 128:(g + 1) * 128], False, True,
                           [cbf_t, bsrow_t], [svt_])
                    tt(yT[:, :, tb], yT[:, :, tb], v3(sv_[:], 128), ALU.mult, [svt_, yT_tt[t_i]], [yT_tt[t_i]])

        def branch_a():
            S.barrier()
            ws = WSAlloc()
            xbcT = ws.get([128, 8, 512], BF16); xbcT_t = S.T("xbcT")
            rawh = ws.get([128, 516], F32); rawh_t = S.T("rawh")
            acc = ws.get([128, 512], F32); acc_t = S.T("aacc")
            halo = ws.get([128, 8, 4], F32); halo_t = S.T("halo")
            dtr = ws.get([128, 32], F32); dtr_t = S.T("dtr")
            dtv = ws.get([128, 32], F32); dtv_t = S.T("dtv")
            av = ws.get([128, 32], F32); av_t = S.T("av")
            sm = ws.get([128, 128], F32); sm_t = S.T("sm")
            cbm = ws.get([128, 2, 128], F32); cbm_t = S.T("cbm")
            pgA, pgA_t = ring_next()
            pgB, pgB_t = ring_next()
            fA = pgA[:].bitcast(F32)
            fB = pgB[:].bitcast(F32)
            Rbs = [ws.get([128, 8, 128], F32), fA[:, 0:1024].rearrange("p (a b) -> p a b", b=128)]
            W1s = [ws.get([128, 8, 128], F32), fA[:, 1024:2048].rearrange("p (a b) -> p a b", b=128)]
            MTs = [ws.get([128, 8, 128], BF16), pgB[:, 0:1024].rearrange("p (a b) -> p a b", b=128)]
            xdts = [ws.get([128, 8, 64], BF16), pgB[:, 1024:1536].rearrange("p (a b) -> p a b", b=64)]
            xws = [ws.get([128, 8, 64], BF16), pgB[:, 1536:2048].rearrange("p (a b) -> p a b", b=64)]
            Btoks = [ws.get([128, 256], BF16), pgB[:, 2048:2304]]
            ysbs = [ws.get([128, 512], F32), fB[:, 1280:1792]]
            Rb_ts = [S.T("Rb%d" % i) for i in range(2)]
            W1_ts = [S.T("W1%d" % i) for i in range(2)]
            MT_ts = [S.T("MT%d" % i) for i in range(2)]
            xdt_ts = [S.T("xdt%d" % i) for i in range(2)]
            xw_ts = [S.T("xw%d" % i) for i in range(2)]
            Btok_ts = [S.T("Btok%d" % i) for i in range(2)]
            ysb_ts = [S.T("ysb%d" % i) for i in range(2)]
            MT, MT_t = MTs[0], MT_ts[0]
            pg_first = [True]
            st32 = ws.get([128, 8, 64], F32); st32_t = S.T("st32")
            st16 = ws.get([128, 512], BF16); st16_t = S.T("st16")
            ysb = ws.get([128, 512], F32); ysb_t = S.T("ysb")
            yraw = ws.get([128, 4, 512], F32); yraw_t = S.T("yraw")
            sz = acc; sz_t = acc_t
            gsq = MT.rearrange("p a b -> p (a b)")[:, 0:512]; gsq_t = MT_t
            grs = rawh[:, 0:512]; grs_t = rawh_t
            wz = wpage(O_Z, 512)
            wx0 = wpage(O_XBC, 512)
            wx1 = wpage(O_XBC + 512, 512)
            wdt = wpage(O_DT, 8)
            wxb = (wx0, wx1)
            acs4, eacs4, dout4, eatot4 = sm[:, 0:32], sm[:, 32:64], sm[:, 64:96], sm[:, 96:128]
            for t_i in range(NT):
                tok = slice(t_i * 512, (t_i + 1) * 512)
                for f in range(8):
                    p_, pt_ = ps()
                    wv, w_t = wxb[f // 4]
                    for k in range(8):
                        mm(p_[:], wv[:, k, (f % 4) * 128:(f % 4 + 1) * 128], hT[:, k, tok], k == 0, k == 7,
                           [w_t, hT_tt[t_i]], [pt_])
                    if t_i == 0:
                        S.op("dve", lambda e: e.memset(rawh[:, 0:3], 0.0), [], [rawh_t])
                    else:
                        cp(rawh[:, 0:3], halo[:, f, 0:3], [halo_t], [rawh_t])
                    act(rawh[:, 3:515], p_[:], AF.Copy, [pt_], [rawh_t])
                    cp(halo[:, f, 0:3], rawh[:, 512:515], [rawh_t], [halo_t])
                    ts(acc, rawh[:, 0:512], pc[:, 24 + f:25 + f], None, ALU.mult, None, [rawh_t, pc_t], [acc_t])
                    for k in (1, 2, 3):
                        stt(acc, rawh[:, k:k + 512], pc[:, 24 + k * 8 + f:25 + k * 8 + f], acc, ALU.mult, ALU.add,
                            [rawh_t, pc_t, acc_t], [acc_t])
                    act(xbcT[:, f, :], acc, AF.Silu, [acc_t, pc_t], [xbcT_t], bias=pc[:, 56 + f:57 + f])
                d_, dt_ = ps()
                for c in range(4):
                    for k in range(8):
                        mm(d_[:, c * 8:(c + 1) * 8], hT[:, k, t_i * 512 + c * 128:t_i * 512 + (c + 1) * 128], wdt[0][:, k, :],
                           k == 0, k == 7, [wdt[1], hT_tt[t_i]], [dt_])
                tt(v3(dtr, 8), v3(d_[:, 0:32], 8), bc_mid(prow[:, 0:8], 4), ALU.add, [dt_, prow_t], [dtr_t])
                act(dtr, dtr, AF.Exp, [dtr_t], [dtr_t])
                act(dtv, dtr, AF.Ln, [dtr_t], [dtv_t], bias=1.0)
                stt(v3(av, 8), v3(dtv, 8), -1.0, bc_mid(expA[:], 4), ALU.mult, ALU.mult, [dtv_t, expA_t], [av_t])
                cu_, cut_ = ps()
                mm(cu_[:, 0:32], tri, av, True, True, [cst_t, av_t], [cut_])
                mm(cu_[:, 32:64], ones32, av, True, True, [cst_t, av_t], [cut_])
                act(acs4, cu_[:, 0:32], AF.Copy, [cut_], [sm_t])
                act(eacs4, cu_[:, 0:32], AF.Exp, [cut_], [sm_t])
                act(eatot4, cu_[:, 32:64], AF.Exp, [cut_], [sm_t])
                tt(dout4, cu_[:, 32:64], acs4, ALU.subtract, [cut_, sm_t], [sm_t])
                act(dout4, dout4, AF.Exp, [sm_t], [sm_t])
                for c in range(4):
                    gc = t_i * 4 + c
                    ct = slice(c * 128, (c + 1) * 128)
                    a_c = av[:, c * 8:(c + 1) * 8]
                    bi = c % 2
                    Rb, Rb_t, W1, W1_t = Rbs[bi], Rb_ts[bi], W1s[bi], W1_ts[bi]
                    MTc, MTc_t, xdt, xdt_t = MTs[bi], MT_ts[bi], xdts[bi], xdt_ts[bi]
                    xw, xw_t, Btok, Btok_t, ysb, ysb_t = xws[bi], xw_ts[bi], Btoks[bi], Btok_ts[bi], ysbs[bi], ysb_ts[bi]
                    pgr = [pgA_t, pgB_t] if bi == 1 else []
                    pgw = [pgA_t, pgB_t] if (bi == 1 and pg_first[0]) else []
                    if bi == 1:
                        pg_first[0] = False
                    acs = acs4[:, c * 8:(c + 1) * 8]
                    eacs = eacs4[:, c * 8:(c + 1) * 8]
                    dout = dout4[:, c * 8:(c + 1) * 8]
                    eatot = eatot4[:, c * 8:(c + 1) * 8]
                    tt(Rb, bc_mid(tri, 8), bc_last(a_c, 128), ALU.mult, [cst_t, av_t] + pgr, [Rb_t] + pgw)
                    cb_, cbt_ = ps()
                    for g in range(2):
                        mm(cb_[:, g * 128:(g + 1) * 128], xbcT[:, 4 + g, ct], xbcT[:, 6 + g, ct], True, True,
                           [xbcT_t], [cbt_])
                    tt(cbm, v3(cb_[:, 0:256], 128), bc_mid(tri, 2), ALU.mult, [cbt_, cst_t], [cbm_t])
                    for g in range(2):
                        bc_, bct_ = ps()
                        mm(bc_[:], triS, Rb[:, g * 4:(g + 1) * 4, :].rearrange("p a b -> p (a b)"), True, True,
                           [cst_t, Rb_t] + pgr, [bct_])
                        act(W1[:, g * 4:(g + 1) * 4, :], v3(bc_[:], 128), AF.Exp, [bct_] + pgr, [W1_t])
                    for g in range(2):
                        tt(MTc[:, g * 4:(g + 1) * 4, :], W1[:, g * 4:(g + 1) * 4, :], bc_mid(cbm[:, g, :], 4),
                           ALU.mult, [W1_t, cbm_t] + pgr, [MTc_t])
                    xs_, xst_ = ps()
                    xs16 = xs_[:].bitcast(BF16)
                    for cc in range(4):
                        tr(xs16[:, cc * 128:(cc + 1) * 128], xbcT[:, cc, ct], ident16, [xbcT_t, cbf_t], [xst_])
                    tt(xdt, v3(xs16[:, 0:512], 64), bc_last(dtv[:, c * 8:(c + 1) * 8], 64), ALU.mult, [xst_, dtv_t] + pgr, [xdt_t])
                    tt(xw, xdt, bc_last(dout, 64), ALU.mult, [xdt_t, sm_t] + pgr, [xw_t])
                    b_, bt_ = ps()
                    b16 = b_[:].bitcast(BF16)
                    for g in range(2):
                        tr(b16[:, g * 128:(g + 1) * 128], xbcT[:, 4 + g, ct], ident16, [xbcT_t, cbf_t], [bt_])
                    act(Btok, b16[:, 0:256], AF.Copy, [bt_] + pgr, [Btok_t])
                    y_, yt_ = ps()
                    for h in range(8):
                        mm(y_[:, h * 64:(h + 1) * 64], MTc[:, h, :], xdt[:, h, :], True, True, [MTc_t, xdt_t] + pgr, [yt_])
                    if gc > 0:
                        yo_, yot_ = ps()
                        for g in range(2):
                            mm(yo_[:, g * 256:(g + 1) * 256], xbcT[:, 6 + g, ct], st16[:, g * 256:(g + 1) * 256], True, True,
                               [xbcT_t, st16_t], [yot_])
                        tt(v3(ysb, 64), v3(yo_[:], 64), bc_last(eacs, 64), ALU.mult, [yot_, sm_t] + pgr, [ysb_t])
                        tt(ysb, ysb, y_[:], ALU.add, [ysb_t, yt_] + pgr, [ysb_t])
                    else:
                        cp(ysb, y_[:], [yt_] + pgr, [ysb_t])
                    s_, st_ = ps()
                    for g in range(2):
                        mm(s_[:, g * 256:(g + 1) * 256], Btok[:, g * 128:(g + 1) * 128],
                           xw[:, g * 4:(g + 1) * 4, :].rearrange("p a b -> p (a b)"), True, True, [Btok_t, xw_t] + pgr, [st_])
                    if gc > 0:
                        tt(st32, st32, bc_last(eatot, 64), ALU.mult, [st32_t, sm_t], [st32_t])
                        tt(st32, st32, v3(s_[:], 64), ALU.add, [st32_t, st_], [st32_t])
                    else:
                        cp(st32, v3(s_[:], 64), [st_], [st32_t])
                    act(st16, st32.rearrange("p a b -> p (a b)"), AF.Copy, [st32_t], [st16_t])
                    yT_, yTt_ = ps()
                    for cc in range(4):
                        tr(yT_[:, cc * 128:(cc + 1) * 128], ysb[:, cc * 128:(cc + 1) * 128], ident, [ysb_t, cst_t] + pgr, [yTt_])
                    act(yraw[:, :, ct], v3(yT_[:], 128), AF.Copy, [yTt_], [yraw_t])
                for cc in range(4):
                    stt(yraw[:, cc, :], xbcT[:, cc, :], pc[:, 96 + cc:97 + cc], yraw[:, cc, :], ALU.mult, ALU.add,
                        [xbcT_t, pc_t, yraw_t], [yraw_t])
                for cc in range(4):
                    z_, zt_ = ps()
                    for k in range(8):
                        mm(z_[:], wz[0][:, k, cc * 128:(cc + 1) * 128], hT[:, k, tok], k == 0, k == 7,
                           [wz[1], hT_tt[t_i]], [zt_])
                    act(sz, z_[:], AF.Silu, [zt_], [sz_t])
                    tt(yraw[:, cc, :], yraw[:, cc, :], sz, ALU.mult, [yraw_t, sz_t], [yraw_t])
                for g in range(2):
                    ss_, sst_ = ps()
                    for j, cc in enumerate((2 * g, 2 * g + 1)):
                        act(gsq, yraw[:, cc, :], AF.Square, [yraw_t], [gsq_t])
                        mm(ss_[:], ones16, gsq, j == 0, j == 1, [gsq_t, cbf_t], [sst_])
                    rsqrt_from_ss(grs, ss_[:], 1.0 / 256, [sst_], grs_t)
                    for cc in (2 * g, 2 * g + 1):
                        stt(yT[:, cc, tok], yraw[:, cc, :], pc[:, 64 + cc:65 + cc], grs, ALU.mult, ALU.mult,
                            [yraw_t, pc_t, grs_t], [yT_tt[t_i]])


        def branch_c():
            S.barrier()
            import math
            ws = WSAlloc()
            qnT = ws.get([128, 3, L], BF16); qnT_tt = [S.T("qnT%d" % i) for i in range(NT)]
            kvnT = ws.get([128, L], BF16); kvnT_tt = [S.T("kvnT%d" % i) for i in range(NT)]
            kper = ws.get([64, L], F32); kper_tt = [S.T("kper%d" % i) for i in range(NT)]
            sqkpe = ws.get([64, L], BF16); sqkpe_tt = [S.T("sqkpe%d" % i) for i in range(NT)]
            sqb = ws.get([128, 512], BF16); sqb_t = S.T("csq")
            rs = ws.get([128, 512], F32); rs_t = S.T("crs")
            t1 = ws.get([128, 512], F32); t1_t = S.PT("ct1")
            t2 = ws.get([128, 512], F32); t2_t = S.T("ct2")
            Qns = [ws.get([128, 512], BF16) for _ in range(2)]; Qn_ts = [S.T("Qn%d" % i) for i in range(2)]
            Qrs = [ws.get([64, 512], BF16) for _ in range(2)]; Qr_ts = [S.T("Qr%d" % i) for i in range(2)]
            pT = [ws.get([128, 512], BF16) for _ in range(3)]; pT_t = [S.T("pT%d" % i) for i in range(3)]
            pq, pq_t = ring_next()
            wql = pq[:, 0:8 * 384].rearrange("p (a b) -> p a b", b=384)
            wload(wql, win[:, :, O_QL:O_QL + 384], pq_t)
            pk, pk_t = ring_next()
            wkl = pk[:, 0:8 * 192].rearrange("p (a b) -> p a b", b=192)
            wload(wkl, win[:, :, O_KVL:O_KVL + 192], pk_t)
            pk2, pk2_t = ring_next()
            wks = pk2[:, 0:8 * 64].rearrange("p (a b) -> p a b", b=64)
            wks_b = pk2[:, 1024:1024 + 8 * 64].rearrange("p (a b) -> p a b", b=64)
            dma("pool", wks[:, :, 0:32], win[:, :, O_KPE + 32:O_KPE + 64], pk2_t, writes=[pk2_t])
            dma("pool", wks[:, :, 32:64], win[:, :, O_KPE:O_KPE + 32], pk2_t, writes=[pk2_t])
            pcs, pcs_t = ring_next()
            cs32 = pcs[:].bitcast(F32)
            assert L <= 1024 or True
            cos2 = None
            if 2 * L * 4 <= 8192:
                cos2 = cs32[0:64, 0:L]
                sin2 = cs32[0:64, L:2 * L]
                sin_t = pcs_t
            else:
                pcs2, pcs2_t = ring_next()
                cos2 = cs32[0:64, 0:L]
                sin2 = pcs2[:].bitcast(F32)[0:64, 0:L]
                sin_t = pcs2_t
            cos_t = pcs_t
            invf = cst[0:64, 512:513]
            sgn = cst[0:64, 513:514]
            TWO_PI = 2.0 * math.pi
            posi = t1[0:64, :].bitcast(I32)
            for t_i in range(NT):
                tok = slice(t_i * 512, (t_i + 1) * 512)
                a_, k_, m_ = t2[0:64, :], rs[0:64, :], t1[0:64, :]
                dma("sp", posi, pos_d[s:s + 1, tok].partition_broadcast(64), t1_t, writes=[t1_t])
                cp(a_, posi, [t1_t], [t2_t])
                ts(a_, a_, invf, None, ALU.mult, None, [t2_t, cst_t], [t2_t])
                for which, dst, dst_t in ((0, sin2, sin_t), (1, cos2, cos_t)):
                    if which == 1:
                        ts(a_, a_, math.pi / 2, None, ALU.add, None, [t2_t], [t2_t])
                    ts(k_, a_, 1.0 / TWO_PI, None, ALU.mult, None, [t2_t], [rs_t])
                    cp(posi, k_, [rs_t], [t1_t])
                    cp(k_, posi, [t1_t], [rs_t])
                    stt(k_, k_, -TWO_PI, a_, ALU.mult, ALU.add, [rs_t, t2_t], [rs_t])
                    ts(m_, k_, math.pi, None, ALU.is_gt, None, [rs_t], [t1_t])
                    stt(k_, m_, -TWO_PI, k_, ALU.mult, ALU.add, [t1_t, rs_t], [rs_t])
                    ts(m_, k_, -math.pi, None, ALU.is_lt, None, [rs_t], [t1_t])
                    stt(k_, m_, TWO_PI, k_, ALU.mult, ALU.add, [t1_t, rs_t], [rs_t])
                    ts(k_, k_, math.pi, -math.pi, ALU.min, ALU.max, [rs_t], [rs_t])
                    act(dst[:, tok], k_, AF.Sin, [rs_t], [dst_t])
                ts(sin2[:, tok], sin2[:, tok], sgn, None, ALU.mult, None, [sin_t, cst_t], [sin_t])
            for t_i in range(NT):
                tok = slice(t_i * 512, (t_i + 1) * 512)
                qps = []
                for kc in range(3):
                    p_, pt_ = ps()
                    for k in range(8):
                        mm(p_[:], wql[:, k, kc * 128:(kc + 1) * 128], hT[:, k, tok], k == 0, k == 7, [pq_t, hT_tt[t_i]], [pt_])
                    qps.append((p_, pt_))
                ss_, sst_ = ps()
                for kc in range(3):
                    act(sqb, qps[kc][0][:], AF.Square, [qps[kc][1]], [sqb_t])
                    mm(ss_[:], ones16, sqb, kc == 0, kc == 2, [sqb_t, cbf_t], [sst_])
                rsqrt_from_ss(rs, ss_[:], 1.0 / 384, [sst_], rs_t)
                for kc in range(3):
                    stt(qnT[:, kc, tok], qps[kc][0][:], pc[:, 68 + kc:69 + kc], rs, ALU.mult, ALU.mult,
                        [qps[kc][1], pc_t, rs_t], [qnT_tt[t_i]])
                p_, pt_ = ps()
                for k in range(8):
                    mm(p_[:], wkl[:, k, 0:128], hT[:, k, tok], k == 0, k == 7, [pk_t, hT_tt[t_i]], [pt_])
                act(sqb, p_[:], AF.Square, [pt_], [sqb_t])
                ss_, sst_ = ps()
                mm(ss_[:], ones16, sqb, True, True, [sqb_t, cbf_t], [sst_])
                rsqrt_from_ss(rs, ss_[:], 1.0 / 128, [sst_], rs_t)
                stt(kvnT[:, tok], p_[:], pc[:, 71:72], rs, ALU.mult, ALU.mult, [pt_, pc_t, rs_t], [kvnT_tt[t_i]])
                kp_, kpt_ = ps()
                for k in range(8):
                    mm(kp_[0:64, :], wkl[:, k, 128:192], hT[:, k, tok], k == 0, k == 7, [pk_t, hT_tt[t_i]], [kpt_])
                ks_, kst_ = ps()
                for k in range(8):
                    mm(ks_[0:64, :], wks[:, k, :], hT[:, k, tok], k == 0, k == 7, [pk2_t, hT_tt[t_i]], [kst_])
                act(sqkpe[:, tok], kp_[0:64, :], AF.Square, [kpt_], [sqkpe_tt[t_i]])
                stt(t1[0:64, :], kp_[0:64, :], pc[0:64, 76:77], cos2[:, tok], ALU.mult, ALU.mult, [kpt_, pc_t, cos_t], [t1_t])
                stt(t2[0:64, :], ks_[0:64, :], pc[0:64, 77:78], sin2[:, tok], ALU.mult, ALU.mult, [kst_, pc_t, sin_t], [t2_t])
                tt(kper[:, tok], t1[0:64, :], t2[0:64, :], ALU.add, [t1_t, t2_t], [kper_tt[t_i]])
            pw, pw_t = ring_next()
            wqb = pw[:, 0:3 * 768].rearrange("p (a b) -> p a b", b=768)
            wqs = pw[:, 3072:3072 + 3 * 256].rearrange("p (a b) -> p a b", b=256)
            wqv = wqb_d[l].rearrange("(kc p) n -> p kc n", p=128)
            wq4 = wqb_d[l].rearrange("(kc p) (h d) -> p kc h d", p=128, d=192)
            dma("pool", wqb, wqv, pw_t, writes=[pw_t])
            wqs4 = wqs.rearrange("p a (h d) -> p a h d", d=64)
            for kc in range(3):
                dma("pool", wqs4[:, kc, :, 0:32], wq4[:, kc, :, 160:192], pw_t, writes=[pw_t])
                dma("pool", wqs4[:, kc, :, 32:64], wq4[:, kc, :, 128:160], pw_t, writes=[pw_t])
            pv, pv_t = ring_next()
            wkvb = pv[:, 0:1024]
            wload(wkvb, wkvb_d[l], pv_t)
            scale = 192.0 ** -0.5
            pKn, pKn_t = ring_next()
            pKV, pKV_t = ring_next()
            Kn_tt = [S.T("Kn%d" % i) for i in range(NT)]
            KV_tt = [S.T("KV%d" % i) for i in range(NT)]
            Kn = pKn[:, 0:L]
            Kr = pKn[0:64, 2048:2048 + L]
            Vt = pKV[:, 0:NB * 128].rearrange("p (a b) -> p a b", b=128)
            first = [True]

            def prep(h, t_i):
                tok = slice(t_i * 512, (t_i + 1) * 512)
                Qn, Qn_t, Qr, Qr_t = Qns[t_i % 2], Qn_ts[t_i % 2], Qrs[t_i % 2], Qr_ts[t_i % 2]
                wK = [Kn_tt[t_i]] + ([pKn_t] if first[0] else [])
                wV = [KV_tt[t_i]] + ([pKV_t] if first[0] else [])
                first[0] = False
                kn_, knt_ = ps()
                mm(kn_[:], wkvb[:, h * 256:h * 256 + 128], kvnT[:, tok], True, True, [pv_t, kvnT_tt[t_i]], [knt_])
                act(sqb, kn_[:], AF.Square, [knt_], [sqb_t])
                ss_, sst_ = ps()
                mm(ss_[:], ones16, sqb, True, False, [sqb_t, cbf_t], [sst_])
                mm(ss_[:], ones16[0:64, :], sqkpe[:, tok], False, True, [sqkpe_tt[t_i], cbf_t], [sst_])
                rsqrt_from_ss(rs, ss_[:], 1.0 / 192, [sst_], rs_t)
                stt(Kn[:, tok], kn_[:], pc[:, 75:76], rs, ALU.mult, ALU.mult, [knt_, pc_t, rs_t], wK)
                tt(Kr[:, tok], kper[:, tok], rs[0:64, :], ALU.mult, [kper_tt[t_i], rs_t], [Kn_tt[t_i]])
                v_, vt_ = ps()
                for b_ in range(4):
                    tb = slice(t_i * 512 + b_ * 128, t_i * 512 + (b_ + 1) * 128)
                    mm(v_[:, b_ * 128:(b_ + 1) * 128], kvnT[:, tb], wkvb[:, h * 256 + 128:h * 256 + 256], True, True,
                       [pv_t, kvnT_tt[t_i]], [vt_])
                act(Vt[:, t_i * 4:(t_i + 1) * 4, :], v3(v_[:], 128), AF.Copy, [vt_], wV)
                qn_, qnt_ = ps()
                for kc in range(3):
                    mm(qn_[:], wqb[:, kc, h * 192:h * 192 + 128], qnT[:, kc, tok], kc == 0, kc == 2, [pw_t, qnT_tt[t_i]], [qnt_])
                qr_, qrt_ = ps()
                for kc in range(3):
                    mm(qr_[0:64, :], wqb[:, kc, h * 192 + 128:h * 192 + 192], qnT[:, kc, tok], kc == 0, kc == 2,
                       [pw_t, qnT_tt[t_i]], [qrt_])
                qs_, qst_ = ps()
                for kc in range(3):
                    mm(qs_[0:64, :], wqs[:, kc, h * 64:(h + 1) * 64], qnT[:, kc, tok], kc == 0, kc == 2,
                       [pw_t, qnT_tt[t_i]], [qst_])
                ss_, sst_ = ps()
                act(sqb, qn_[:], AF.Square, [qnt_], [sqb_t])
                mm(ss_[:], ones16, sqb, True, False, [sqb_t, cbf_t], [sst_])
                act(Qr, qr_[0:64, :], AF.Square, [qrt_], [Qr_t])
                mm(ss_[:], ones16[0:64, :], Qr, False, True, [Qr_t, cbf_t], [sst_])
                rsqrt_from_ss(rs, ss_[:], 1.0 / 192, [sst_], rs_t)
                stt(Qn, qn_[:], pc[:, 72:73], rs, ALU.mult, ALU.mult, [qnt_, pc_t, rs_t], [Qn_t])
                stt(t1[0:64, :], qr_[0:64, :], pc[0:64, 73:74], cos2[:, tok], ALU.mult, ALU.mult, [qrt_, pc_t, cos_t], [t1_t])
                stt(t2[0:64, :], qs_[0:64, :], pc[0:64, 74:75], sin2[:, tok], ALU.mult, ALU.mult, [qst_, pc_t, sin_t], [t2_t])
                tt(t1[0:64, :], t1[0:64, :], t2[0:64, :], ALU.add, [t1_t, t2_t], [t1_t])
                tt(Qr, t1[0:64, :], rs[0:64, :], ALU.mult, [t1_t, rs_t], [Qr_t])

            def attention(h, t_i):
                tok = slice(t_i * 512, (t_i + 1) * 512)
                Qn, Qn_t, Qr, Qr_t = Qns[t_i % 2], Qn_ts[t_i % 2], Qrs[t_i % 2], Qr_ts[t_i % 2]
                ps_n[0] = 6
                o_, ot_ = psb[6], psb_t[6]
                dn_, dnt_ = psb[7], psb_t[7]
                nk = 4 * t_i + 4

                def s_stage(kc):
                    j = kc - 4 * t_i
                    q0 = j * 128 if j >= 0 else 0
                    kk = slice(kc * 128, (kc + 1) * 128)
                    ktt = Kn_tt[kc // 4]
                    s_, st_ = ps()
                    mm(s_[:, q0:512], Kn[:, kk], Qn[:, q0:512], True, False, [pKn_t, ktt, Qn_t], [st_])
                    mm(s_[:, q0:512], Kr[:, kk], Qr[:, q0:512], False, True, [pKn_t, ktt, Qr_t], [st_])
                    p_i, p_it = pT[kc % 3], pT_t[kc % 3]
                    act(p_i[:, q0:512], s_[:, q0:512], AF.Exp, [st_], [p_it], scale=scale)
                    if j >= 0:
                        tt(p_i[:, q0:q0 + 128], p_i[:, q0:q0 + 128], tri16, ALU.mult, [p_it, cbf_t], [p_it])
                    return (kc, q0, p_i, p_it)

                def pv_stage(item):
                    kc, q0, p_i, p_it = item
                    mm(o_[:, q0:512], Vt[:, kc, :], p_i[:, q0:512], kc == 0, kc == nk - 1, [pKV_t, KV_tt[kc // 4], p_it], [ot_])
                    mm(dn_[:, q0:512], ones16, p_i[:, q0:512], kc == 0, kc == nk - 1, [cbf_t, p_it], [dnt_])

                pend_ = []
                for kc in range(nk):
                    pend_.append(s_stage(kc))
                    if len(pend_) > 2:
                        pv_stage(pend_.pop(0))
                while pend_:
                    pv_stage(pend_.pop(0))
                S.op("dve", lambda e: e.reciprocal(out=rs, in_=dn_[:]), [dnt_], [rs_t], fs=512)
                tt(yT[:, h, tok], o_[:], rs, ALU.mult, [ot_, rs_t], [yT_tt[t_i]])
                ps_n[0] = 8

            for h in range(4):
                prep(h, 0)
                for t_i in range(NT):
                    if t_i + 1 < NT:
                        prep(h, t_i + 1)
                    attention(h, t_i)

        env = dict(locals())
        for i, (ch, fn) in enumerate((("A", branch_a), ("B", branch_b), ("C", None), ("D", branch_d))):
            if ch not in cfg.get("branches", "ABCD"):
                continue
            if ch == "C":
                branch_c()
            else:
                fn()
            gating(i)

    mixer = cfg.get("mixer", None)
    for s in range(NSEQ):
        S.epoch = s
        S.barrier()
        load_x(s)
        for l in range(DEPTH):
            S.barrier()
            load_layer_params(l)
            if cfg.get("ffn1", True):
                ffn(l, f1gu, f1dn, 0)
            if mixer is not None:
                mixer_layer(l, s)
            if cfg.get("ffn2", True):
                ffn(l, f2gu, f2dn, 16)
        S.barrier()
        store_x(s)
    S.barrier()
    S.emit()
    es.close()
    return nc


def host_consts():
    c = np.zeros((128, 648), np.float32)
    c[:, 0:128] = np.eye(128, dtype=np.float32)
    s = np.arange(128)[:, None]
    t = np.arange(128)[None, :]
    c[:, 128:256] = (s <= t).astype(np.float32)
    c[:, 256:384] = np.where(t >= s, 0.0, -30000.0).astype(np.float32)
    c[:, 384:512] = 1.0
    invf = (10000.0 ** (-(np.arange(0, 64, 2, dtype=np.float32) / np.float32(64.0)))).astype(np.float32)
    c[0:32, 512] = invf
    c[32:64, 512] = invf
    c[0:32, 513] = -1.0
    c[32:64, 513] = 1.0
    c[:, 520:648] = (s > t).astype(np.float32)
    return c


def host_layout(inp, DEPTH):
    pcols = np.zeros((DEPTH, NPC, 128), np.float32)
    prow = np.zeros((DEPTH, NPR), np.float32)
    for l in range(DEPTH):
        pcols[l, 0:8] = inp["ffn1_norm"][l].reshape(8, 128)
        pcols[l, 8:16] = inp["mix_norm"][l].reshape(8, 128)
        pcols[l, 16:24] = inp["ffn2_norm"][l].reshape(8, 128)
        pcols[l, 24:56] = inp["ssd_conv_w"][l].reshape(4, 8, 128).reshape(32, 128)
        pcols[l, 56:64] = inp["ssd_conv_b"][l].reshape(8, 128)
        pcols[l, 64:68] = inp["ssd_norm"][l].reshape(4, 128)
        pcols[l, 68:71] = inp["mla_q_norm"][l].reshape(3, 128)
        pcols[l, 71] = inp["mla_kv_norm"][l]
        for r0, w in ((72, inp["mla_qk_q"][l]), (75, inp["mla_qk_k"][l])):
            pcols[l, r0] = w[0:128]
            pcols[l, r0 + 1, 0:64] = w[128:192]
            pcols[l, r0 + 2, 0:32] = w[160:192]
            pcols[l, r0 + 2, 32:64] = w[128:160]
        pcols[l, 78:90] = inp["sc_conv_w"][l].reshape(3, 4, 128).reshape(12, 128)
        pcols[l, 96:100] = np.repeat(inp["ssd_d"][l], 64).reshape(4, 128)
        prow[l, 0:8] = inp["ssd_dt_bias"][l]
        prow[l, 8:16] = inp["ssd_a_log"][l]
        prow[l, 16:24] = inp["ssd_d"][l]
    return pcols, prow


_CACHE = {}


def make_in_maps(inp, NCORE, NSEQ, DEPTH):
    pcols, prow = host_layout(inp, DEPTH)
    cstv = host_consts()
    shared = {
        "cst": cstv, "pcols": pcols, "prow": prow,
        "ffn1_w_gu": inp["ffn1_w_gu"], "ffn1_w_down": inp["ffn1_w_down"],
        "ffn2_w_gu": inp["ffn2_w_gu"], "ffn2_w_down": inp["ffn2_w_down"],
        "w_in": inp["w_in"], "gmlp_w_s": inp["gmlp_w_s"],
        "gmlp_b_s": inp["gmlp_b_s"].reshape(DEPTH, 512),
        "gmlp_v_norm": inp["gmlp_v_norm"],
        "mla_w_qb": inp["mla_w_qb"], "mla_w_kvb": inp["mla_w_kvb"],
        "w_branch": inp["w_branch"], "w_out": inp["w_out"],
    }
    in_maps = []
    for c in range(NCORE):
        m = dict(shared)
        m["x"] = np.ascontiguousarray(inp["x"][c * NSEQ:(c + 1) * NSEQ])
        m["positions"] = np.ascontiguousarray(inp["positions"][c * NSEQ:(c + 1) * NSEQ]).astype(np.int32)
        in_maps.append(m)
    return in_maps


def mixer_block(env, l, s):
    pass


def kernel(**inputs):
    inp = {k: np.asarray(v) for k, v in inputs.items()}
    B, L, _ = inp["x"].shape
    DEPTH = inp["w_in"].shape[0]
    NCORE = 8
    NSEQ = B // NCORE
    key = (L, NSEQ, DEPTH)
    if key not in _CACHE:
        _CACHE[key] = build_program(L, NSEQ, DEPTH, {"mixer": mixer_block})
    nc = _CACHE[key]
    in_maps = make_in_maps(inp, NCORE, NSEQ, DEPTH)
    res = run_bass_kernel_spmd(nc, in_maps, core_ids=list(range(NCORE)))
    return np.concatenate([r["out"] for r in res.results], axis=0)
```

```python
import numpy as np
import concourse.bass as bass
import concourse.mybir as mybir
from concourse.bass_utils import run_bass_kernel_spmd
from contextlib import ExitStack

F32 = mybir.dt.float32
BF16 = mybir.dt.bfloat16
I32 = mybir.dt.int32
AF = mybir.ActivationFunctionType
ALU = mybir.AluOpType

D = 1024
NCH = 8
DFF = 2816
NJ = 22
INTOT = 8776
EPS = 1e-6
O_Z, O_XBC, O_DT, O_UV, O_QL, O_KVL, O_KPE, O_SC, O_G = 0, 512, 1536, 1544, 2568, 2952, 3080, 3144, 4680
NPC = 112
NPR = 24


class T:
    __slots__ = ("name", "w", "r", "sem", "ndma", "persist")

    def __init__(self, name, persist=False):
        self.name = name
        self.persist = persist
        self.w = None
        self.r = {}
        self.sem = None
        self.ndma = 0


class Op:
    __slots__ = ("eng", "fn", "deps", "needs", "key", "val", "dma", "epoch", "fs")


ENGS = ("pe", "act", "dve", "pool", "sp")


class Sched:
    def __init__(self, nc, es):
        self.nc = nc
        self.es = es
        self.ops = {e: [] for e in ENGS}
        self.epoch = 0
        self.dma_ops = []
        self.tiles = []

    def T(self, name, persist=False):
        t = T(name, persist)
        self.tiles.append(t)
        return t

    def PT(self, name):
        if not hasattr(self, "_pt"):
            self._pt = {}
        if name not in self._pt:
            self._pt[name] = self.T(name)
        return self._pt[name]

    def sb(self, name, shape, dtype):
        return self.es.enter_context(self.nc.sbuf_tensor("sb_" + name, list(shape), dtype))

    def op(self, eng, fn, reads=(), writes=(), dma_tile=None, fs=0):
        o = Op()
        o.fs = fs
        o.eng = eng
        o.fn = fn
        o.deps = []
        o.needs = False
        o.dma = dma_tile
        o.epoch = self.epoch
        o.key = None
        o.val = 0
        is_dma = dma_tile is not None

        def dep(p, raw=False):
            if p is None or p is o:
                return
            if p.dma is None and not is_dma and p.eng == eng:
                if not raw or eng == "pe":
                    return
                if p.fs >= 512 and o.fs >= 512:
                    return
            p.needs = True
            o.deps.append(p)

        for t in reads:
            dep(t.w, True)
        for t in writes:
            dep(t.w)
            for r in t.r.values():
                dep(r)
        for t in reads:
            t.r[("dma", id(o)) if is_dma else eng] = o
        for t in writes:
            t.w = o
            t.r = {}
        if is_dma:
            o.needs = True
            if not dma_tile.persist:
                self.dma_ops.append(o)
        self.ops[eng].append(o)
        return o

    def barrier(self):
        lasts = []
        BENGS = ("pe", "act", "dve", "sp")
        for e in BENGS:
            for o in reversed(self.ops[e]):
                if o.dma is None and o.fn is not None:
                    lasts.append(o)
                    break
        pend = list(self.dma_ops)
        self.dma_ops = []
        for e in BENGS:
            o = Op()
            o.fs = 0
            o.eng = e
            o.fn = None
            o.deps = []
            o.needs = False
            o.dma = None
            o.epoch = self.epoch
            o.key = None
            o.val = 0
            for p in lasts:
                if p.eng != e:
                    p.needs = True
                    o.deps.append(p)
            for p in pend:
                o.deps.append(p)
            self.ops[e].append(o)
        for t in self.tiles:
            if not t.persist:
                t.w = None
                t.r = {}

    def emit(self):
        nc = self.nc
        sems = {}

        def getsem(key):
            if key not in sems:
                sems[key] = self.es.enter_context(nc.semaphore("s%d" % len(sems)))
            return sems[key]

        for e in ENGS:
            cnt = {}
            for o in self.ops[e]:
                if o.dma is not None:
                    t = o.dma
                    t.ndma += 1
                    o.key = ("dma", id(t))
                    o.val = 16 * t.ndma
                elif o.needs:
                    k = (e, o.epoch)
                    cnt[k] = cnt.get(k, 0) + 1
                    o.key = k
                    o.val = cnt[k]
        for e in ENGS:
            for o in self.ops[e]:
                if o.needs:
                    getsem(o.key)
        import os
        if os.environ.get("MK_DEBUG"):
            mx = {}
            for e in ENGS:
                for o in self.ops[e]:
                    if o.key is not None:
                        mx[o.key] = max(mx.get(o.key, 0), o.val)
            print("NSEMS", len(sems), "MAXVALS", sorted([(str(k)[:30], v) for k, v in mx.items()], key=lambda kv: -kv[1])[:12])
            print("NOPS", {e: len(self.ops[e]) for e in ENGS})
        block = self.es.enter_context(nc.Block())

        def run(eng_name, eng):
            waited = {}
            for o in self.ops[eng_name]:
                need = {}
                for p in o.deps:
                    if waited.get(p.key, 0) < p.val:
                        if need.get(p.key, 0) < p.val:
                            need[p.key] = p.val
                for k, v in need.items():
                    eng.wait_ge(sems[k], v)
                    waited[k] = v
                if o.fn is None:
                    continue
                ins = o.fn(eng)
                if o.dma is not None:
                    ins.then_inc(sems[o.key], 16)
                elif o.needs:
                    ins.then_inc(sems[o.key], 1)

        @block.tensor
        def _(e):
            run("pe", e)

        @block.scalar
        def _(e):
            run("act", e)

        @block.vector
        def _(e):
            run("dve", e)

        @block.gpsimd
        def _(e):
            run("pool", e)

        @block.sync
        def _(e):
            run("sp", e)


def build_program(L, NSEQ, DEPTH, cfg=None):
    cfg = cfg or {}
    NT = L // 512
    NB = L // 128
    nc = bass.Bass("TRN2", target_bir_lowering=False)
    dr = {}

    def din(name, shape, dt=F32):
        dr[name] = nc.dram_tensor(name, list(shape), dt, kind="ExternalInput").ap()
        return dr[name]

    x_d = din("x", [NSEQ, L, D])
    pos_d = din("positions", [NSEQ, L], I32)
    cst_d = din("cst", [128, 648])
    pcols_d = din("pcols", [DEPTH, NPC, 128])
    prow_d = din("prow", [DEPTH, NPR])
    f1gu = din("ffn1_w_gu", [DEPTH, D, 2 * DFF])
    f1dn = din("ffn1_w_down", [DEPTH, DFF, D])
    f2gu = din("ffn2_w_gu", [DEPTH, D, 2 * DFF])
    f2dn = din("ffn2_w_down", [DEPTH, DFF, D])
    win_d = din("w_in", [DEPTH, D, INTOT])
    ws_d = din("gmlp_w_s", [DEPTH, 4, 128, 128])
    bs_d = din("gmlp_b_s", [DEPTH, 512])
    vnw_d = din("gmlp_v_norm", [DEPTH, 512])
    wqb_d = din("mla_w_qb", [DEPTH, 384, 768])
    wkvb_d = din("mla_w_kvb", [DEPTH, 128, 1024])
    wbr_d = din("w_branch", [DEPTH, 4, 512, D])
    wout_d = din("w_out", [DEPTH, D, D])
    out_d = nc.dram_tensor("out", [NSEQ, L, D], F32, kind="ExternalOutput").ap()

    es = ExitStack()
    S = Sched(nc, es)

    xT = S.sb("xT", [128, NCH, L], F32)
    xT_t = [S.T("xT%d" % i) for i in range(NT)]
    NRING = 6
    ring = [S.sb("ring%d" % i, [128, 4096], BF16) for i in range(NRING)]
    ring_t = [S.T("ring%d" % i, True) for i in range(NRING)]
    ring_pos = [0]
    cst = S.sb("cst", [128, 648], F32)
    cst_t = S.T("cst", True)
    ident = cst[:, 0:128]
    tri = cst[:, 128:256]
    maskneg = cst[:, 256:384]
    ones32 = cst[:, 384:512]
    triS = cst[:, 520:648]
    cbf = S.sb("cbf", [128, 384], BF16)
    cbf_t = S.T("cbf", True)
    ones16 = cbf[:, 0:128]
    tri16 = cbf[:, 128:256]
    ident16 = cbf[:, 256:384]
    prow = S.sb("prow", [128, NPR], F32)
    prow_t = S.T("prow", True)
    expA = S.sb("expA", [128, 8], F32)
    expA_t = S.T("expA", True)
    pc = S.sb("pc", [128, NPC], F32)
    pc_t = S.T("pc", True)
    pcst = S.sb("pcst", [NPC, 128], F32)
    pcst_t = S.T("pcst", True)
    ARENA = 90 * 1024
    arena = S.sb("arena", [128, ARENA // 4], F32)

    def carve(off, shape, dt):
        n = 1
        for s in shape[1:]:
            n *= s
        bpe = 4 if dt in (F32, I32) else 2
        assert off % 4 == 0 and off + n * bpe <= ARENA, (off, shape)
        v = arena[0:shape[0], off // 4: off // 4 + (n * bpe) // 4]
        if dt != F32:
            v = v.bitcast(dt)
        if len(shape) == 3:
            v = v.rearrange("p (a b) -> p a b", b=shape[2])
        elif len(shape) == 4:
            v = v.rearrange("p (a b c) -> p a b c", b=shape[2], c=shape[3])
        return v

    psb = [es.enter_context(nc.psum_tensor("ps%d" % i, [128, 512], F32)) for i in range(8)]
    psb_t = [S.T("ps%d" % i) for i in range(8)]
    ps_pos = [0]

    ps_n = [8]

    def ps():
        i = ps_pos[0] % ps_n[0]
        ps_pos[0] += 1
        return psb[i], psb_t[i]

    def ring_next():
        i = ring_pos[0] % NRING
        ring_pos[0] += 1
        return ring[i], ring_t[i]

    def fsz(ap):
        n = 1
        for d_ in ap.shape[1:]:
            n *= d_
        return n

    def mm(out, lhsT, rhs, start, stop, reads, writes):
        S.op("pe", lambda e: e.matmul(out, lhsT, rhs, start=start, stop=stop), reads, writes)

    def tr(out, in_, idn, reads, writes):
        S.op("pe", lambda e: e.transpose(out, in_, idn), reads, writes)

    def act(out, in_, func, reads, writes, bias=None, scale=None):
        kw = {}
        if bias is not None:
            kw["bias"] = bias
        if scale is not None:
            kw["scale"] = scale
        S.op("act", lambda e: e.activation(out=out, in_=in_, func=func, **kw), reads, writes, fs=fsz(out))

    def tt(out, in0, in1, op, reads, writes, eng="dve"):
        S.op(eng, lambda e: e.tensor_tensor(out=out, in0=in0, in1=in1, op=op), reads, writes, fs=fsz(out))

    def ts(out, in0, s1, s2, op0, op1, reads, writes, eng="dve"):
        if op1 is None:
            S.op(eng, lambda e: e.tensor_scalar(out=out, in0=in0, scalar1=s1, scalar2=None, op0=op0), reads, writes, fs=fsz(out))
        else:
            S.op(eng, lambda e: e.tensor_scalar(out=out, in0=in0, scalar1=s1, scalar2=s2, op0=op0, op1=op1), reads, writes, fs=fsz(out))

    def stt(out, in0, scalar, in1, op0, op1, reads, writes, eng="dve"):
        S.op(eng, lambda e: e.scalar_tensor_tensor(out=out, in0=in0, scalar=scalar, in1=in1, op0=op0, op1=op1), reads, writes, fs=fsz(out))

    def cp(out, in_, reads, writes, eng="dve"):
        S.op(eng, lambda e: e.tensor_copy(out=out, in_=in_), reads, writes, fs=fsz(out))

    def dma(eng, out, in_, tile, reads=(), writes=()):
        S.op(eng, lambda e: e.dma_start(out=out, in_=in_), reads, writes, dma_tile=tile)

    def wload(view_out, src, page_t):
        dma("pool", view_out, src, page_t, writes=[page_t])

    dma("sp", cst[:], cst_d, cst_t, writes=[cst_t])
    cp(ones16, ones32, [cst_t], [cbf_t])
    cp(tri16, tri, [cst_t], [cbf_t])
    cp(ident16, ident, [cst_t], [cbf_t])

    def load_layer_params(l):
        dma("sp", pcst[:], pcols_d[l], pcst_t, writes=[pcst_t])
        p_, pt_ = ps()
        tr(p_[:, 0:NPC], pcst[:], ident[0:NPC, 0:NPC], [pcst_t, cst_t], [pt_])
        cp(pc[:], p_[:, 0:NPC], [pt_], [pc_t])
        dma("sp", prow[:], prow_d[l:l + 1, :].partition_broadcast(128), prow_t, writes=[prow_t])
        act(expA[:], prow[:, 8:16], AF.Exp, [prow_t], [expA_t])

    def load_x(s):
        stg = [carve(i * 4096, [128, 1024], F32) for i in range(2)]
        stg_t = [S.PT("stg%d" % i) for i in range(2)]
        for b in range(NB):
            st, st_t = stg[b % 2], stg_t[b % 2]
            dma("sp", st, x_d[s, b * 128:(b + 1) * 128, :], st_t, writes=[st_t])
            for half in range(2):
                p_, pt_ = ps()
                for c4 in range(4):
                    c = half * 4 + c4
                    tr(p_[:, c4 * 128:(c4 + 1) * 128], st[:, c * 128:(c + 1) * 128], ident, [st_t, cst_t], [pt_])
                S.op("act", (lambda e, p_=p_, half=half, b=b: e.activation(
                    out=xT[:, half * 4:half * 4 + 4, b * 128:(b + 1) * 128],
                    in_=p_[:].rearrange("p (a b) -> p a b", b=128), func=AF.Copy)),
                    [pt_], [xT_t[b // 4]])

    def store_x(s):
        stg = [carve(i * 4096, [128, 1024], F32) for i in range(2)]
        stg_t = [S.PT("ostg%d" % i) for i in range(2)]
        for b in range(NB):
            st, st_t = stg[b % 2], stg_t[b % 2]
            for half in range(2):
                p_, pt_ = ps()
                for c4 in range(4):
                    c = half * 4 + c4
                    tr(p_[:, c4 * 128:(c4 + 1) * 128], xT[:, c, b * 128:(b + 1) * 128], ident, [xT_t[b // 4], cst_t], [pt_])
                act(st[:, half * 512:(half + 1) * 512], p_[:], AF.Copy, [pt_], [st_t])
            dma("sp", out_d[s, b * 128:(b + 1) * 128, :], st, st_t, reads=[st_t])

    def rsqrt_from_ss(out, ss, inv_n, reads, out_t):
        ts(out, ss, inv_n, EPS, ALU.mult, ALU.add, reads, [out_t])
        act(out, out, AF.Sqrt, [out_t], [out_t])
        S.op("dve", lambda e: e.reciprocal(out=out, in_=out), [out_t], [out_t], fs=fsz(out))

    def rmsnorm_tile(tt_i, wcol0, hT, hT_tt, sq, sq_t, rstd, rstd_t):
        tok = slice(tt_i * 512, (tt_i + 1) * 512)
        p_, pt_ = ps()
        for c in range(NCH):
            q, q_t = sq[c % len(sq)], sq_t[c % len(sq)]
            act(q, xT[:, c, tok], AF.Square, [xT_t[tt_i]], [q_t])
            mm(p_[:], ones16, q, c == 0, c == NCH - 1, [q_t, cbf_t], [pt_])
        if isinstance(rstd, list):
            rstd, rstd_t = rstd[tt_i % len(rstd)], rstd_t[tt_i % len(rstd_t)]
        rsqrt_from_ss(rstd, p_[:], 1.0 / D, [pt_], rstd_t)
        for c in range(NCH):
            stt(hT[:, c, tok], xT[:, c, tok], pc[:, wcol0 + c:wcol0 + c + 1], rstd, ALU.mult, ALU.mult,
                [xT_t[tt_i], pc_t, rstd_t], [hT_tt[tt_i]])

    FF_PARTS = [(0, 8), (8, 15), (15, 22)]

    def ffn(l, wgu_d, wdn_d, normcol):
        S.barrier()
        off = 0
        hT = carve(off, [128, NCH, L], BF16); off += NCH * L * 2
        aT = carve(off, [128, 8, L], BF16); off += 8 * L * 2
        sq = [carve(off + i * 1024, [128, 512], BF16) for i in range(3)]; off += 3 * 1024
        sg = [carve(off + i * 2048, [128, 512], F32) for i in range(3)]; off += 3 * 2048
        rstd = [carve(off + i * 2048, [128, 512], F32) for i in range(2)]; off += 4096
        hT_tt = [S.T("hT%d" % i) for i in range(NT)]
        aT_tt = [S.T("aT%d" % i) for i in range(NT)]
        sq_t = [S.T("sq%d" % i) for i in range(3)]
        sg_t = [S.T("sg%d" % i) for i in range(3)]
        rstd_t = [S.T("rstd%d" % i) for i in range(2)]
        for t_i in range(NT):
            rmsnorm_tile(t_i, normcol, hT, hT_tt, sq, sq_t, rstd, rstd_t)
        wgu = wgu_d[l].rearrange("(kc p) n -> p kc n", p=128)
        wdn = wdn_d[l].rearrange("(j p) n -> p j n", p=128)
        sgi = 0
        for (j0, j1) in FF_PARTS:
            j = j0
            while j < j1:
                nb = min(4, j1 - j)
                pg, pg_t = ring_next()
                pu, pu_t = ring_next()
                wg_v = pg[:, 0:8 * nb * 128].rearrange("p (a b) -> p a b", b=nb * 128)
                wu_v = pu[:, 0:8 * nb * 128].rearrange("p (a b) -> p a b", b=nb * 128)
                wload(wg_v, wgu[:, :, j * 128:(j + nb) * 128], pg_t)
                wload(wu_v, wgu[:, :, DFF + j * 128:DFF + (j + nb) * 128], pu_t)
                for jj in range(nb):
                    for t_i in range(NT):
                        tok = slice(t_i * 512, (t_i + 1) * 512)
                        g_, gt_ = ps()
                        u_, ut_ = ps()
                        for k in range(NCH):
                            mm(g_[:], wg_v[:, k, jj * 128:(jj + 1) * 128], hT[:, k, tok], k == 0, k == NCH - 1,
                               [pg_t, hT_tt[t_i]], [gt_])
                        for k in range(NCH):
                            mm(u_[:], wu_v[:, k, jj * 128:(jj + 1) * 128], hT[:, k, tok], k == 0, k == NCH - 1,
                               [pu_t, hT_tt[t_i]], [ut_])
                        s_, st_ = sg[sgi % 3], sg_t[sgi % 3]
                        sgi += 1
                        act(s_, g_[:], AF.Silu, [gt_], [st_])
                        tt(aT[:, j + jj - j0, tok], u_[:], s_, ALU.mult, [ut_, st_], [aT_tt[t_i]])
                j += nb
            nj = j1 - j0
            pages = []
            j = 0
            while j < nj:
                nb = min(4, nj - j)
                pd, pd_t = ring_next()
                wd_v = pd[:, 0:nb * 1024].rearrange("p (a b) -> p a b", b=1024)
                wload(wd_v, wdn[:, j0 + j:j0 + j + nb, :], pd_t)
                for jj in range(nb):
                    pages.append((wd_v, jj, pd_t))
                j += nb
            for oc in range(NCH):
                for t_i in range(NT):
                    tok = slice(t_i * 512, (t_i + 1) * 512)
                    d_, dt_ = ps()
                    for jx in range(nj):
                        wd_v, jj, pd_t = pages[jx]
                        mm(d_[:], wd_v[:, jj, oc * 128:(oc + 1) * 128], aT[:, jx, tok], jx == 0, jx == nj - 1,
                           [pd_t, aT_tt[t_i]], [dt_])
                    stt(xT[:, oc, tok], d_[:], 0.5, xT[:, oc, tok], ALU.mult, ALU.add, [dt_, xT_t[t_i]], [xT_t[t_i]])


    WS0 = 48 * 1024

    class WSAlloc:
        def __init__(self):
            self.off = WS0

        def get(self, shape, dt):
            n = 1
            for d_ in shape[1:]:
                n *= d_
            nb = n * (4 if dt in (F32, I32) else 2)
            nb = (nb + 3) // 4 * 4
            v = carve(self.off, shape, dt)
            self.off += nb
            return v

    def bc_mid(ap2d, n):
        return ap2d.unsqueeze(1).to_broadcast([ap2d.shape[0], n, ap2d.shape[1]])

    def bc_last(ap2d, n):
        return ap2d.unsqueeze(2).to_broadcast([ap2d.shape[0], ap2d.shape[1], n])

    def v3(ap2d, b):
        return ap2d.rearrange("p (a b) -> p a b", b=b)

    def mixer_layer(l, s):
        S.barrier()
        hT = carve(0, [128, NCH, L], BF16)
        yT = carve(32 * 1024, [128, 4, L], BF16)
        hT_tt = [S.T("mhT%d" % i) for i in range(NT)]
        yT_tt = [S.T("yT%d" % i) for i in range(NT)]
        win = win_d[l].rearrange("(kc p) n -> p kc n", p=128)

        def wpage(c0, ncols):
            pg, pg_t = ring_next()
            v = pg[:, 0:8 * ncols].rearrange("p (a b) -> p a b", b=ncols)
            wload(v, win[:, :, c0:c0 + ncols], pg_t)
            return v, pg_t

        wsn = WSAlloc()
        sq = [wsn.get([128, 512], BF16) for _ in range(3)]
        sq_t = [S.T("msq%d" % i) for i in range(3)]
        rstd = [wsn.get([128, 512], F32) for _ in range(2)]
        rstd_t = [S.T("mrstd%d" % i) for i in range(2)]
        for t_i in range(NT):
            rmsnorm_tile(t_i, 8, hT, hT_tt, sq, sq_t, rstd, rstd_t)

        def gating(i):
            S.barrier()
            ws = WSAlloc()
            gated = ws.get([128, 8, 512], BF16)
            gated_t = S.T("gated")
            sig = [ws.get([128, 512], F32) for _ in range(2)]
            sig_t = [S.T("sig%d" % k) for k in range(2)]
            pb, pb_t = ring_next()
            wbr = pb[:, 0:4096].rearrange("p (a b) -> p a b", b=1024)
            wload(wbr, wbr_d[l, i].rearrange("(kc p) n -> p kc n", p=128), pb_t)
            wg = [wpage(O_G + i * 1024 + hh * 512, 512) for hh in range(2)]
            wo = []
            wov = wout_d[l].rearrange("(kc p) n -> p kc n", p=128)
            for hh in range(2):
                pg, pg_t = ring_next()
                v = pg[:, 0:4096].rearrange("p (a b) -> p a b", b=512)
                wload(v, wov[:, :, hh * 512:(hh + 1) * 512], pg_t)
                wo.append((v, pg_t))
            si = 0
            for t_i in range(NT):
                tok = slice(t_i * 512, (t_i + 1) * 512)
                for oc in range(8):
                    per_, pert_ = ps()
                    for kc in range(4):
                        mm(per_[:], wbr[:, kc, oc * 128:(oc + 1) * 128], yT[:, kc, tok], kc == 0, kc == 3,
                           [pb_t, yT_tt[t_i]], [pert_])
                    g_, gt_ = ps()
                    wgv, wg_t = wg[oc // 4]
                    for k in range(8):
                        mm(g_[:], wgv[:, k, (oc % 4) * 128:(oc % 4 + 1) * 128], hT[:, k, tok], k == 0, k == 7,
                           [wg_t, hT_tt[t_i]], [gt_])
                    sg_, sgt_ = sig[si % 2], sig_t[si % 2]
                    si += 1
                    act(sg_, g_[:], AF.Sigmoid, [gt_], [sgt_])
                    tt(gated[:, oc, :], per_[:], sg_, ALU.mult, [pert_, sgt_], [gated_t])
                for oc2 in range(8):
                    o_, ot_ = ps()
                    wov_, wo_t = wo[oc2 // 4]
                    for oc in range(8):
                        mm(o_[:], wov_[:, oc, (oc2 % 4) * 128:(oc2 % 4 + 1) * 128], gated[:, oc, :], oc == 0, oc == 7,
                           [wo_t, gated_t], [ot_])
                    tt(xT[:, oc2, tok], o_[:], xT[:, oc2, tok], ALU.add, [ot_, xT_t[t_i]], [xT_t[t_i]])
            S.barrier()

        def branch_d():
            S.barrier()
            ws = WSAlloc()
            tbuf = ws.get([128, 516], F32)
            tbuf_t = S.T("tbuf")
            acc = ws.get([128, 512], F32)
            acc_t = S.T("dacc")
            cgs = ws.get([128, 512], F32)
            cgs_t = S.T("cgs")
            wb = wpage(O_SC, 512)
            wc = wpage(O_SC + 512, 512)
            wx = wpage(O_SC + 1024, 512)
            for c in range(4):
                for t_i in range(NT):
                    tok = slice(t_i * 512, (t_i + 1) * 512)
                    pss = []
                    for (wv, w_t) in (wb, wc, wx):
                        p_, pt_ = ps()
                        for k in range(8):
                            mm(p_[:], wv[:, k, c * 128:(c + 1) * 128], hT[:, k, tok], k == 0, k == 7,
                               [w_t, hT_tt[t_i]], [pt_])
                        pss.append((p_, pt_))
                    (b_, bt_), (c_, ct_), (x_, xt_) = pss
                    act(cgs, c_[:], AF.Copy, [ct_], [cgs_t])
                    if t_i == 0:
                        S.op("dve", lambda e: e.memset(tbuf[:, 0:2], 0.0), [], [tbuf_t])
                    else:
                        cp(tbuf[:, 0:2], tbuf[:, 512:514], [tbuf_t], [tbuf_t])
                    tt(tbuf[:, 2:514], x_[:], cgs, ALU.mult, [xt_, cgs_t], [tbuf_t])
                    ts(acc, tbuf[:, 0:512], pc[:, 78 + c:79 + c], None, ALU.mult, None, [tbuf_t, pc_t], [acc_t])
                    for k in (1, 2):
                        stt(acc, tbuf[:, k:k + 512], pc[:, 78 + k * 4 + c:79 + k * 4 + c], acc, ALU.mult, ALU.add,
                            [tbuf_t, pc_t, acc_t], [acc_t])
                    tt(yT[:, c, tok], b_[:], acc, ALU.mult, [bt_, acc_t], [yT_tt[t_i]])

        def branch_b():
            S.barrier()
            ws = WSAlloc()
            wstg = ws.get([128, 4, 128], F32)
            wstg_t = S.PT("wstg")
            wsT = ws.get([128, 4, 128], BF16)
            wsT_t = S.T("wsT")
            bsrow = ws.get([1, 512], BF16)
            bsrow_t = S.T("bsrow")
            vnw = ws.get([128, 512], F32)
            vnw_t = S.PT("vnw")
            dma("sp", vnw, vnw_d[l:l + 1, :].partition_broadcast(128), vnw_t, writes=[vnw_t])
            vgs = [ws.get([128, 512], F32) for _ in range(2)]
            vg_ts = [S.T("vg%d" % i) for i in range(2)]
            vsqs = [ws.get([128, 512], F32) for _ in range(2)]
            vsq_ts = [S.T("vsq%d" % i) for i in range(2)]
            vsss = [ws.get([128, 2], F32) for _ in range(2)]
            vss_ts = [S.T("vss%d" % i) for i in range(2)]
            vns = [ws.get([128, 512], BF16) for _ in range(2)]
            vn_ts = [S.T("vn%d" % i) for i in range(2)]
            dma("sp", wstg, ws_d[l].rearrange("g t s -> t g s"), wstg_t, writes=[wstg_t])
            bsf = ws.get([1, 512], F32)
            bsf_t = S.PT("bsf")
            dma("sp", bsf, bs_d[l:l + 1, :], bsf_t, writes=[bsf_t])
            cp(bsrow, bsf, [bsf_t], [bsrow_t])
            p_, pt_ = ps()
            for g in range(4):
                tr(p_[:, g * 128:(g + 1) * 128], wstg[:, g, :], ident, [wstg_t, cst_t], [pt_])
            tt(wsT, v3(p_[:], 128), bc_mid(tri, 4), ALU.mult, [pt_, cst_t], [wsT_t])
            wu = wpage(O_UV, 512)
            wv = wpage(O_UV + 512, 512)
            def upath(t_i):
                tok = slice(t_i * 512, (t_i + 1) * 512)
                for c in range(4):
                    u_, ut_ = ps()
                    for k in range(8):
                        mm(u_[:], wu[0][:, k, c * 128:(c + 1) * 128], hT[:, k, tok], k == 0, k == 7,
                           [wu[1], hT_tt[t_i]], [ut_])
                    act(yT[:, c, tok], u_[:], AF.Gelu, [ut_], [yT_tt[t_i]])

            def s1(gb):
                t_i, b = gb // 4, gb % 4
                tb = slice(t_i * 512 + b * 128, t_i * 512 + (b + 1) * 128)
                bi = gb % 2
                vg, vg_t, vsq, vsq_t = vgs[bi], vg_ts[bi], vsqs[bi], vsq_ts[bi]
                vss, vss_t, vn, vn_t = vsss[bi], vss_ts[bi], vns[bi], vn_ts[bi]
                v_, vt_ = ps()
                for k in range(8):
                    mm(v_[:], hT[:, k, tb], wv[0][:, k, :], k == 0, k == 7, [wv[1], hT_tt[t_i]], [vt_])
                act(vg, v_[:], AF.Gelu, [vt_], [vg_t])
                tt(vsq, vg, vg, ALU.mult, [vg_t], [vsq_t])
                S.op("dve", (lambda e, vss=vss, vsq=vsq: e.reduce_sum(out=vss[:, 0:1], in_=vsq, axis=mybir.AxisListType.X)), [vsq_t], [vss_t])
                rsqrt_from_ss(vss[:, 1:2], vss[:, 0:1], 1.0 / 512, [vss_t], vss_t)
                stt(vn, vg, vss[:, 1:2], vnw, ALU.mult, ALU.mult, [vg_t, vss_t, vnw_t], [vn_t])
                return (t_i, tb, vn, vn_t)

            def s2(item):
                t_i, tb, vn, vn_t = item
                sv_, svt_ = ps()
                for g in range(4):
                    mm(sv_[:, g * 128:(g + 1) * 128], vn[:, g * 128:(g + 1) * 128], wsT[:, g, :], True, False,
                       [vn_t, wsT_t], [svt_])
                    mm(sv_[:, g * 128:(g + 1) * 128], ones16[0:1, :], bsrow[0:1, g * 128:(g + 1) * 128], False, True,
                       [cbf_t, bsrow_t], [svt_])
                tt(yT[:, :, tb], yT[:, :, tb], v3(sv_[:], 128), ALU.mult, [svt_, yT_tt[t_i]], [yT_tt[t_i]])

            pend_b = []
            for gb in range(NT * 4):
                if gb % 4 == 0:
                    upath(gb // 4)
                pend_b.append(s1(gb))
                if len(pend_b) > 1:
                    s2(pend_b.pop(0))
            while pend_b:
                s2(pend_b.pop(0))

        def branch_a():
            S.barrier()
            ws = WSAlloc()
            xbcT = ws.get([128, 8, 512], BF16); xbcT_t = S.T("xbcT")
            rawh = ws.get([128, 516], F32); rawh_t = S.T("rawh")
            acc = ws.get([128, 512], F32); acc_t = S.T("aacc")
            halo = ws.get([128, 8, 4], F32); halo_t = S.T("halo")
            dtr = ws.get([128, 32], F32); dtr_t = S.T("dtr")
            dtv = ws.get([128, 32], F32); dtv_t = S.T("dtv")
            av = ws.get([128, 32], F32); av_t = S.T("av")
            sm = ws.get([128, 128], F32); sm_t = S.T("sm")
            cbm = ws.get([128, 2, 128], F32); cbm_t = S.T("cbm")
            pgA, pgA_t = ring_next()
            pgB, pgB_t = ring_next()
            fA = pgA[:].bitcast(F32)
            fB = pgB[:].bitcast(F32)
            Rbs = [ws.get([128, 8, 128], F32), fA[:, 0:1024].rearrange("p (a b) -> p a b", b=128)]
            W1s = [ws.get([128, 8, 128], F32), fA[:, 1024:2048].rearrange("p (a b) -> p a b", b=128)]
            MTs = [ws.get([128, 8, 128], BF16), pgB[:, 0:1024].rearrange("p (a b) -> p a b", b=128)]
            xdts = [ws.get([128, 8, 64], BF16), pgB[:, 1024:1536].rearrange("p (a b) -> p a b", b=64)]
            xws = [ws.get([128, 8, 64], BF16), pgB[:, 1536:2048].rearrange("p (a b) -> p a b", b=64)]
            Btoks = [ws.get([128, 256], BF16), pgB[:, 2048:2304]]
            ysbs = [ws.get([128, 512], F32), fB[:, 1280:1792]]
            Rb_ts = [S.T("Rb%d" % i) for i in range(2)]
            W1_ts = [S.T("W1%d" % i) for i in range(2)]
            MT_ts = [S.T("MT%d" % i) for i in range(2)]
            xdt_ts = [S.T("xdt%d" % i) for i in range(2)]
            xw_ts = [S.T("xw%d" % i) for i in range(2)]
            Btok_ts = [S.T("Btok%d" % i) for i in range(2)]
            ysb_ts = [S.T("ysb%d" % i) for i in range(2)]
            MT, MT_t = MTs[0], MT_ts[0]
            pg_first = [True]
            st32 = ws.get([128, 8, 64], F32); st32_t = S.T("st32")
            st16 = ws.get([128, 512], BF16); st16_t = S.T("st16")
            ysb = ws.get([128, 512], F32); ysb_t = S.T("ysb")
            yraw = ws.get([128, 4, 512], F32); yraw_t = S.T("yraw")
            sz = acc; sz_t = acc_t
            gsq = MT.rearrange("p a b -> p (a b)")[:, 0:512]; gsq_t = MT_t
            grs = rawh[:, 0:512]; grs_t = rawh_t
            wz = wpage(O_Z, 512)
            wx0 = wpage(O_XBC, 512)
            wx1 = wpage(O_XBC + 512, 512)
            wdt = wpage(O_DT, 8)
            wxb = (wx0, wx1)
            acs4, eacs4, dout4, eatot4 = sm[:, 0:32], sm[:, 32:64], sm[:, 64:96], sm[:, 96:128]
            for t_i in range(NT):
                tok = slice(t_i * 512, (t_i + 1) * 512)
                for f in range(8):
                    p_, pt_ = ps()
                    wv, w_t = wxb[f // 4]
                    for k in range(8):
                        mm(p_[:], wv[:, k, (f % 4) * 128:(f % 4 + 1) * 128], hT[:, k, tok], k == 0, k == 7,
                           [w_t, hT_tt[t_i]], [pt_])
                    if t_i == 0:
                        S.op("dve", lambda e: e.memset(rawh[:, 0:3], 0.0), [], [rawh_t])
                    else:
                        cp(rawh[:, 0:3], halo[:, f, 0:3], [halo_t], [rawh_t])
                    act(rawh[:, 3:515], p_[:], AF.Copy, [pt_], [rawh_t])
                    cp(halo[:, f, 0:3], rawh[:, 512:515], [rawh_t], [halo_t])
                    ts(acc, rawh[:, 0:512], pc[:, 24 + f:25 + f], None, ALU.mult, None, [rawh_t, pc_t], [acc_t])
                    for k in (1, 2, 3):
                        stt(acc, rawh[:, k:k + 512], pc[:, 24 + k * 8 + f:25 + k * 8 + f], acc, ALU.mult, ALU.add,
                            [rawh_t, pc_t, acc_t], [acc_t])
                    act(xbcT[:, f, :], acc, AF.Silu, [acc_t, pc_t], [xbcT_t], bias=pc[:, 56 + f:57 + f])
                d_, dt_ = ps()
                for c in range(4):
                    for k in range(8):
                        mm(d_[:, c * 8:(c + 1) * 8], hT[:, k, t_i * 512 + c * 128:t_i * 512 + (c + 1) * 128], wdt[0][:, k, :],
                           k == 0, k == 7, [wdt[1], hT_tt[t_i]], [dt_])
                tt(v3(dtr, 8), v3(d_[:, 0:32], 8), bc_mid(prow[:, 0:8], 4), ALU.add, [dt_, prow_t], [dtr_t])
                act(dtr, dtr, AF.Exp, [dtr_t], [dtr_t])
                act(dtv, dtr, AF.Ln, [dtr_t], [dtv_t], bias=1.0)
                stt(v3(av, 8), v3(dtv, 8), -1.0, bc_mid(expA[:], 4), ALU.mult, ALU.mult, [dtv_t, expA_t], [av_t])
                cu_, cut_ = ps()
                mm(cu_[:, 0:32], tri, av, True, True, [cst_t, av_t], [cut_])
                mm(cu_[:, 32:64], ones32, av, True, True, [cst_t, av_t], [cut_])
                act(acs4, cu_[:, 0:32], AF.Copy, [cut_], [sm_t])
                act(eacs4, cu_[:, 0:32], AF.Exp, [cut_], [sm_t])
                act(eatot4, cu_[:, 32:64], AF.Exp, [cut_], [sm_t])
                tt(dout4, cu_[:, 32:64], acs4, ALU.subtract, [cut_, sm_t], [sm_t])
                act(dout4, dout4, AF.Exp, [sm_t], [sm_t])
                def chunk_ctx(c):
                        gc = t_i * 4 + c
                        ct = slice(c * 128, (c + 1) * 128)
                        a_c = av[:, c * 8:(c + 1) * 8]
                        bi = c % 2
                        Rb, Rb_t, W1, W1_t = Rbs[bi], Rb_ts[bi], W1s[bi], W1_ts[bi]
                        MTc, MTc_t, xdt, xdt_t = MTs[bi], MT_ts[bi], xdts[bi], xdt_ts[bi]
                        xw, xw_t, Btok, Btok_t, ysb, ysb_t = xws[bi], xw_ts[bi], Btoks[bi], Btok_ts[bi], ysbs[bi], ysb_ts[bi]
                        pgr = [pgA_t, pgB_t] if bi == 1 else []
                        pgw = [pgA_t, pgB_t] if (bi == 1 and pg_first[0]) else []
                        if bi == 1:
                            pg_first[0] = False
                        acs = acs4[:, c * 8:(c + 1) * 8]
                        eacs = eacs4[:, c * 8:(c + 1) * 8]
                        dout = dout4[:, c * 8:(c + 1) * 8]
                        eatot = eatot4[:, c * 8:(c + 1) * 8]

                        return locals()

                def p1(c):
                    L_ = chunk_ctx(c)
                    (gc, ct, a_c, bi, Rb, Rb_t, W1, W1_t, MTc, MTc_t, xdt, xdt_t, xw, xw_t, Btok, Btok_t, ysb, ysb_t, pgr, pgw, acs, eacs, dout, eatot) = [L_[k] for k in ('gc','ct','a_c','bi','Rb','Rb_t','W1','W1_t','MTc','MTc_t','xdt','xdt_t','xw','xw_t','Btok','Btok_t','ysb','ysb_t','pgr','pgw','acs','eacs','dout','eatot')]
                    tt(Rb, bc_mid(tri, 8), bc_last(a_c, 128), ALU.mult, [cst_t, av_t] + pgr, [Rb_t] + pgw)
                    cb_, cbt_ = ps()
                    for g in range(2):
                        mm(cb_[:, g * 128:(g + 1) * 128], xbcT[:, 4 + g, ct], xbcT[:, 6 + g, ct], True, True,
                           [xbcT_t], [cbt_])
                    tt(cbm, v3(cb_[:, 0:256], 128), bc_mid(tri, 2), ALU.mult, [cbt_, cst_t], [cbm_t])
                    for g in range(2):
                        bc_, bct_ = ps()
                        mm(bc_[:], triS, Rb[:, g * 4:(g + 1) * 4, :].rearrange("p a b -> p (a b)"), True, True,
                           [cst_t, Rb_t] + pgr, [bct_])
                        act(W1[:, g * 4:(g + 1) * 4, :], v3(bc_[:], 128), AF.Exp, [bct_] + pgr, [W1_t])
                    for g in range(2):
                        tt(MTc[:, g * 4:(g + 1) * 4, :], W1[:, g * 4:(g + 1) * 4, :], bc_mid(cbm[:, g, :], 4),
                           ALU.mult, [W1_t, cbm_t] + pgr, [MTc_t])
                    xs_, xst_ = ps()
                    xs16 = xs_[:].bitcast(BF16)
                    for cc in range(4):
                        tr(xs16[:, cc * 128:(cc + 1) * 128], xbcT[:, cc, ct], ident16, [xbcT_t, cbf_t], [xst_])
                    tt(xdt, v3(xs16[:, 0:512], 64), bc_last(dtv[:, c * 8:(c + 1) * 8], 64), ALU.mult, [xst_, dtv_t] + pgr, [xdt_t])
                    tt(xw, xdt, bc_last(dout, 64), ALU.mult, [xdt_t, sm_t] + pgr, [xw_t])
                    b_, bt_ = ps()
                    b16 = b_[:].bitcast(BF16)
                    for g in range(2):
                        tr(b16[:, g * 128:(g + 1) * 128], xbcT[:, 4 + g, ct], ident16, [xbcT_t, cbf_t], [bt_])
                    act(Btok, b16[:, 0:256], AF.Copy, [bt_] + pgr, [Btok_t])
                    return L_

                def p2(L_):
                    (gc, ct, a_c, bi, Rb, Rb_t, W1, W1_t, MTc, MTc_t, xdt, xdt_t, xw, xw_t, Btok, Btok_t, ysb, ysb_t, pgr, pgw, acs, eacs, dout, eatot) = [L_[k] for k in ('gc','ct','a_c','bi','Rb','Rb_t','W1','W1_t','MTc','MTc_t','xdt','xdt_t','xw','xw_t','Btok','Btok_t','ysb','ysb_t','pgr','pgw','acs','eacs','dout','eatot')]
                    y_, yt_ = ps()
                    for h in range(8):
                        mm(y_[:, h * 64:(h + 1) * 64], MTc[:, h, :], xdt[:, h, :], True, True, [MTc_t, xdt_t] + pgr, [yt_])
                    if gc > 0:
                        yo_, yot_ = ps()
                        for g in range(2):
                            mm(yo_[:, g * 256:(g + 1) * 256], xbcT[:, 6 + g, ct], st16[:, g * 256:(g + 1) * 256], True, True,
                               [xbcT_t, st16_t], [yot_])
                        tt(v3(ysb, 64), v3(yo_[:], 64), bc_last(eacs, 64), ALU.mult, [yot_, sm_t] + pgr, [ysb_t])
                        tt(ysb, ysb, y_[:], ALU.add, [ysb_t, yt_] + pgr, [ysb_t])
                    else:
                        cp(ysb, y_[:], [yt_] + pgr, [ysb_t])
                    s_, st_ = ps()
                    for g in range(2):
                        mm(s_[:, g * 256:(g + 1) * 256], Btok[:, g * 128:(g + 1) * 128],
                           xw[:, g * 4:(g + 1) * 4, :].rearrange("p a b -> p (a b)"), True, True, [Btok_t, xw_t] + pgr, [st_])
                    if gc > 0:
                        tt(st32, st32, bc_last(eatot, 64), ALU.mult, [st32_t, sm_t], [st32_t])
                        tt(st32, st32, v3(s_[:], 64), ALU.add, [st32_t, st_], [st32_t])
                    else:
                        cp(st32, v3(s_[:], 64), [st_], [st32_t])
                    act(st16, st32.rearrange("p a b -> p (a b)"), AF.Copy, [st32_t], [st16_t])
                    yT_, yTt_ = ps()
                    for cc in range(4):
                        tr(yT_[:, cc * 128:(cc + 1) * 128], ysb[:, cc * 128:(cc + 1) * 128], ident, [ysb_t, cst_t] + pgr, [yTt_])
                    act(yraw[:, :, ct], v3(yT_[:], 128), AF.Copy, [yTt_], [yraw_t])

                pend_c = []
                for c in range(4):
                    pend_c.append(p1(c))
                    if len(pend_c) > 1:
                        p2(pend_c.pop(0))
                while pend_c:
                    p2(pend_c.pop(0))
                for cc in range(4):
                    stt(yraw[:, cc, :], xbcT[:, cc, :], pc[:, 96 + cc:97 + cc], yraw[:, cc, :], ALU.mult, ALU.add,
                        [xbcT_t, pc_t, yraw_t], [yraw_t])
                for cc in range(4):
                    z_, zt_ = ps()
                    for k in range(8):
                        mm(z_[:], wz[0][:, k, cc * 128:(cc + 1) * 128], hT[:, k, tok], k == 0, k == 7,
                           [wz[1], hT_tt[t_i]], [zt_])
                    act(sz, z_[:], AF.Silu, [zt_], [sz_t])
                    tt(yraw[:, cc, :], yraw[:, cc, :], sz, ALU.mult, [yraw_t, sz_t], [yraw_t])
                for g in range(2):
                    ss_, sst_ = ps()
                    for j, cc in enumerate((2 * g, 2 * g + 1)):
                        act(gsq, yraw[:, cc, :], AF.Square, [yraw_t], [gsq_t])
                        mm(ss_[:], ones16, gsq, j == 0, j == 1, [gsq_t, cbf_t], [sst_])
                    rsqrt_from_ss(grs, ss_[:], 1.0 / 256, [sst_], grs_t)
                    for cc in (2 * g, 2 * g + 1):
                        stt(yT[:, cc, tok], yraw[:, cc, :], pc[:, 64 + cc:65 + cc], grs, ALU.mult, ALU.mult,
                            [yraw_t, pc_t, grs_t], [yT_tt[t_i]])


        def branch_c():
            S.barrier()
            import math
            ws = WSAlloc()
            qnT = ws.get([128, 3, L], BF16); qnT_tt = [S.T("qnT%d" % i) for i in range(NT)]
            kvnT = ws.get([128, L], BF16); kvnT_tt = [S.T("kvnT%d" % i) for i in range(NT)]
            kper = ws.get([64, L], F32); kper_tt = [S.T("kper%d" % i) for i in range(NT)]
            sqkpe = ws.get([64, L], BF16); sqkpe_tt = [S.T("sqkpe%d" % i) for i in range(NT)]
            sqb = ws.get([128, 512], BF16); sqb_t = S.T("csq")
            rs = ws.get([128, 512], F32); rs_t = S.T("crs")
            t1 = ws.get([128, 512], F32); t1_t = S.PT("ct1")
            t2 = ws.get([128, 512], F32); t2_t = S.T("ct2")
            Qns = [ws.get([128, 512], BF16) for _ in range(2)]; Qn_ts = [S.T("Qn%d" % i) for i in range(2)]
            Qrs = [ws.get([64, 512], BF16) for _ in range(2)]; Qr_ts = [S.T("Qr%d" % i) for i in range(2)]
            pT = [ws.get([128, 512], BF16) for _ in range(3)]; pT_t = [S.T("pT%d" % i) for i in range(3)]
            pq, pq_t = ring_next()
            wql = pq[:, 0:8 * 384].rearrange("p (a b) -> p a b", b=384)
            wload(wql, win[:, :, O_QL:O_QL + 384], pq_t)
            pk, pk_t = ring_next()
            wkl = pk[:, 0:8 * 192].rearrange("p (a b) -> p a b", b=192)
            wload(wkl, win[:, :, O_KVL:O_KVL + 192], pk_t)
            pk2, pk2_t = ring_next()
            wks = pk2[:, 0:8 * 64].rearrange("p (a b) -> p a b", b=64)
            wks_b = pk2[:, 1024:1024 + 8 * 64].rearrange("p (a b) -> p a b", b=64)
            dma("pool", wks[:, :, 0:32], win[:, :, O_KPE + 32:O_KPE + 64], pk2_t, writes=[pk2_t])
            dma("pool", wks[:, :, 32:64], win[:, :, O_KPE:O_KPE + 32], pk2_t, writes=[pk2_t])
            pcs, pcs_t = ring_next()
            cs32 = pcs[:].bitcast(F32)
            assert L <= 1024 or True
            cos2 = None
            if 2 * L * 4 <= 8192:
                cos2 = cs32[0:64, 0:L]
                sin2 = cs32[0:64, L:2 * L]
                sin_t = pcs_t
            else:
                pcs2, pcs2_t = ring_next()
                cos2 = cs32[0:64, 0:L]
                sin2 = pcs2[:].bitcast(F32)[0:64, 0:L]
                sin_t = pcs2_t
            cos_t = pcs_t
            invf = cst[0:64, 512:513]
            sgn = cst[0:64, 513:514]
            TWO_PI = 2.0 * math.pi
            posi = t1[0:64, :].bitcast(I32)
            for t_i in range(NT):
                tok = slice(t_i * 512, (t_i + 1) * 512)
                a_, k_, m_ = t2[0:64, :], rs[0:64, :], t1[0:64, :]
                dma("sp", posi, pos_d[s:s + 1, tok].partition_broadcast(64), t1_t, writes=[t1_t])
                cp(a_, posi, [t1_t], [t2_t])
                ts(a_, a_, invf, None, ALU.mult, None, [t2_t, cst_t], [t2_t])
                for which, dst, dst_t in ((0, sin2, sin_t), (1, cos2, cos_t)):
                    if which == 1:
                        ts(a_, a_, math.pi / 2, None, ALU.add, None, [t2_t], [t2_t])
                    ts(k_, a_, 1.0 / TWO_PI, None, ALU.mult, None, [t2_t], [rs_t])
                    cp(posi, k_, [rs_t], [t1_t])
                    cp(k_, posi, [t1_t], [rs_t])
                    stt(k_, k_, -TWO_PI, a_, ALU.mult, ALU.add, [rs_t, t2_t], [rs_t])
                    ts(m_, k_, math.pi, None, ALU.is_gt, None, [rs_t], [t1_t])
                    stt(k_, m_, -TWO_PI, k_, ALU.mult, ALU.add, [t1_t, rs_t], [rs_t])
                    ts(m_, k_, -math.pi, None, ALU.is_lt, None, [rs_t], [t1_t])
                    stt(k_, m_, TWO_PI, k_, ALU.mult, ALU.add, [t1_t, rs_t], [rs_t])
                    ts(k_, k_, math.pi, -math.pi, ALU.min, ALU.max, [rs_t], [rs_t])
                    act(dst[:, tok], k_, AF.Sin, [rs_t], [dst_t])
                ts(sin2[:, tok], sin2[:, tok], sgn, None, ALU.mult, None, [sin_t, cst_t], [sin_t])
            for t_i in range(NT):
                tok = slice(t_i * 512, (t_i + 1) * 512)
                qps = []
                for kc in range(3):
                    p_, pt_ = ps()
                    for k in range(8):
                        mm(p_[:], wql[:, k, kc * 128:(kc + 1) * 128], hT[:, k, tok], k == 0, k == 7, [pq_t, hT_tt[t_i]], [pt_])
                    qps.append((p_, pt_))
                ss_, sst_ = ps()
                for kc in range(3):
                    act(sqb, qps[kc][0][:], AF.Square, [qps[kc][1]], [sqb_t])
                    mm(ss_[:], ones16, sqb, kc == 0, kc == 2, [sqb_t, cbf_t], [sst_])
                rsqrt_from_ss(rs, ss_[:], 1.0 / 384, [sst_], rs_t)
                for kc in range(3):
                    stt(qnT[:, kc, tok], qps[kc][0][:], pc[:, 68 + kc:69 + kc], rs, ALU.mult, ALU.mult,
                        [qps[kc][1], pc_t, rs_t], [qnT_tt[t_i]])
                p_, pt_ = ps()
                for k in range(8):
                    mm(p_[:], wkl[:, k, 0:128], hT[:, k, tok], k == 0, k == 7, [pk_t, hT_tt[t_i]], [pt_])
                act(sqb, p_[:], AF.Square, [pt_], [sqb_t])
                ss_, sst_ = ps()
                mm(ss_[:], ones16, sqb, True, True, [sqb_t, cbf_t], [sst_])
                rsqrt_from_ss(rs, ss_[:], 1.0 / 128, [sst_], rs_t)
                stt(kvnT[:, tok], p_[:], pc[:, 71:72], rs, ALU.mult, ALU.mult, [pt_, pc_t, rs_t], [kvnT_tt[t_i]])
                kp_, kpt_ = ps()
                for k in range(8):
                    mm(kp_[0:64, :], wkl[:, k, 128:192], hT[:, k, tok], k == 0, k == 7, [pk_t, hT_tt[t_i]], [kpt_])
                ks_, kst_ = ps()
                for k in range(8):
                    mm(ks_[0:64, :], wks[:, k, :], hT[:, k, tok], k == 0, k == 7, [pk2_t, hT_tt[t_i]], [kst_])
                act(sqkpe[:, tok], kp_[0:64, :], AF.Square, [kpt_], [sqkpe_tt[t_i]])
                stt(t1[0:64, :], kp_[0:64, :], pc[0:64, 76:77], cos2[:, tok], ALU.mult, ALU.mult, [kpt_, pc_t, cos_t], [t1_t])
                stt(t2[0:64, :], ks_[0:64, :], pc[0:64, 77:78], sin2[:, tok], ALU.mult, ALU.mult, [kst_, pc_t, sin_t], [t2_t])
                tt(kper[:, tok], t1[0:64, :], t2[0:64, :], ALU.add, [t1_t, t2_t], [kper_tt[t_i]])
            pw, pw_t = ring_next()
            wqb = pw[:, 0:3 * 768].rearrange("p (a b) -> p a b", b=768)
            wqs = pw[:, 3072:3072 + 3 * 256].rearrange("p (a b) -> p a b", b=256)
            wqv = wqb_d[l].rearrange("(kc p) n -> p kc n", p=128)
            wq4 = wqb_d[l].rearrange("(kc p) (h d) -> p kc h d", p=128, d=192)
            dma("pool", wqb, wqv, pw_t, writes=[pw_t])
            wqs4 = wqs.rearrange("p a (h d) -> p a h d", d=64)
            for kc in range(3):
                dma("pool", wqs4[:, kc, :, 0:32], wq4[:, kc, :, 160:192], pw_t, writes=[pw_t])
                dma("pool", wqs4[:, kc, :, 32:64], wq4[:, kc, :, 128:160], pw_t, writes=[pw_t])
            pv, pv_t = ring_next()
            wkvb = pv[:, 0:1024]
            wload(wkvb, wkvb_d[l], pv_t)
            scale = 192.0 ** -0.5
            pKn, pKn_t = ring_next()
            pKV, pKV_t = ring_next()
            Kn_tt = [S.T("Kn%d" % i) for i in range(NT)]
            KV_tt = [S.T("KV%d" % i) for i in range(NT)]
            Kn = pKn[:, 0:L]
            Kr = pKn[0:64, 2048:2048 + L]
            Vt = pKV[:, 0:NB * 128].rearrange("p (a b) -> p a b", b=128)
            first = [True]

            def prep(h, t_i):
                tok = slice(t_i * 512, (t_i + 1) * 512)
                Qn, Qn_t, Qr, Qr_t = Qns[t_i % 2], Qn_ts[t_i % 2], Qrs[t_i % 2], Qr_ts[t_i % 2]
                wK = [Kn_tt[t_i]] + ([pKn_t] if first[0] else [])
                wV = [KV_tt[t_i]] + ([pKV_t] if first[0] else [])
                first[0] = False
                kn_, knt_ = ps()
                mm(kn_[:], wkvb[:, h * 256:h * 256 + 128], kvnT[:, tok], True, True, [pv_t, kvnT_tt[t_i]], [knt_])
                act(sqb, kn_[:], AF.Square, [knt_], [sqb_t])
                ss_, sst_ = ps()
                mm(ss_[:], ones16, sqb, True, False, [sqb_t, cbf_t], [sst_])
                mm(ss_[:], ones16[0:64, :], sqkpe[:, tok], False, True, [sqkpe_tt[t_i], cbf_t], [sst_])
                rsqrt_from_ss(rs, ss_[:], 1.0 / 192, [sst_], rs_t)
                stt(Kn[:, tok], kn_[:], pc[:, 75:76], rs, ALU.mult, ALU.mult, [knt_, pc_t, rs_t], wK)
                tt(Kr[:, tok], kper[:, tok], rs[0:64, :], ALU.mult, [kper_tt[t_i], rs_t], [Kn_tt[t_i]])
                v_, vt_ = ps()
                for b_ in range(4):
                    tb = slice(t_i * 512 + b_ * 128, t_i * 512 + (b_ + 1) * 128)
                    mm(v_[:, b_ * 128:(b_ + 1) * 128], kvnT[:, tb], wkvb[:, h * 256 + 128:h * 256 + 256], True, True,
                       [pv_t, kvnT_tt[t_i]], [vt_])
                act(Vt[:, t_i * 4:(t_i + 1) * 4, :], v3(v_[:], 128), AF.Copy, [vt_], wV)
                qn_, qnt_ = ps()
                for kc in range(3):
                    mm(qn_[:], wqb[:, kc, h * 192:h * 192 + 128], qnT[:, kc, tok], kc == 0, kc == 2, [pw_t, qnT_tt[t_i]], [qnt_])
                qr_, qrt_ = ps()
                for kc in range(3):
                    mm(qr_[0:64, :], wqb[:, kc, h * 192 + 128:h * 192 + 192], qnT[:, kc, tok], kc == 0, kc == 2,
                       [pw_t, qnT_tt[t_i]], [qrt_])
                qs_, qst_ = ps()
                for kc in range(3):
                    mm(qs_[0:64, :], wqs[:, kc, h * 64:(h + 1) * 64], qnT[:, kc, tok], kc == 0, kc == 2,
                       [pw_t, qnT_tt[t_i]], [qst_])
                ss_, sst_ = ps()
                act(sqb, qn_[:], AF.Square, [qnt_], [sqb_t])
                mm(ss_[:], ones16, sqb, True, False, [sqb_t, cbf_t], [sst_])
                act(Qr, qr_[0:64, :], AF.Square, [qrt_], [Qr_t])
                mm(ss_[:], ones16[0:64, :], Qr, False, True, [Qr_t, cbf_t], [sst_])
                rsqrt_from_ss(rs, ss_[:], 1.0 / 192, [sst_], rs_t)
                stt(Qn, qn_[:], pc[:, 72:73], rs, ALU.mult, ALU.mult, [qnt_, pc_t, rs_t], [Qn_t])
                stt(t1[0:64, :], qr_[0:64, :], pc[0:64, 73:74], cos2[:, tok], ALU.mult, ALU.mult, [qrt_, pc_t, cos_t], [t1_t])
                stt(t2[0:64, :], qs_[0:64, :], pc[0:64, 74:75], sin2[:, tok], ALU.mult, ALU.mult, [qst_, pc_t, sin_t], [t2_t])
                tt(t1[0:64, :], t1[0:64, :], t2[0:64, :], ALU.add, [t1_t, t2_t], [t1_t])
                tt(Qr, t1[0:64, :], rs[0:64, :], ALU.mult, [t1_t, rs_t], [Qr_t])

            def attention(h, t_i):
                tok = slice(t_i * 512, (t_i + 1) * 512)
                Qn, Qn_t, Qr, Qr_t = Qns[t_i % 2], Qn_ts[t_i % 2], Qrs[t_i % 2], Qr_ts[t_i % 2]
                ps_n[0] = 6
                o_, ot_ = psb[6], psb_t[6]
                dn_, dnt_ = psb[7], psb_t[7]
                nk = 4 * t_i + 4

                def s_stage(kc):
                    j = kc - 4 * t_i
                    q0 = j * 128 if j >= 0 else 0
                    kk = slice(kc * 128, (kc + 1) * 128)
                    ktt = Kn_tt[kc // 4]
                    s_, st_ = ps()
                    mm(s_[:, q0:512], Kn[:, kk], Qn[:, q0:512], True, False, [pKn_t, ktt, Qn_t], [st_])
                    mm(s_[:, q0:512], Kr[:, kk], Qr[:, q0:512], False, True, [pKn_t, ktt, Qr_t], [st_])
                    p_i, p_it = pT[kc % 3], pT_t[kc % 3]
                    act(p_i[:, q0:512], s_[:, q0:512], AF.Exp, [st_], [p_it], scale=scale)
                    if j >= 0:
                        tt(p_i[:, q0:q0 + 128], p_i[:, q0:q0 + 128], tri16, ALU.mult, [p_it, cbf_t], [p_it])
                    return (kc, q0, p_i, p_it)

                def pv_stage(item):
                    kc, q0, p_i, p_it = item
                    mm(o_[:, q0:512], Vt[:, kc, :], p_i[:, q0:512], kc == 0, kc == nk - 1, [pKV_t, KV_tt[kc // 4], p_it], [ot_])
                    mm(dn_[:, q0:512], ones16, p_i[:, q0:512], kc == 0, kc == nk - 1, [cbf_t, p_it], [dnt_])

                pend_ = []
                for kc in range(nk):
                    pend_.append(s_stage(kc))
                    if len(pend_) > 2:
                        pv_stage(pend_.pop(0))
                while pend_:
                    pv_stage(pend_.pop(0))
                S.op("dve", lambda e: e.reciprocal(out=rs, in_=dn_[:]), [dnt_], [rs_t], fs=512)
                tt(yT[:, h, tok], o_[:], rs, ALU.mult, [ot_, rs_t], [yT_tt[t_i]])
                ps_n[0] = 8

            for h in range(4):
                prep(h, 0)
                for t_i in range(NT):
                    if t_i + 1 < NT:
                        prep(h, t_i + 1)
                    attention(h, t_i)

        env = dict(locals())
        for i, (ch, fn) in enumerate((("A", branch_a), ("B", branch_b), ("C", None), ("D", branch_d))):
            if ch not in cfg.get("branches", "ABCD"):
                continue
            if ch == "C":
                branch_c()
            else:
                fn()
            gating(i)

    mixer = cfg.get("mixer", None)
    for s in range(NSEQ):
        S.epoch = s
        S.barrier()
        load_x(s)
        for l in range(DEPTH):
            S.barrier()
            load_layer_params(l)
            if cfg.get("ffn1", True):
                ffn(l, f1gu, f1dn, 0)
            if mixer is not None:
                mixer_layer(l, s)
            if cfg.get("ffn2", True):
                ffn(l, f2gu, f2dn, 16)
        S.barrier()
        store_x(s)
    S.barrier()
    S.emit()
    es.close()
    return nc


def host_consts():
    c = np.zeros((128, 648), np.float32)
    c[:, 0:128] = np.eye(128, dtype=np.float32)
    s = np.arange(128)[:, None]
    t = np.arange(128)[None, :]
    c[:, 128:256] = (s <= t).astype(np.float32)
    c[:, 256:384] = np.where(t >= s, 0.0, -30000.0).astype(np.float32)
    c[:, 384:512] = 1.0
    invf = (10000.0 ** (-(np.arange(0, 64, 2, dtype=np.float32) / np.float32(64.0)))).astype(np.float32)
    c[0:32, 512] = invf
    c[32:64, 512] = invf
    c[0:32, 513] = -1.0
    c[32:64, 513] = 1.0
    c[:, 520:648] = (s > t).astype(np.float32)
    return c


def host_layout(inp, DEPTH):
    pcols = np.zeros((DEPTH, NPC, 128), np.float32)
    prow = np.zeros((DEPTH, NPR), np.float32)
    for l in range(DEPTH):
        pcols[l, 0:8] = inp["ffn1_norm"][l].reshape(8, 128)
        pcols[l, 8:16] = inp["mix_norm"][l].reshape(8, 128)
        pcols[l, 16:24] = inp["ffn2_norm"][l].reshape(8, 128)
        pcols[l, 24:56] = inp["ssd_conv_w"][l].reshape(4, 8, 128).reshape(32, 128)
        pcols[l, 56:64] = inp["ssd_conv_b"][l].reshape(8, 128)
        pcols[l, 64:68] = inp["ssd_norm"][l].reshape(4, 128)
        pcols[l, 68:71] = inp["mla_q_norm"][l].reshape(3, 128)
        pcols[l, 71] = inp["mla_kv_norm"][l]
        for r0, w in ((72, inp["mla_qk_q"][l]), (75, inp["mla_qk_k"][l])):
            pcols[l, r0] = w[0:128]
            pcols[l, r0 + 1, 0:64] = w[128:192]
            pcols[l, r0 + 2, 0:32] = w[160:192]
            pcols[l, r0 + 2, 32:64] = w[128:160]
        pcols[l, 78:90] = inp["sc_conv_w"][l].reshape(3, 4, 128).reshape(12, 128)
        pcols[l, 96:100] = np.repeat(inp["ssd_d"][l], 64).reshape(4, 128)
        prow[l, 0:8] = inp["ssd_dt_bias"][l]
        prow[l, 8:16] = inp["ssd_a_log"][l]
        prow[l, 16:24] = inp["ssd_d"][l]
    return pcols, prow


_CACHE = {}


def make_in_maps(inp, NCORE, NSEQ, DEPTH):
    pcols, prow = host_layout(inp, DEPTH)
    cstv = host_consts()
    shared = {
        "cst": cstv, "pcols": pcols, "prow": prow,
        "ffn1_w_gu": inp["ffn1_w_gu"], "ffn1_w_down": inp["ffn1_w_down"],
        "ffn2_w_gu": inp["ffn2_w_gu"], "ffn2_w_down": inp["ffn2_w_down"],
        "w_in": inp["w_in"], "gmlp_w_s": inp["gmlp_w_s"],
        "gmlp_b_s": inp["gmlp_b_s"].reshape(DEPTH, 512),
        "gmlp_v_norm": inp["gmlp_v_norm"],
        "mla_w_qb": inp["mla_w_qb"], "mla_w_kvb": inp["mla_w_kvb"],
        "w_branch": inp["w_branch"], "w_out": inp["w_out"],
    }
    in_maps = []
    for c in range(NCORE):
        m = dict(shared)
        m["x"] = np.ascontiguousarray(inp["x"][c * NSEQ:(c + 1) * NSEQ])
        m["positions"] = np.ascontiguousarray(inp["positions"][c * NSEQ:(c + 1) * NSEQ]).astype(np.int32)
        in_maps.append(m)
    return in_maps


def mixer_block(env, l, s):
    pass


def kernel(**inputs):
    inp = {k: np.asarray(v) for k, v in inputs.items()}
    B, L, _ = inp["x"].shape
    DEPTH = inp["w_in"].shape[0]
    NCORE = 8
    NSEQ = B // NCORE
    key = (L, NSEQ, DEPTH)
    if key not in _CACHE:
        _CACHE[key] = build_program(L, NSEQ, DEPTH, {"mixer": mixer_block})
    nc = _CACHE[key]
    in_maps = make_in_maps(inp, NCORE, NSEQ, DEPTH)
    res = run_bass_kernel_spmd(nc, in_maps, core_ids=list(range(NCORE)))
    return np.concatenate([r["out"] for r in res.results], axis=0)
```

```python
import numpy as np
import concourse.bass as bass
import concourse.mybir as mybir
from concourse.bass_utils import run_bass_kernel_spmd
from contextlib import ExitStack

F32 = mybir.dt.float32
BF16 = mybir.dt.bfloat16
I32 = mybir.dt.int32
AF = mybir.ActivationFunctionType
ALU = mybir.AluOpType

D = 1024
NCH = 8
DFF = 2816
NJ = 22
INTOT = 8776
EPS = 1e-6
O_Z, O_XBC, O_DT, O_UV, O_QL, O_KVL, O_KPE, O_SC, O_G = 0, 512, 1536, 1544, 2568, 2952, 3080, 3144, 4680
NPC = 112
NPR = 24


class T:
    __slots__ = ("name", "w", "r", "sem", "ndma", "persist")

    def __init__(self, name, persist=False):
        self.name = name
        self.persist = persist
        self.w = None
        self.r = {}
        self.sem = None
        self.ndma = 0


class Op:
    __slots__ = ("eng", "fn", "deps", "needs", "key", "val", "dma", "epoch", "fs")


ENGS = ("pe", "act", "dve", "pool", "sp")


class Sched:
    def __init__(self, nc, es):
        self.nc = nc
        self.es = es
        self.ops = {e: [] for e in ENGS}
        self.epoch = 0
        self.dma_ops = []
        self.tiles = []

    def T(self, name, persist=False):
        t = T(name, persist)
        self.tiles.append(t)
        return t

    def PT(self, name):
        if not hasattr(self, "_pt"):
            self._pt = {}
        if name not in self._pt:
            self._pt[name] = self.T(name)
        return self._pt[name]

    def sb(self, name, shape, dtype):
        return self.es.enter_context(self.nc.sbuf_tensor("sb_" + name, list(shape), dtype))

    def op(self, eng, fn, reads=(), writes=(), dma_tile=None, fs=0):
        o = Op()
        o.fs = fs
        o.eng = eng
        o.fn = fn
        o.deps = []
        o.needs = False
        o.dma = dma_tile
        o.epoch = self.epoch
        o.key = None
        o.val = 0
        is_dma = dma_tile is not None

        def dep(p, raw=False):
            if p is None or p is o:
                return
            if p.dma is None and not is_dma and p.eng == eng:
                if not raw or eng == "pe":
                    return
                if p.fs >= 512 and o.fs >= 512 and eng in ("dve", "act"):
                    return
            p.needs = True
            o.deps.append(p)

        for t in reads:
            dep(t.w, True)
        for t in writes:
            dep(t.w)
            for r in t.r.values():
                dep(r)
        for t in reads:
            t.r[("dma", id(o)) if is_dma else eng] = o
        for t in writes:
            t.w = o
            t.r = {}
        if is_dma:
            o.needs = True
            if not dma_tile.persist:
                self.dma_ops.append(o)
        self.ops[eng].append(o)
        return o

    def barrier(self):
        lasts = []
        BENGS = ("pe", "act", "dve", "sp")
        for e in BENGS:
            for o in reversed(self.ops[e]):
                if o.dma is None and o.fn is not None:
                    lasts.append(o)
                    break
        pend = list(self.dma_ops)
        self.dma_ops = []
        for e in BENGS:
            o = Op()
            o.fs = 0
            o.eng = e
            o.fn = None
            o.deps = []
            o.needs = False
            o.dma = None
            o.epoch = self.epoch
            o.key = None
            o.val = 0
            for p in lasts:
                if p.eng != e:
                    p.needs = True
                    o.deps.append(p)
            for p in pend:
                o.deps.append(p)
            self.ops[e].append(o)
        for t in self.tiles:
            if not t.persist:
                t.w = None
                t.r = {}

    def emit(self):
        nc = self.nc
        sems = {}

        def getsem(key):
            if key not in sems:
                sems[key] = self.es.enter_context(nc.semaphore("s%d" % len(sems)))
            return sems[key]

        for e in ENGS:
            cnt = {}
            for o in self.ops[e]:
                if o.dma is not None:
                    t = o.dma
                    t.ndma += 1
                    o.key = ("dma", id(t))
                    o.val = 16 * t.ndma
                elif o.needs:
                    k = (e, o.epoch)
                    cnt[k] = cnt.get(k, 0) + 1
                    o.key = k
                    o.val = cnt[k]
        for e in ENGS:
            for o in self.ops[e]:
                if o.needs:
                    getsem(o.key)
        import os
        if os.environ.get("MK_DEBUG"):
            mx = {}
            for e in ENGS:
                for o in self.ops[e]:
                    if o.key is not None:
                        mx[o.key] = max(mx.get(o.key, 0), o.val)
            print("NSEMS", len(sems), "MAXVALS", sorted([(str(k)[:30], v) for k, v in mx.items()], key=lambda kv: -kv[1])[:12])
            print("NOPS", {e: len(self.ops[e]) for e in ENGS})
        block = self.es.enter_context(nc.Block())

        def run(eng_name, eng):
            waited = {}
            for o in self.ops[eng_name]:
                need = {}
                for p in o.deps:
                    if waited.get(p.key, 0) < p.val:
                        if need.get(p.key, 0) < p.val:
                            need[p.key] = p.val
                for k, v in need.items():
                    eng.wait_ge(sems[k], v)
                    waited[k] = v
                if o.fn is None:
                    continue
                ins = o.fn(eng)
                if o.dma is not None:
                    ins.then_inc(sems[o.key], 16)
                elif o.needs:
                    ins.then_inc(sems[o.key], 1)

        @block.tensor
        def _(e):
            run("pe", e)

        @block.scalar
        def _(e):
            run("act", e)

        @block.vector
        def _(e):
            run("dve", e)

        @block.gpsimd
        def _(e):
            run("pool", e)

        @block.sync
        def _(e):
            run("sp", e)


def build_program(L, NSEQ, DEPTH, cfg=None):
    cfg = cfg or {}
    NT = L // 512
    NB = L // 128
    nc = bass.Bass("TRN2", target_bir_lowering=False)
    dr = {}

    def din(name, shape, dt=F32):
        dr[name] = nc.dram_tensor(name, list(shape), dt, kind="ExternalInput").ap()
        return dr[name]

    x_d = din("x", [NSEQ, L, D])
    pos_d = din("positions", [NSEQ, L], I32)
    cst_d = din("cst", [128, 648])
    pcols_d = din("pcols", [DEPTH, NPC, 128])
    prow_d = din("prow", [DEPTH, NPR])
    f1gu = din("ffn1_w_gu", [DEPTH, D, 2 * DFF])
    f1dn = din("ffn1_w_down", [DEPTH, DFF, D])
    f2gu = din("ffn2_w_gu", [DEPTH, D, 2 * DFF])
    f2dn = din("ffn2_w_down", [DEPTH, DFF, D])
    win_d = din("w_in", [DEPTH, D, INTOT])
    ws_d = din("gmlp_w_s", [DEPTH, 4, 128, 128])
    bs_d = din("gmlp_b_s", [DEPTH, 512])
    vnw_d = din("gmlp_v_norm", [DEPTH, 512])
    wqb_d = din("mla_w_qb", [DEPTH, 384, 768])
    wkvb_d = din("mla_w_kvb", [DEPTH, 128, 1024])
    wbr_d = din("w_branch", [DEPTH, 4, 512, D])
    wout_d = din("w_out", [DEPTH, D, D])
    out_d = nc.dram_tensor("out", [NSEQ, L, D], F32, kind="ExternalOutput").ap()

    es = ExitStack()
    S = Sched(nc, es)

    xT = S.sb("xT", [128, NCH, L], F32)
    xT_t = [S.T("xT%d" % i) for i in range(NT)]
    NRING = 6
    ring = [S.sb("ring%d" % i, [128, 4096], BF16) for i in range(NRING)]
    ring_t = [S.T("ring%d" % i, True) for i in range(NRING)]
    ring_pos = [0]
    cst = S.sb("cst", [128, 648], F32)
    cst_t = S.T("cst", True)
    ident = cst[:, 0:128]
    tri = cst[:, 128:256]
    maskneg = cst[:, 256:384]
    ones32 = cst[:, 384:512]
    triS = cst[:, 520:648]
    cbf = S.sb("cbf", [128, 384], BF16)
    cbf_t = S.T("cbf", True)
    ones16 = cbf[:, 0:128]
    tri16 = cbf[:, 128:256]
    ident16 = cbf[:, 256:384]
    prow = S.sb("prow", [128, NPR], F32)
    prow_t = S.T("prow", True)
    expA = S.sb("expA", [128, 8], F32)
    expA_t = S.T("expA", True)
    pc = S.sb("pc", [128, NPC], F32)
    pc_t = S.T("pc", True)
    pcst = S.sb("pcst", [NPC, 128], F32)
    pcst_t = S.T("pcst", True)
    ARENA = 90 * 1024
    arena = S.sb("arena", [128, ARENA // 4], F32)

    def carve(off, shape, dt):
        n = 1
        for s in shape[1:]:
            n *= s
        bpe = 4 if dt in (F32, I32) else 2
        assert off % 4 == 0 and off + n * bpe <= ARENA, (off, shape)
        v = arena[0:shape[0], off // 4: off // 4 + (n * bpe) // 4]
        if dt != F32:
            v = v.bitcast(dt)
        if len(shape) == 3:
            v = v.rearrange("p (a b) -> p a b", b=shape[2])
        elif len(shape) == 4:
            v = v.rearrange("p (a b c) -> p a b c", b=shape[2], c=shape[3])
        return v

    psb = [es.enter_context(nc.psum_tensor("ps%d" % i, [128, 512], F32)) for i in range(8)]
    psb_t = [S.T("ps%d" % i) for i in range(8)]
    ps_pos = [0]

    ps_n = [8]

    def ps():
        i = ps_pos[0] % ps_n[0]
        ps_pos[0] += 1
        return psb[i], psb_t[i]

    def ring_next():
        i = ring_pos[0] % NRING
        ring_pos[0] += 1
        return ring[i], ring_t[i]

    def fsz(ap):
        n = 1
        for d_ in ap.shape[1:]:
            n *= d_
        return n

    def mm(out, lhsT, rhs, start, stop, reads, writes):
        S.op("pe", lambda e: e.matmul(out, lhsT, rhs, start=start, stop=stop), reads, writes)

    def tr(out, in_, idn, reads, writes):
        S.op("pe", lambda e: e.transpose(out, in_, idn), reads, writes)

    def act(out, in_, func, reads, writes, bias=None, scale=None):
        kw = {}
        if bias is not None:
            kw["bias"] = bias
        if scale is not None:
            kw["scale"] = scale
        S.op("act", lambda e: e.activation(out=out, in_=in_, func=func, **kw), reads, writes, fs=fsz(out))

    def tt(out, in0, in1, op, reads, writes, eng="dve"):
        S.op(eng, lambda e: e.tensor_tensor(out=out, in0=in0, in1=in1, op=op), reads, writes, fs=fsz(out))

    def ts(out, in0, s1, s2, op0, op1, reads, writes, eng="dve"):
        if op1 is None:
            S.op(eng, lambda e: e.tensor_scalar(out=out, in0=in0, scalar1=s1, scalar2=None, op0=op0), reads, writes, fs=fsz(out))
        else:
            S.op(eng, lambda e: e.tensor_scalar(out=out, in0=in0, scalar1=s1, scalar2=s2, op0=op0, op1=op1), reads, writes, fs=fsz(out))

    def stt(out, in0, scalar, in1, op0, op1, reads, writes, eng="dve"):
        S.op(eng, lambda e: e.scalar_tensor_tensor(out=out, in0=in0, scalar=scalar, in1=in1, op0=op0, op1=op1), reads, writes, fs=fsz(out))

    def cp(out, in_, reads, writes, eng="dve"):
        S.op(eng, lambda e: e.tensor_copy(out=out, in_=in_), reads, writes, fs=fsz(out))

    def dma(eng, out, in_, tile, reads=(), writes=()):
        S.op(eng, lambda e: e.dma_start(out=out, in_=in_), reads, writes, dma_tile=tile)

    def wload(view_out, src, page_t):
        dma("pool", view_out, src, page_t, writes=[page_t])

    dma("sp", cst[:], cst_d, cst_t, writes=[cst_t])
    cp(ones16, ones32, [cst_t], [cbf_t])
    cp(tri16, tri, [cst_t], [cbf_t])
    cp(ident16, ident, [cst_t], [cbf_t])

    def load_layer_params(l):
        dma("sp", pcst[:], pcols_d[l], pcst_t, writes=[pcst_t])
        p_, pt_ = ps()
        tr(p_[:, 0:NPC], pcst[:], ident[0:NPC, 0:NPC], [pcst_t, cst_t], [pt_])
        cp(pc[:], p_[:, 0:NPC], [pt_], [pc_t])
        dma("sp", prow[:], prow_d[l:l + 1, :].partition_broadcast(128), prow_t, writes=[prow_t])
        act(expA[:], prow[:, 8:16], AF.Exp, [prow_t], [expA_t])

    def load_x(s):
        stg = [carve(i * 4096, [128, 1024], F32) for i in range(2)]
        stg_t = [S.PT("stg%d" % i) for i in range(2)]
        for b in range(NB):
            st, st_t = stg[b % 2], stg_t[b % 2]
            dma("sp", st, x_d[s, b * 128:(b + 1) * 128, :], st_t, writes=[st_t])
            for half in range(2):
                p_, pt_ = ps()
                for c4 in range(4):
                    c = half * 4 + c4
                    tr(p_[:, c4 * 128:(c4 + 1) * 128], st[:, c * 128:(c + 1) * 128], ident, [st_t, cst_t], [pt_])
                S.op("act", (lambda e, p_=p_, half=half, b=b: e.activation(
                    out=xT[:, half * 4:half * 4 + 4, b * 128:(b + 1) * 128],
                    in_=p_[:].rearrange("p (a b) -> p a b", b=128), func=AF.Copy)),
                    [pt_], [xT_t[b // 4]])

    def store_x(s):
        stg = [carve(i * 4096, [128, 1024], F32) for i in range(2)]
        stg_t = [S.PT("ostg%d" % i) for i in range(2)]
        for b in range(NB):
            st, st_t = stg[b % 2], stg_t[b % 2]
            for half in range(2):
                p_, pt_ = ps()
                for c4 in range(4):
                    c = half * 4 + c4
                    tr(p_[:, c4 * 128:(c4 + 1) * 128], xT[:, c, b * 128:(b + 1) * 128], ident, [xT_t[b // 4], cst_t], [pt_])
                act(st[:, half * 512:(half + 1) * 512], p_[:], AF.Copy, [pt_], [st_t])
            dma("sp", out_d[s, b * 128:(b + 1) * 128, :], st, st_t, reads=[st_t])

    def rsqrt_from_ss(out, ss, inv_n, reads, out_t):
        ts(out, ss, inv_n, EPS, ALU.mult, ALU.add, reads, [out_t])
        act(out, out, AF.Sqrt, [out_t], [out_t])
        S.op("dve", lambda e: e.reciprocal(out=out, in_=out), [out_t], [out_t], fs=fsz(out))

    def rmsnorm_tile(tt_i, wcol0, hT, hT_tt, sq, sq_t, rstd, rstd_t):
        tok = slice(tt_i * 512, (tt_i + 1) * 512)
        p_, pt_ = ps()
        for c in range(NCH):
            q, q_t = sq[c % len(sq)], sq_t[c % len(sq)]
            act(q, xT[:, c, tok], AF.Square, [xT_t[tt_i]], [q_t])
            mm(p_[:], ones16, q, c == 0, c == NCH - 1, [q_t, cbf_t], [pt_])
        if isinstance(rstd, list):
            rstd, rstd_t = rstd[tt_i % len(rstd)], rstd_t[tt_i % len(rstd_t)]
        rsqrt_from_ss(rstd, p_[:], 1.0 / D, [pt_], rstd_t)
        for c in range(NCH):
            stt(hT[:, c, tok], xT[:, c, tok], pc[:, wcol0 + c:wcol0 + c + 1], rstd, ALU.mult, ALU.mult,
                [xT_t[tt_i], pc_t, rstd_t], [hT_tt[tt_i]])

    FF_PARTS = [(0, 8), (8, 15), (15, 22)]

    def ffn(l, wgu_d, wdn_d, normcol):
        S.barrier()
        off = 0
        hT = carve(off, [128, NCH, L], BF16); off += NCH * L * 2
        aT = carve(off, [128, 8, L], BF16); off += 8 * L * 2
        sq = [carve(off + i * 1024, [128, 512], BF16) for i in range(3)]; off += 3 * 1024
        sg = [carve(off + i * 2048, [128, 512], F32) for i in range(3)]; off += 3 * 2048
        rstd = [carve(off + i * 2048, [128, 512], F32) for i in range(2)]; off += 4096
        hT_tt = [S.T("hT%d" % i) for i in range(NT)]
        aT_tt = [S.T("aT%d" % i) for i in range(NT)]
        sq_t = [S.T("sq%d" % i) for i in range(3)]
        sg_t = [S.T("sg%d" % i) for i in range(3)]
        rstd_t = [S.T("rstd%d" % i) for i in range(2)]
        for t_i in range(NT):
            rmsnorm_tile(t_i, normcol, hT, hT_tt, sq, sq_t, rstd, rstd_t)
        wgu = wgu_d[l].rearrange("(kc p) n -> p kc n", p=128)
        wdn = wdn_d[l].rearrange("(j p) n -> p j n", p=128)
        sgi = 0
        for (j0, j1) in FF_PARTS:
            j = j0
            while j < j1:
                nb = min(4, j1 - j)
                pg, pg_t = ring_next()
                pu, pu_t = ring_next()
                wg_v = pg[:, 0:8 * nb * 128].rearrange("p (a b) -> p a b", b=nb * 128)
                wu_v = pu[:, 0:8 * nb * 128].rearrange("p (a b) -> p a b", b=nb * 128)
                wload(wg_v, wgu[:, :, j * 128:(j + nb) * 128], pg_t)
                wload(wu_v, wgu[:, :, DFF + j * 128:DFF + (j + nb) * 128], pu_t)
                for jj in range(nb):
                    for t_i in range(NT):
                        tok = slice(t_i * 512, (t_i + 1) * 512)
                        g_, gt_ = ps()
                        u_, ut_ = ps()
                        for k in range(NCH):
                            mm(g_[:], wg_v[:, k, jj * 128:(jj + 1) * 128], hT[:, k, tok], k == 0, k == NCH - 1,
                               [pg_t, hT_tt[t_i]], [gt_])
                        for k in range(NCH):
                            mm(u_[:], wu_v[:, k, jj * 128:(jj + 1) * 128], hT[:, k, tok], k == 0, k == NCH - 1,
                               [pu_t, hT_tt[t_i]], [ut_])
                        s_, st_ = sg[sgi % 3], sg_t[sgi % 3]
                        sgi += 1
                        act(s_, g_[:], AF.Silu, [gt_], [st_])
                        tt(aT[:, j + jj - j0, tok], u_[:], s_, ALU.mult, [ut_, st_], [aT_tt[t_i]])
                j += nb
            nj = j1 - j0
            pages = []
            j = 0
            while j < nj:
                nb = min(4, nj - j)
                pd, pd_t = ring_next()
                wd_v = pd[:, 0:nb * 1024].rearrange("p (a b) -> p a b", b=1024)
                wload(wd_v, wdn[:, j0 + j:j0 + j + nb, :], pd_t)
                for jj in range(nb):
                    pages.append((wd_v, jj, pd_t))
                j += nb
            for oc in range(NCH):
                for t_i in range(NT):
                    tok = slice(t_i * 512, (t_i + 1) * 512)
                    d_, dt_ = ps()
                    for jx in range(nj):
                        wd_v, jj, pd_t = pages[jx]
                        mm(d_[:], wd_v[:, jj, oc * 128:(oc + 1) * 128], aT[:, jx, tok], jx == 0, jx == nj - 1,
                           [pd_t, aT_tt[t_i]], [dt_])
                    stt(xT[:, oc, tok], d_[:], 0.5, xT[:, oc, tok], ALU.mult, ALU.add, [dt_, xT_t[t_i]], [xT_t[t_i]])


    WS0 = 48 * 1024

    class WSAlloc:
        def __init__(self):
            self.off = WS0

        def get(self, shape, dt):
            n = 1
            for d_ in shape[1:]:
                n *= d_
            nb = n * (4 if dt in (F32, I32) else 2)
            nb = (nb + 3) // 4 * 4
            v = carve(self.off, shape, dt)
            self.off += nb
            return v

    def bc_mid(ap2d, n):
        return ap2d.unsqueeze(1).to_broadcast([ap2d.shape[0], n, ap2d.shape[1]])

    def bc_last(ap2d, n):
        return ap2d.unsqueeze(2).to_broadcast([ap2d.shape[0], ap2d.shape[1], n])

    def v3(ap2d, b):
        return ap2d.rearrange("p (a b) -> p a b", b=b)

    def mixer_layer(l, s):
        S.barrier()
        hT = carve(0, [128, NCH, L], BF16)
        yT = carve(32 * 1024, [128, 4, L], BF16)
        hT_tt = [S.T("mhT%d" % i) for i in range(NT)]
        yT_tt = [S.T("yT%d" % i) for i in range(NT)]
        win = win_d[l].rearrange("(kc p) n -> p kc n", p=128)

        def wpage(c0, ncols):
            pg, pg_t = ring_next()
            v = pg[:, 0:8 * ncols].rearrange("p (a b) -> p a b", b=ncols)
            wload(v, win[:, :, c0:c0 + ncols], pg_t)
            return v, pg_t

        wsn = WSAlloc()
        sq = [wsn.get([128, 512], BF16) for _ in range(3)]
        sq_t = [S.T("msq%d" % i) for i in range(3)]
        rstd = [wsn.get([128, 512], F32) for _ in range(2)]
        rstd_t = [S.T("mrstd%d" % i) for i in range(2)]
        for t_i in range(NT):
            rmsnorm_tile(t_i, 8, hT, hT_tt, sq, sq_t, rstd, rstd_t)

        def gating(i):
            S.barrier()
            ws = WSAlloc()
            gated = ws.get([128, 8, 512], BF16)
            gated_t = S.T("gated")
            sig = [ws.get([128, 512], F32) for _ in range(2)]
            sig_t = [S.T("sig%d" % k) for k in range(2)]
            pb, pb_t = ring_next()
            wbr = pb[:, 0:4096].rearrange("p (a b) -> p a b", b=1024)
            wload(wbr, wbr_d[l, i].rearrange("(kc p) n -> p kc n", p=128), pb_t)
            wg = [wpage(O_G + i * 1024 + hh * 512, 512) for hh in range(2)]
            wo = []
            wov = wout_d[l].rearrange("(kc p) n -> p kc n", p=128)
            for hh in range(2):
                pg, pg_t = ring_next()
                v = pg[:, 0:4096].rearrange("p (a b) -> p a b", b=512)
                wload(v, wov[:, :, hh * 512:(hh + 1) * 512], pg_t)
                wo.append((v, pg_t))
            si = 0
            for t_i in range(NT):
                tok = slice(t_i * 512, (t_i + 1) * 512)
                for oc in range(8):
                    per_, pert_ = ps()
                    for kc in range(4):
                        mm(per_[:], wbr[:, kc, oc * 128:(oc + 1) * 128], yT[:, kc, tok], kc == 0, kc == 3,
                           [pb_t, yT_tt[t_i]], [pert_])
                    g_, gt_ = ps()
                    wgv, wg_t = wg[oc // 4]
                    for k in range(8):
                        mm(g_[:], wgv[:, k, (oc % 4) * 128:(oc % 4 + 1) * 128], hT[:, k, tok], k == 0, k == 7,
                           [wg_t, hT_tt[t_i]], [gt_])
                    sg_, sgt_ = sig[si % 2], sig_t[si % 2]
                    si += 1
                    act(sg_, g_[:], AF.Sigmoid, [gt_], [sgt_])
                    tt(gated[:, oc, :], per_[:], sg_, ALU.mult, [pert_, sgt_], [gated_t])
                for oc2 in range(8):
                    o_, ot_ = ps()
                    wov_, wo_t = wo[oc2 // 4]
                    for oc in range(8):
                        mm(o_[:], wov_[:, oc, (oc2 % 4) * 128:(oc2 % 4 + 1) * 128], gated[:, oc, :], oc == 0, oc == 7,
                           [wo_t, gated_t], [ot_])
                    tt(xT[:, oc2, tok], o_[:], xT[:, oc2, tok], ALU.add, [ot_, xT_t[t_i]], [xT_t[t_i]])
            S.barrier()

        def branch_d():
            S.barrier()
            ws = WSAlloc()
            tbuf = ws.get([128, 516], F32)
            tbuf_t = S.T("tbuf")
            acc = ws.get([128, 512], F32)
            acc_t = S.T("dacc")
            cgs = ws.get([128, 512], F32)
            cgs_t = S.T("cgs")
            wb = wpage(O_SC, 512)
            wc = wpage(O_SC + 512, 512)
            wx = wpage(O_SC + 1024, 512)
            for c in range(4):
                for t_i in range(NT):
                    tok = slice(t_i * 512, (t_i + 1) * 512)
                    pss = []
                    for (wv, w_t) in (wb, wc, wx):
                        p_, pt_ = ps()
                        for k in range(8):
                            mm(p_[:], wv[:, k, c * 128:(c + 1) * 128], hT[:, k, tok], k == 0, k == 7,
                               [w_t, hT_tt[t_i]], [pt_])
                        pss.append((p_, pt_))
                    (b_, bt_), (c_, ct_), (x_, xt_) = pss
                    act(cgs, c_[:], AF.Copy, [ct_], [cgs_t])
                    if t_i == 0:
                        S.op("dve", lambda e: e.memset(tbuf[:, 0:2], 0.0), [], [tbuf_t])
                    else:
                        cp(tbuf[:, 0:2], tbuf[:, 512:514], [tbuf_t], [tbuf_t])
                    tt(tbuf[:, 2:514], x_[:], cgs, ALU.mult, [xt_, cgs_t], [tbuf_t])
                    ts(acc, tbuf[:, 0:512], pc[:, 78 + c:79 + c], None, ALU.mult, None, [tbuf_t, pc_t], [acc_t])
                    for k in (1, 2):
                        stt(acc, tbuf[:, k:k + 512], pc[:, 78 + k * 4 + c:79 + k * 4 + c], acc, ALU.mult, ALU.add,
                            [tbuf_t, pc_t, acc_t], [acc_t])
                    tt(yT[:, c, tok], b_[:], acc, ALU.mult, [bt_, acc_t], [yT_tt[t_i]])

        def branch_b():
            S.barrier()
            ws = WSAlloc()
            wstg = ws.get([128, 4, 128], F32)
            wstg_t = S.PT("wstg")
            wsT = ws.get([128, 4, 128], BF16)
            wsT_t = S.T("wsT")
            bsrow = ws.get([1, 512], BF16)
            bsrow_t = S.T("bsrow")
            vnw = ws.get([128, 512], F32)
            vnw_t = S.PT("vnw")
            dma("sp", vnw, vnw_d[l:l + 1, :].partition_broadcast(128), vnw_t, writes=[vnw_t])
            vgs = [ws.get([128, 512], F32) for _ in range(2)]
            vg_ts = [S.T("vg%d" % i) for i in range(2)]
            vsqs = [ws.get([128, 512], F32) for _ in range(2)]
            vsq_ts = [S.T("vsq%d" % i) for i in range(2)]
            vsss = [ws.get([128, 2], F32) for _ in range(2)]
            vss_ts = [S.T("vss%d" % i) for i in range(2)]
            vns = [ws.get([128, 512], BF16) for _ in range(2)]
            vn_ts = [S.T("vn%d" % i) for i in range(2)]
            dma("sp", wstg, ws_d[l].rearrange("g t s -> t g s"), wstg_t, writes=[wstg_t])
            bsf = ws.get([1, 512], F32)
            bsf_t = S.PT("bsf")
            dma("sp", bsf, bs_d[l:l + 1, :], bsf_t, writes=[bsf_t])
            cp(bsrow, bsf, [bsf_t], [bsrow_t])
            p_, pt_ = ps()
            for g in range(4):
                tr(p_[:, g * 128:(g + 1) * 128], wstg[:, g, :], ident, [wstg_t, cst_t], [pt_])
            tt(wsT, v3(p_[:], 128), bc_mid(tri, 4), ALU.mult, [pt_, cst_t], [wsT_t])
            wu = wpage(O_UV, 512)
            wv = wpage(O_UV + 512, 512)
            def upath(t_i):
                tok = slice(t_i * 512, (t_i + 1) * 512)
                for c in range(4):
                    u_, ut_ = ps()
                    for k in range(8):
                        mm(u_[:], wu[0][:, k, c * 128:(c + 1) * 128], hT[:, k, tok], k == 0, k == 7,
                           [wu[1], hT_tt[t_i]], [ut_])
                    act(yT[:, c, tok], u_[:], AF.Gelu, [ut_], [yT_tt[t_i]])

            def s1(gb):
                t_i, b = gb // 4, gb % 4
                tb = slice(t_i * 512 + b * 128, t_i * 512 + (b + 1) * 128)
                bi = gb % 2
                vg, vg_t, vsq, vsq_t = vgs[bi], vg_ts[bi], vsqs[bi], vsq_ts[bi]
                vss, vss_t, vn, vn_t = vsss[bi], vss_ts[bi], vns[bi], vn_ts[bi]
                v_, vt_ = ps()
                for k in range(8):
                    mm(v_[:], hT[:, k, tb], wv[0][:, k, :], k == 0, k == 7, [wv[1], hT_tt[t_i]], [vt_])
                act(vg, v_[:], AF.Gelu, [vt_], [vg_t])
                tt(vsq, vg, vg, ALU.mult, [vg_t], [vsq_t])
                S.op("dve", (lambda e, vss=vss, vsq=vsq: e.reduce_sum(out=vss[:, 0:1], in_=vsq, axis=mybir.AxisListType.X)), [vsq_t], [vss_t])
                rsqrt_from_ss(vss[:, 1:2], vss[:, 0:1], 1.0 / 512, [vss_t], vss_t)
                stt(vn, vg, vss[:, 1:2], vnw, ALU.mult, ALU.mult, [vg_t, vss_t, vnw_t], [vn_t])
                return (t_i, tb, vn, vn_t)

            def s2(item):
                t_i, tb, vn, vn_t = item
                sv_, svt_ = ps()
                for g in range(4):
                    mm(sv_[:, g * 128:(g + 1) * 128], vn[:, g * 128:(g + 1) * 128], wsT[:, g, :], True, False,
                       [vn_t, wsT_t], [svt_])
                    mm(sv_[:, g * 128:(g + 1) * 128], ones16[0:1, :], bsrow[0:1, g * 128:(g + 1) * 128], False, True,
                       [cbf_t, bsrow_t], [svt_])
                tt(yT[:, :, tb], yT[:, :, tb], v3(sv_[:], 128), ALU.mult, [svt_, yT_tt[t_i]], [yT_tt[t_i]])

            pend_b = []
            for gb in range(NT * 4):
                if gb % 4 == 0:
                    upath(gb // 4)
                pend_b.append(s1(gb))
                if len(pend_b) > 1:
                    s2(pend_b.pop(0))
            while pend_b:
                s2(pend_b.pop(0))

        def branch_a():
            S.barrier()
            ws = WSAlloc()
            xbcT = ws.get([128, 8, 512], BF16); xbcT_t = S.T("xbcT")
            rawh = ws.get([128, 516], F32); rawh_t = S.T("rawh")
            acc = ws.get([128, 512], F32); acc_t = S.T("aacc")
            halo = ws.get([128, 8, 4], F32); halo_t = S.T("halo")
            dtr = ws.get([128, 32], F32); dtr_t = S.T("dtr")
            dtv = ws.get([128, 32], F32); dtv_t = S.T("dtv")
            av = ws.get([128, 32], F32); av_t = S.T("av")
            sm = ws.get([128, 128], F32); sm_t = S.T("sm")
            cbm = ws.get([128, 2, 128], F32); cbm_t = S.T("cbm")
            pgA, pgA_t = ring_next()
            pgB, pgB_t = ring_next()
            fA = pgA[:].bitcast(F32)
            fB = pgB[:].bitcast(F32)
            Rbs = [ws.get([128, 8, 128], F32), fA[:, 0:1024].rearrange("p (a b) -> p a b", b=128)]
            W1s = [ws.get([128, 8, 128], F32), fA[:, 1024:2048].rearrange("p (a b) -> p a b", b=128)]
            MTs = [ws.get([128, 8, 128], BF16), pgB[:, 0:1024].rearrange("p (a b) -> p a b", b=128)]
            xdts = [ws.get([128, 8, 64], BF16), pgB[:, 1024:1536].rearrange("p (a b) -> p a b", b=64)]
            xws = [ws.get([128, 8, 64], BF16), pgB[:, 1536:2048].rearrange("p (a b) -> p a b", b=64)]
            Btoks = [ws.get([128, 256], BF16), pgB[:, 2048:2304]]
            ysbs = [ws.get([128, 512], F32), fB[:, 1280:1792]]
            Rb_ts = [S.T("Rb%d" % i) for i in range(2)]
            W1_ts = [S.T("W1%d" % i) for i in range(2)]
            MT_ts = [S.T("MT%d" % i) for i in range(2)]
            xdt_ts = [S.T("xdt%d" % i) for i in range(2)]
            xw_ts = [S.T("xw%d" % i) for i in range(2)]
            Btok_ts = [S.T("Btok%d" % i) for i in range(2)]
            ysb_ts = [S.T("ysb%d" % i) for i in range(2)]
            MT, MT_t = MTs[0], MT_ts[0]
            pg_first = [True]
            st32 = ws.get([128, 8, 64], F32); st32_t = S.T("st32")
            st16 = ws.get([128, 512], BF16); st16_t = S.T("st16")
            ysb = ws.get([128, 512], F32); ysb_t = S.T("ysb")
            yraw = ws.get([128, 4, 512], F32); yraw_t = S.T("yraw")
            sz = acc; sz_t = acc_t
            gsq = MT.rearrange("p a b -> p (a b)")[:, 0:512]; gsq_t = MT_t
            grs = rawh[:, 0:512]; grs_t = rawh_t
            wz = wpage(O_Z, 512)
            wx0 = wpage(O_XBC, 512)
            wx1 = wpage(O_XBC + 512, 512)
            wdt = wpage(O_DT, 8)
            wxb = (wx0, wx1)
            acs4, eacs4, dout4, eatot4 = sm[:, 0:32], sm[:, 32:64], sm[:, 64:96], sm[:, 96:128]
            for t_i in range(NT):
                tok = slice(t_i * 512, (t_i + 1) * 512)
                for f in range(8):
                    p_, pt_ = ps()
                    wv, w_t = wxb[f // 4]
                    for k in range(8):
                        mm(p_[:], wv[:, k, (f % 4) * 128:(f % 4 + 1) * 128], hT[:, k, tok], k == 0, k == 7,
                           [w_t, hT_tt[t_i]], [pt_])
                    if t_i == 0:
                        S.op("dve", lambda e: e.memset(rawh[:, 0:3], 0.0), [], [rawh_t])
                    else:
                        cp(rawh[:, 0:3], halo[:, f, 0:3], [halo_t], [rawh_t])
                    act(rawh[:, 3:515], p_[:], AF.Copy, [pt_], [rawh_t])
                    cp(halo[:, f, 0:3], rawh[:, 512:515], [rawh_t], [halo_t])
                    ts(acc, rawh[:, 0:512], pc[:, 24 + f:25 + f], None, ALU.mult, None, [rawh_t, pc_t], [acc_t])
                    for k in (1, 2, 3):
                        stt(acc, rawh[:, k:k + 512], pc[:, 24 + k * 8 + f:25 + k * 8 + f], acc, ALU.mult, ALU.add,
                            [rawh_t, pc_t, acc_t], [acc_t])
                    act(xbcT[:, f, :], acc, AF.Silu, [acc_t, pc_t], [xbcT_t], bias=pc[:, 56 + f:57 + f])
                d_, dt_ = ps()
                for c in range(4):
                    for k in range(8):
                        mm(d_[:, c * 8:(c + 1) * 8], hT[:, k, t_i * 512 + c * 128:t_i * 512 + (c + 1) * 128], wdt[0][:, k, :],
                           k == 0, k == 7, [wdt[1], hT_tt[t_i]], [dt_])
                tt(v3(dtr, 8), v3(d_[:, 0:32], 8), bc_mid(prow[:, 0:8], 4), ALU.add, [dt_, prow_t], [dtr_t])
                act(dtr, dtr, AF.Exp, [dtr_t], [dtr_t])
                act(dtv, dtr, AF.Ln, [dtr_t], [dtv_t], bias=1.0)
                stt(v3(av, 8), v3(dtv, 8), -1.0, bc_mid(expA[:], 4), ALU.mult, ALU.mult, [dtv_t, expA_t], [av_t])
                cu_, cut_ = ps()
                mm(cu_[:, 0:32], tri, av, True, True, [cst_t, av_t], [cut_])
                mm(cu_[:, 32:64], ones32, av, True, True, [cst_t, av_t], [cut_])
                act(acs4, cu_[:, 0:32], AF.Copy, [cut_], [sm_t])
                act(eacs4, cu_[:, 0:32], AF.Exp, [cut_], [sm_t])
                act(eatot4, cu_[:, 32:64], AF.Exp, [cut_], [sm_t])
                tt(dout4, cu_[:, 32:64], acs4, ALU.subtract, [cut_, sm_t], [sm_t])
                act(dout4, dout4, AF.Exp, [sm_t], [sm_t])
                def chunk_ctx(c):
                        gc = t_i * 4 + c
                        ct = slice(c * 128, (c + 1) * 128)
                        a_c = av[:, c * 8:(c + 1) * 8]
                        bi = c % 2
                        Rb, Rb_t, W1, W1_t = Rbs[bi], Rb_ts[bi], W1s[bi], W1_ts[bi]
                        MTc, MTc_t, xdt, xdt_t = MTs[bi], MT_ts[bi], xdts[bi], xdt_ts[bi]
                        xw, xw_t, Btok, Btok_t, ysb, ysb_t = xws[bi], xw_ts[bi], Btoks[bi], Btok_ts[bi], ysbs[bi], ysb_ts[bi]
                        pgr = [pgA_t, pgB_t] if bi == 1 else []
                        pgw = [pgA_t, pgB_t] if (bi == 1 and pg_first[0]) else []
                        if bi == 1:
                            pg_first[0] = False
                        acs = acs4[:, c * 8:(c + 1) * 8]
                        eacs = eacs4[:, c * 8:(c + 1) * 8]
                        dout = dout4[:, c * 8:(c + 1) * 8]
                        eatot = eatot4[:, c * 8:(c + 1) * 8]

                        return locals()

                def p1(c):
                    L_ = chunk_ctx(c)
                    (gc, ct, a_c, bi, Rb, Rb_t, W1, W1_t, MTc, MTc_t, xdt, xdt_t, xw, xw_t, Btok, Btok_t, ysb, ysb_t, pgr, pgw, acs, eacs, dout, eatot) = [L_[k] for k in ('gc','ct','a_c','bi','Rb','Rb_t','W1','W1_t','MTc','MTc_t','xdt','xdt_t','xw','xw_t','Btok','Btok_t','ysb','ysb_t','pgr','pgw','acs','eacs','dout','eatot')]
                    tt(Rb, bc_mid(tri, 8), bc_last(a_c, 128), ALU.mult, [cst_t, av_t] + pgr, [Rb_t] + pgw)
                    cb_, cbt_ = ps()
                    for g in range(2):
                        mm(cb_[:, g * 128:(g + 1) * 128], xbcT[:, 4 + g, ct], xbcT[:, 6 + g, ct], True, True,
                           [xbcT_t], [cbt_])
                    tt(cbm, v3(cb_[:, 0:256], 128), bc_mid(tri, 2), ALU.mult, [cbt_, cst_t], [cbm_t])
                    for g in range(2):
                        bc_, bct_ = ps()
                        mm(bc_[:], triS, Rb[:, g * 4:(g + 1) * 4, :].rearrange("p a b -> p (a b)"), True, True,
                           [cst_t, Rb_t] + pgr, [bct_])
                        act(W1[:, g * 4:(g + 1) * 4, :], v3(bc_[:], 128), AF.Exp, [bct_] + pgr, [W1_t])
                    for g in range(2):
                        tt(MTc[:, g * 4:(g + 1) * 4, :], W1[:, g * 4:(g + 1) * 4, :], bc_mid(cbm[:, g, :], 4),
                           ALU.mult, [W1_t, cbm_t] + pgr, [MTc_t])
                    xs_, xst_ = ps()
                    xs16 = xs_[:].bitcast(BF16)
                    for cc in range(4):
                        tr(xs16[:, cc * 128:(cc + 1) * 128], xbcT[:, cc, ct], ident16, [xbcT_t, cbf_t], [xst_])
                    tt(xdt, v3(xs16[:, 0:512], 64), bc_last(dtv[:, c * 8:(c + 1) * 8], 64), ALU.mult, [xst_, dtv_t] + pgr, [xdt_t])
                    tt(xw, xdt, bc_last(dout, 64), ALU.mult, [xdt_t, sm_t] + pgr, [xw_t])
                    b_, bt_ = ps()
                    b16 = b_[:].bitcast(BF16)
                    for g in range(2):
                        tr(b16[:, g * 128:(g + 1) * 128], xbcT[:, 4 + g, ct], ident16, [xbcT_t, cbf_t], [bt_])
                    act(Btok, b16[:, 0:256], AF.Copy, [bt_] + pgr, [Btok_t])
                    return L_

                def p2(L_):
                    (gc, ct, a_c, bi, Rb, Rb_t, W1, W1_t, MTc, MTc_t, xdt, xdt_t, xw, xw_t, Btok, Btok_t, ysb, ysb_t, pgr, pgw, acs, eacs, dout, eatot) = [L_[k] for k in ('gc','ct','a_c','bi','Rb','Rb_t','W1','W1_t','MTc','MTc_t','xdt','xdt_t','xw','xw_t','Btok','Btok_t','ysb','ysb_t','pgr','pgw','acs','eacs','dout','eatot')]
                    y_, yt_ = ps()
                    for h in range(8):
                        mm(y_[:, h * 64:(h + 1) * 64], MTc[:, h, :], xdt[:, h, :], True, True, [MTc_t, xdt_t] + pgr, [yt_])
                    if gc > 0:
                        yo_, yot_ = ps()
                        for g in range(2):
                            mm(yo_[:, g * 256:(g + 1) * 256], xbcT[:, 6 + g, ct], st16[:, g * 256:(g + 1) * 256], True, True,
                               [xbcT_t, st16_t], [yot_])
                        tt(v3(ysb, 64), v3(yo_[:], 64), bc_last(eacs, 64), ALU.mult, [yot_, sm_t] + pgr, [ysb_t])
                        tt(ysb, ysb, y_[:], ALU.add, [ysb_t, yt_] + pgr, [ysb_t])
                    else:
                        cp(ysb, y_[:], [yt_] + pgr, [ysb_t])
                    s_, st_ = ps()
                    for g in range(2):
                        mm(s_[:, g * 256:(g + 1) * 256], Btok[:, g * 128:(g + 1) * 128],
                           xw[:, g * 4:(g + 1) * 4, :].rearrange("p a b -> p (a b)"), True, True, [Btok_t, xw_t] + pgr, [st_])
                    if gc > 0:
                        tt(st32, st32, bc_last(eatot, 64), ALU.mult, [st32_t, sm_t], [st32_t])
                        tt(st32, st32, v3(s_[:], 64), ALU.add, [st32_t, st_], [st32_t])
                    else:
                        cp(st32, v3(s_[:], 64), [st_], [st32_t])
                    act(st16, st32.rearrange("p a b -> p (a b)"), AF.Copy, [st32_t], [st16_t])
                    yT_, yTt_ = ps()
                    for cc in range(4):
                        tr(yT_[:, cc * 128:(cc + 1) * 128], ysb[:, cc * 128:(cc + 1) * 128], ident, [ysb_t, cst_t] + pgr, [yTt_])
                    act(yraw[:, :, ct], v3(yT_[:], 128), AF.Copy, [yTt_], [yraw_t])

                pend_c = []
                for c in range(4):
                    pend_c.append(p1(c))
                    if len(pend_c) > 1:
                        p2(pend_c.pop(0))
                while pend_c:
                    p2(pend_c.pop(0))
                for cc in range(4):
                    stt(yraw[:, cc, :], xbcT[:, cc, :], pc[:, 96 + cc:97 + cc], yraw[:, cc, :], ALU.mult, ALU.add,
                        [xbcT_t, pc_t, yraw_t], [yraw_t])
                for cc in range(4):
                    z_, zt_ = ps()
                    for k in range(8):
                        mm(z_[:], wz[0][:, k, cc * 128:(cc + 1) * 128], hT[:, k, tok], k == 0, k == 7,
                           [wz[1], hT_tt[t_i]], [zt_])
                    act(sz, z_[:], AF.Silu, [zt_], [sz_t])
                    tt(yraw[:, cc, :], yraw[:, cc, :], sz, ALU.mult, [yraw_t, sz_t], [yraw_t])
                for g in range(2):
                    ss_, sst_ = ps()
                    for j, cc in enumerate((2 * g, 2 * g + 1)):
                        act(gsq, yraw[:, cc, :], AF.Square, [yraw_t], [gsq_t])
                        mm(ss_[:], ones16, gsq, j == 0, j == 1, [gsq_t, cbf_t], [sst_])
                    rsqrt_from_ss(grs, ss_[:], 1.0 / 256, [sst_], grs_t)
                    for cc in (2 * g, 2 * g + 1):
                        stt(yT[:, cc, tok], yraw[:, cc, :], pc[:, 64 + cc:65 + cc], grs, ALU.mult, ALU.mult,
                            [yraw_t, pc_t, grs_t], [yT_tt[t_i]])


        def branch_c():
            S.barrier()
            import math
            ws = WSAlloc()
            qnT = ws.get([128, 3, L], BF16); qnT_tt = [S.T("qnT%d" % i) for i in range(NT)]
            kvnT = ws.get([128, L], BF16); kvnT_tt = [S.T("kvnT%d" % i) for i in range(NT)]
            kper = ws.get([64, L], F32); kper_tt = [S.T("kper%d" % i) for i in range(NT)]
            sqkpe = ws.get([64, L], BF16); sqkpe_tt = [S.T("sqkpe%d" % i) for i in range(NT)]
            sqb = ws.get([128, 512], BF16); sqb_t = S.T("csq")
            rs = ws.get([128, 512], F32); rs_t = S.T("crs")
            t1 = ws.get([128, 512], F32); t1_t = S.PT("ct1")
            t2 = ws.get([128, 512], F32); t2_t = S.T("ct2")
            Qns = [ws.get([128, 512], BF16) for _ in range(2)]; Qn_ts = [S.T("Qn%d" % i) for i in range(2)]
            Qrs = [ws.get([64, 512], BF16) for _ in range(2)]; Qr_ts = [S.T("Qr%d" % i) for i in range(2)]
            pT = [ws.get([128, 512], BF16) for _ in range(3)]; pT_t = [S.T("pT%d" % i) for i in range(3)]
            pq, pq_t = ring_next()
            wql = pq[:, 0:8 * 384].rearrange("p (a b) -> p a b", b=384)
            wload(wql, win[:, :, O_QL:O_QL + 384], pq_t)
            pk, pk_t = ring_next()
            wkl = pk[:, 0:8 * 192].rearrange("p (a b) -> p a b", b=192)
            wload(wkl, win[:, :, O_KVL:O_KVL + 192], pk_t)
            pk2, pk2_t = ring_next()
            wks = pk2[:, 0:8 * 64].rearrange("p (a b) -> p a b", b=64)
            wks_b = pk2[:, 1024:1024 + 8 * 64].rearrange("p (a b) -> p a b", b=64)
            dma("pool", wks[:, :, 0:32], win[:, :, O_KPE + 32:O_KPE + 64], pk2_t, writes=[pk2_t])
            dma("pool", wks[:, :, 32:64], win[:, :, O_KPE:O_KPE + 32], pk2_t, writes=[pk2_t])
            pcs, pcs_t = ring_next()
            cs32 = pcs[:].bitcast(F32)
            assert L <= 1024 or True
            cos2 = None
            if 2 * L * 4 <= 8192:
                cos2 = cs32[0:64, 0:L]
                sin2 = cs32[0:64, L:2 * L]
                sin_t = pcs_t
            else:
                pcs2, pcs2_t = ring_next()
                cos2 = cs32[0:64, 0:L]
                sin2 = pcs2[:].bitcast(F32)[0:64, 0:L]
                sin_t = pcs2_t
            cos_t = pcs_t
            invf = cst[0:64, 512:513]
            sgn = cst[0:64, 513:514]
            TWO_PI = 2.0 * math.pi
            posi = t1[0:64, :].bitcast(I32)
            for t_i in range(NT):
                tok = slice(t_i * 512, (t_i + 1) * 512)
                a_, k_, m_ = t2[0:64, :], rs[0:64, :], t1[0:64, :]
                dma("sp", posi, pos_d[s:s + 1, tok].partition_broadcast(64), t1_t, writes=[t1_t])
                cp(a_, posi, [t1_t], [t2_t])
                ts(a_, a_, invf, None, ALU.mult, None, [t2_t, cst_t], [t2_t])
                for which, dst, dst_t in ((0, sin2, sin_t), (1, cos2, cos_t)):
                    if which == 1:
                        ts(a_, a_, math.pi / 2, None, ALU.add, None, [t2_t], [t2_t])
                    ts(k_, a_, 1.0 / TWO_PI, None, ALU.mult, None, [t2_t], [rs_t])
                    cp(posi, k_, [rs_t], [t1_t])
                    cp(k_, posi, [t1_t], [rs_t])
                    stt(k_, k_, -TWO_PI, a_, ALU.mult, ALU.add, [rs_t, t2_t], [rs_t])
                    ts(m_, k_, math.pi, None, ALU.is_gt, None, [rs_t], [t1_t])
                    stt(k_, m_, -TWO_PI, k_, ALU.mult, ALU.add, [t1_t, rs_t], [rs_t])
                    ts(m_, k_, -math.pi, None, ALU.is_lt, None, [rs_t], [t1_t])
                    stt(k_, m_, TWO_PI, k_, ALU.mult, ALU.add, [t1_t, rs_t], [rs_t])
                    ts(k_, k_, math.pi, -math.pi, ALU.min, ALU.max, [rs_t], [rs_t])
                    act(dst[:, tok], k_, AF.Sin, [rs_t], [dst_t])
                ts(sin2[:, tok], sin2[:, tok], sgn, None, ALU.mult, None, [sin_t, cst_t], [sin_t])
            for t_i in range(NT):
                tok = slice(t_i * 512, (t_i + 1) * 512)
                qps = []
                for kc in range(3):
                    p_, pt_ = ps()
                    for k in range(8):
                        mm(p_[:], wql[:, k, kc * 128:(kc + 1) * 128], hT[:, k, tok], k == 0, k == 7, [pq_t, hT_tt[t_i]], [pt_])
                    qps.append((p_, pt_))
                ss_, sst_ = ps()
                for kc in range(3):
                    act(sqb, qps[kc][0][:], AF.Square, [qps[kc][1]], [sqb_t])
                    mm(ss_[:], ones16, sqb, kc == 0, kc == 2, [sqb_t, cbf_t], [sst_])
                rsqrt_from_ss(rs, ss_[:], 1.0 / 384, [sst_], rs_t)
                for kc in range(3):
                    stt(qnT[:, kc, tok], qps[kc][0][:], pc[:, 68 + kc:69 + kc], rs, ALU.mult, ALU.mult,
                        [qps[kc][1], pc_t, rs_t], [qnT_tt[t_i]])
                p_, pt_ = ps()
                for k in range(8):
                    mm(p_[:], wkl[:, k, 0:128], hT[:, k, tok], k == 0, k == 7, [pk_t, hT_tt[t_i]], [pt_])
                act(sqb, p_[:], AF.Square, [pt_], [sqb_t])
                ss_, sst_ = ps()
                mm(ss_[:], ones16, sqb, True, True, [sqb_t, cbf_t], [sst_])
                rsqrt_from_ss(rs, ss_[:], 1.0 / 128, [sst_], rs_t)
                stt(kvnT[:, tok], p_[:], pc[:, 71:72], rs, ALU.mult, ALU.mult, [pt_, pc_t, rs_t], [kvnT_tt[t_i]])
                kp_, kpt_ = ps()
                for k in range(8):
                    mm(kp_[0:64, :], wkl[:, k, 128:192], hT[:, k, tok], k == 0, k == 7, [pk_t, hT_tt[t_i]], [kpt_])
                ks_, kst_ = ps()
                for k in range(8):
                    mm(ks_[0:64, :], wks[:, k, :], hT[:, k, tok], k == 0, k == 7, [pk2_t, hT_tt[t_i]], [kst_])
                act(sqkpe[:, tok], kp_[0:64, :], AF.Square, [kpt_], [sqkpe_tt[t_i]])
                stt(t1[0:64, :], kp_[0:64, :], pc[0:64, 76:77], cos2[:, tok], ALU.mult, ALU.mult, [kpt_, pc_t, cos_t], [t1_t])
                stt(t2[0:64, :], ks_[0:64, :], pc[0:64, 77:78], sin2[:, tok], ALU.mult, ALU.mult, [kst_, pc_t, sin_t], [t2_t])
                tt(kper[:, tok], t1[0:64, :], t2[0:64, :], ALU.add, [t1_t, t2_t], [kper_tt[t_i]])
            pw, pw_t = ring_next()
            wqb = pw[:, 0:3 * 768].rearrange("p (a b) -> p a b", b=768)
            wqs = pw[:, 3072:3072 + 3 * 256].rearrange("p (a b) -> p a b", b=256)
            wqv = wqb_d[l].rearrange("(kc p) n -> p kc n", p=128)
            wq4 = wqb_d[l].rearrange("(kc p) (h d) -> p kc h d", p=128, d=192)
            dma("pool", wqb, wqv, pw_t, writes=[pw_t])
            wqs4 = wqs.rearrange("p a (h d) -> p a h d", d=64)
            for kc in range(3):
                dma("pool", wqs4[:, kc, :, 0:32], wq4[:, kc, :, 160:192], pw_t, writes=[pw_t])
                dma("pool", wqs4[:, kc, :, 32:64], wq4[:, kc, :, 128:160], pw_t, writes=[pw_t])
            pv, pv_t = ring_next()
            wkvb = pv[:, 0:1024]
            wload(wkvb, wkvb_d[l], pv_t)
            scale = 192.0 ** -0.5
            pKn, pKn_t = ring_next()
            pKV, pKV_t = ring_next()
            Kn_tt = [S.T("Kn%d" % i) for i in range(NT)]
            KV_tt = [S.T("KV%d" % i) for i in range(NT)]
            Kn = pKn[:, 0:L]
            Kr = pKn[0:64, 2048:2048 + L]
            Vt = pKV[:, 0:NB * 128].rearrange("p (a b) -> p a b", b=128)
            first = [True]

            def prep(h, t_i):
                tok = slice(t_i * 512, (t_i + 1) * 512)
                Qn, Qn_t, Qr, Qr_t = Qns[t_i % 2], Qn_ts[t_i % 2], Qrs[t_i % 2], Qr_ts[t_i % 2]
                wK = [Kn_tt[t_i]] + ([pKn_t] if first[0] else [])
                wV = [KV_tt[t_i]] + ([pKV_t] if first[0] else [])
                first[0] = False
                kn_, knt_ = ps()
                mm(kn_[:], wkvb[:, h * 256:h * 256 + 128], kvnT[:, tok], True, True, [pv_t, kvnT_tt[t_i]], [knt_])
                act(sqb, kn_[:], AF.Square, [knt_], [sqb_t])
                ss_, sst_ = ps()
                mm(ss_[:], ones16, sqb, True, False, [sqb_t, cbf_t], [sst_])
                mm(ss_[:], ones16[0:64, :], sqkpe[:, tok], False, True, [sqkpe_tt[t_i], cbf_t], [sst_])
                rsqrt_from_ss(rs, ss_[:], 1.0 / 192, [sst_], rs_t)
                stt(Kn[:, tok], kn_[:], pc[:, 75:76], rs, ALU.mult, ALU.mult, [knt_, pc_t, rs_t], wK)
                tt(Kr[:, tok], kper[:, tok], rs[0:64, :], ALU.mult, [kper_tt[t_i], rs_t], [Kn_tt[t_i]], eng="pool")
                v_, vt_ = ps()
                for b_ in range(4):
                    tb = slice(t_i * 512 + b_ * 128, t_i * 512 + (b_ + 1) * 128)
                    mm(v_[:, b_ * 128:(b_ + 1) * 128], kvnT[:, tb], wkvb[:, h * 256 + 128:h * 256 + 256], True, True,
                       [pv_t, kvnT_tt[t_i]], [vt_])
                act(Vt[:, t_i * 4:(t_i + 1) * 4, :], v3(v_[:], 128), AF.Copy, [vt_], wV)
                qn_, qnt_ = ps()
                for kc in range(3):
                    mm(qn_[:], wqb[:, kc, h * 192:h * 192 + 128], qnT[:, kc, tok], kc == 0, kc == 2, [pw_t, qnT_tt[t_i]], [qnt_])
                qr_, qrt_ = ps()
                for kc in range(3):
                    mm(qr_[0:64, :], wqb[:, kc, h * 192 + 128:h * 192 + 192], qnT[:, kc, tok], kc == 0, kc == 2,
                       [pw_t, qnT_tt[t_i]], [qrt_])
                qs_, qst_ = ps()
                for kc in range(3):
                    mm(qs_[0:64, :], wqs[:, kc, h * 64:(h + 1) * 64], qnT[:, kc, tok], kc == 0, kc == 2,
                       [pw_t, qnT_tt[t_i]], [qst_])
                ss_, sst_ = ps()
                act(sqb, qn_[:], AF.Square, [qnt_], [sqb_t])
                mm(ss_[:], ones16, sqb, True, False, [sqb_t, cbf_t], [sst_])
                act(Qr, qr_[0:64, :], AF.Square, [qrt_], [Qr_t])
                mm(ss_[:], ones16[0:64, :], Qr, False, True, [Qr_t, cbf_t], [sst_])
                rsqrt_from_ss(rs, ss_[:], 1.0 / 192, [sst_], rs_t)
                stt(Qn, qn_[:], pc[:, 72:73], rs, ALU.mult, ALU.mult, [qnt_, pc_t, rs_t], [Qn_t])
                stt(t1[0:64, :], qr_[0:64, :], pc[0:64, 73:74], cos2[:, tok], ALU.mult, ALU.mult, [qrt_, pc_t, cos_t], [t1_t])
                stt(t2[0:64, :], qs_[0:64, :], pc[0:64, 74:75], sin2[:, tok], ALU.mult, ALU.mult, [qst_, pc_t, sin_t], [t2_t])
                tt(t1[0:64, :], t1[0:64, :], t2[0:64, :], ALU.add, [t1_t, t2_t], [t1_t], eng="pool")
                tt(Qr, t1[0:64, :], rs[0:64, :], ALU.mult, [t1_t, rs_t], [Qr_t], eng="pool")

            def attention(h, t_i):
                tok = slice(t_i * 512, (t_i + 1) * 512)
                Qn, Qn_t, Qr, Qr_t = Qns[t_i % 2], Qn_ts[t_i % 2], Qrs[t_i % 2], Qr_ts[t_i % 2]
                ps_n[0] = 6
                o_, ot_ = psb[6], psb_t[6]
                dn_, dnt_ = psb[7], psb_t[7]
                nk = 4 * t_i + 4

                def s_stage(kc):
                    j = kc - 4 * t_i
                    q0 = j * 128 if j >= 0 else 0
                    kk = slice(kc * 128, (kc + 1) * 128)
                    ktt = Kn_tt[kc // 4]
                    s_, st_ = ps()
                    mm(s_[:, q0:512], Kn[:, kk], Qn[:, q0:512], True, False, [pKn_t, ktt, Qn_t], [st_])
                    mm(s_[:, q0:512], Kr[:, kk], Qr[:, q0:512], False, True, [pKn_t, ktt, Qr_t], [st_])
                    p_i, p_it = pT[kc % 3], pT_t[kc % 3]
                    act(p_i[:, q0:512], s_[:, q0:512], AF.Exp, [st_], [p_it], scale=scale)
                    if j >= 0:
                        tt(p_i[:, q0:q0 + 128], p_i[:, q0:q0 + 128], tri16, ALU.mult, [p_it, cbf_t], [p_it], eng="pool")
                    return (kc, q0, p_i, p_it)

                def pv_stage(item):
                    kc, q0, p_i, p_it = item
                    mm(o_[:, q0:512], Vt[:, kc, :], p_i[:, q0:512], kc == 0, kc == nk - 1, [pKV_t, KV_tt[kc // 4], p_it], [ot_])
                    mm(dn_[:, q0:512], ones16, p_i[:, q0:512], kc == 0, kc == nk - 1, [cbf_t, p_it], [dnt_])

                pend_ = []
                for kc in range(nk):
                    pend_.append(s_stage(kc))
                    if len(pend_) > 2:
                        pv_stage(pend_.pop(0))
                while pend_:
                    pv_stage(pend_.pop(0))
                S.op("dve", lambda e: e.reciprocal(out=rs, in_=dn_[:]), [dnt_], [rs_t], fs=512)
                tt(yT[:, h, tok], o_[:], rs, ALU.mult, [ot_, rs_t], [yT_tt[t_i]])
                ps_n[0] = 8

            for h in range(4):
                prep(h, 0)
                for t_i in range(NT):
                    if t_i + 1 < NT:
                        prep(h, t_i + 1)
                    attention(h, t_i)

        env = dict(locals())
        for i, (ch, fn) in enumerate((("A", branch_a), ("B", branch_b), ("C", None), ("D", branch_d))):
            if ch not in cfg.get("branches", "ABCD"):
                continue
            if ch == "C":
                branch_c()
            else:
                fn()
            gating(i)

    mixer = cfg.get("mixer", None)
    for s in range(NSEQ):
        S.epoch = s
        S.barrier()
        load_x(s)
        for l in range(DEPTH):
            S.barrier()
            load_layer_params(l)
            if cfg.get("ffn1", True):
                ffn(l, f1gu, f1dn, 0)
            if mixer is not None:
                mixer_layer(l, s)
            if cfg.get("ffn2", True):
                ffn(l, f2gu, f2dn, 16)
        S.barrier()
        store_x(s)
    S.barrier()
    S.emit()
    es.close()
    return nc


def host_consts():
    c = np.zeros((128, 648), np.float32)
    c[:, 0:128] = np.eye(128, dtype=np.float32)
    s = np.arange(128)[:, None]
    t = np.arange(128)[None, :]
    c[:, 128:256] = (s <= t).astype(np.float32)
    c[:, 256:384] = np.where(t >= s, 0.0, -30000.0).astype(np.float32)
    c[:, 384:512] = 1.0
    invf = (10000.0 ** (-(np.arange(0, 64, 2, dtype=np.float32) / np.float32(64.0)))).astype(np.float32)
    c[0:32, 512] = invf
    c[32:64, 512] = invf
    c[0:32, 513] = -1.0
    c[32:64, 513] = 1.0
    c[:, 520:648] = (s > t).astype(np.float32)
    return c


def host_layout(inp, DEPTH):
    pcols = np.zeros((DEPTH, NPC, 128), np.float32)
    prow = np.zeros((DEPTH, NPR), np.float32)
    for l in range(DEPTH):
        pcols[l, 0:8] = inp["ffn1_norm"][l].reshape(8, 128)
        pcols[l, 8:16] = inp["mix_norm"][l].reshape(8, 128)
        pcols[l, 16:24] = inp["ffn2_norm"][l].reshape(8, 128)
        pcols[l, 24:56] = inp["ssd_conv_w"][l].reshape(4, 8, 128).reshape(32, 128)
        pcols[l, 56:64] = inp["ssd_conv_b"][l].reshape(8, 128)
        pcols[l, 64:68] = inp["ssd_norm"][l].reshape(4, 128)
        pcols[l, 68:71] = inp["mla_q_norm"][l].reshape(3, 128)
        pcols[l, 71] = inp["mla_kv_norm"][l]
        for r0, w in ((72, inp["mla_qk_q"][l]), (75, inp["mla_qk_k"][l])):
            pcols[l, r0] = w[0:128]
            pcols[l, r0 + 1, 0:64] = w[128:192]
            pcols[l, r0 + 2, 0:32] = w[160:192]
            pcols[l, r0 + 2, 32:64] = w[128:160]
        pcols[l, 78:90] = inp["sc_conv_w"][l].reshape(3, 4, 128).reshape(12, 128)
        pcols[l, 96:100] = np.repeat(inp["ssd_d"][l], 64).reshape(4, 128)
        prow[l, 0:8] = inp["ssd_dt_bias"][l]
        prow[l, 8:16] = inp["ssd_a_log"][l]
        prow[l, 16:24] = inp["ssd_d"][l]
    return pcols, prow


_CACHE = {}


def make_in_maps(inp, NCORE, NSEQ, DEPTH):
    pcols, prow = host_layout(inp, DEPTH)
    cstv = host_consts()
    shared = {
        "cst": cstv, "pcols": pcols, "prow": prow,
        "ffn1_w_gu": inp["ffn1_w_gu"], "ffn1_w_down": inp["ffn1_w_down"],
        "ffn2_w_gu": inp["ffn2_w_gu"], "ffn2_w_down": inp["ffn2_w_down"],
        "w_in": inp["w_in"], "gmlp_w_s": inp["gmlp_w_s"],
        "gmlp_b_s": inp["gmlp_b_s"].reshape(DEPTH, 512),
        "gmlp_v_norm": inp["gmlp_v_norm"],
        "mla_w_qb": inp["mla_w_qb"], "mla_w_kvb": inp["mla_w_kvb"],
        "w_branch": inp["w_branch"], "w_out": inp["w_out"],
    }
    in_maps = []
    for c in range(NCORE):
        m = dict(shared)
        m["x"] = np.ascontiguousarray(inp["x"][c * NSEQ:(c + 1) * NSEQ])
        m["positions"] = np.ascontiguousarray(inp["positions"][c * NSEQ:(c + 1) * NSEQ]).astype(np.int32)
        in_maps.append(m)
    return in_maps


def mixer_block(env, l, s):
    pass


def kernel(**inputs):
    inp = {k: np.asarray(v) for k, v in inputs.items()}
    B, L, _ = inp["x"].shape
    DEPTH = inp["w_in"].shape[0]
    NCORE = 8
    NSEQ = B // NCORE
    key = (L, NSEQ, DEPTH)
    if key not in _CACHE:
        _CACHE[key] = build_program(L, NSEQ, DEPTH, {"mixer": mixer_block})
    nc = _CACHE[key]
    in_maps = make_in_maps(inp, NCORE, NSEQ, DEPTH)
    res = run_bass_kernel_spmd(nc, in_maps, core_ids=list(range(NCORE)))
    return np.concatenate([r["out"] for r in res.results], axis=0)
```
